# Optimizing a Trainium2 kernel written in Bass

```python
import math
import jax
import jax.numpy as jnp
from jax import lax
import numpy as np

D_MODEL = 1024
BATCH = 4
SEQ = 8192
DEPTH = 2

GRID_W = 64
CTX_LEN = 256
ROPE_THETA = 10000.0
Q_BLOCK = 128
LN_EPS = 1e-6
RMS_EPS = 1e-6

MLA_HEADS = 8
MLA_Q_LORA = 384
MLA_KV_LORA = 256
MLA_NOPE = 64
MLA_ROPE = 32
MLA_V = 64
MLA_WIDTH = MLA_HEADS * MLA_V
MLA_SCALE = (MLA_NOPE + MLA_ROPE) ** -0.5

DIFF_HEADS = 4
DIFF_QK = 64
DIFF_V = 2 * DIFF_QK
DIFF_WIDTH = DIFF_HEADS * DIFF_V
DIFF_SCALE = DIFF_QK ** -0.5

ATTN_WIDTH = MLA_WIDTH + DIFF_WIDTH
EVEN_SPLITS = (MLA_Q_LORA, MLA_KV_LORA, MLA_ROPE,
               DIFF_HEADS * 2 * DIFF_QK, DIFF_HEADS * 2 * DIFF_QK, DIFF_WIDTH,
               ATTN_WIDTH)
EVEN_IN_WIDTH = sum(EVEN_SPLITS)

LRU_WIDTH = 1024
LRU_BLOCKS = 8
LRU_BLOCK_W = LRU_WIDTH // LRU_BLOCKS
LRU_C = 8.0
CONV_W = 4
CONV_PAD_L = CONV_W // 2
CONV_PAD_R = CONV_W - 1 - CONV_PAD_L

DEEPNORM_ALPHA = (2 * DEPTH) ** 0.25
DEEPNORM_BETA = (8 * DEPTH) ** -0.25

kernel_name = 'hybrid_mla_diffattn_rglru_prefix_dit'


def _layer_norm(x, g, b):
    x32 = x.astype(jnp.float32)
    mu = jnp.mean(x32, -1, keepdims=True)
    var = jnp.mean(jnp.square(x32 - mu), -1, keepdims=True)
    y = (x32 - mu) * lax.rsqrt(var + LN_EPS) * g.astype(jnp.float32) + b.astype(jnp.float32)
    return y.astype(x.dtype)


def _rms_norm(x, g):
    x32 = x.astype(jnp.float32)
    y = x32 * lax.rsqrt(jnp.mean(jnp.square(x32), -1, keepdims=True) + RMS_EPS)
    return (y * g.astype(jnp.float32)).astype(x.dtype)


def _adaln(cond, w, b):
    m = jax.nn.silu(cond) @ w + b
    return jnp.split(m, 3, axis=-1)


def _modulate(x, shift, scale):
    return x * (1 + scale) + shift


def _rope_tables(n, rot_dim):
    t = jnp.arange(n)
    rows = (t // GRID_W).astype(jnp.float32)
    cols = (t % GRID_W).astype(jnp.float32)
    n_freq = rot_dim // 4
    freqs = ROPE_THETA ** (-jnp.arange(n_freq, dtype=jnp.float32) / n_freq)
    ang = jnp.concatenate([rows[:, None] * freqs, cols[:, None] * freqs], -1)
    return jnp.cos(ang), jnp.sin(ang)


def _apply_rope(x, cos, sin):
    shape = (x.shape[1],) + (1,) * (x.ndim - 3) + (cos.shape[-1],)
    cos = cos.reshape(shape).astype(x.dtype)
    sin = sin.reshape(shape).astype(x.dtype)
    x1, x2 = jnp.split(x, 2, axis=-1)
    return jnp.concatenate([x1 * cos - x2 * sin, x2 * cos + x1 * sin], -1)


def _sweep_query_blocks(fn, qs):
    b, n = qs[0].shape[:2]
    nb = n // Q_BLOCK
    blocks = tuple(q.reshape((b, nb, Q_BLOCK) + q.shape[2:]).swapaxes(0, 1) for q in qs)
    out = lax.map(lambda qb: fn(*qb), blocks)
    out = out.swapaxes(0, 1)
    return out.reshape((b, n) + out.shape[3:])


def _softmax_attn(q, k, v, scale):
    s = jnp.einsum('bqhd,bkhd->bhqk', q, k).astype(jnp.float32) * scale
    p = jax.nn.softmax(s, axis=-1).astype(v.dtype)
    return jnp.einsum('bhqk,bkhd->bqhd', p, v)


def _diff_attn(q1, q2, k1, k2, v, lam, scale):
    s1 = jnp.einsum('bqhd,bkhd->bhqk', q1, k1).astype(jnp.float32) * scale
    s2 = jnp.einsum('bqhd,bkhd->bhqk', q2, k2).astype(jnp.float32) * scale
    p = (jax.nn.softmax(s1, axis=-1) - lam * jax.nn.softmax(s2, axis=-1)).astype(v.dtype)
    return jnp.einsum('bhqk,bkhd->bqhd', p, v)


def _project_even(z, q_norm_g, w_uq, kv_norm_g, w_ukv):
    idx = np.cumsum(EVEN_SPLITS)[:-1].tolist()
    cq, ckv, k_rope, dq, dk, dv, gate = jnp.split(z, idx, axis=-1)
    b, n = z.shape[:2]
    q = (_rms_norm(cq, q_norm_g) @ w_uq).reshape(b, n, MLA_HEADS, MLA_NOPE + MLA_ROPE)
    kv = (_rms_norm(ckv, kv_norm_g) @ w_ukv).reshape(b, n, MLA_HEADS, MLA_NOPE + MLA_V)
    return (q, kv[..., :MLA_NOPE], k_rope, kv[..., MLA_NOPE:],
            dq.reshape(b, n, DIFF_HEADS, 2, DIFF_QK),
            dk.reshape(b, n, DIFF_HEADS, 2, DIFF_QK),
            dv.reshape(b, n, DIFF_HEADS, DIFF_V), gate)


def _mla_keys(k_nope, k_rope):
    k_r = jnp.broadcast_to(k_rope[:, :, None, :], k_nope.shape[:3] + (MLA_ROPE,))
    return jnp.concatenate([k_nope, k_r], -1)


def _even_merge(o_a, o_b, gate, subln_g, lambda_init, w_out):
    b, n = o_a.shape[:2]
    o_b = _rms_norm(o_b, subln_g) * (1.0 - lambda_init)
    o = jnp.concatenate([o_a.reshape(b, n, MLA_WIDTH), o_b.reshape(b, n, DIFF_WIDTH)], -1)
    return (o * jax.nn.silu(gate)) @ w_out


def _even_layer(x, ctx, c, c_ctx, ada_w, ada_b, ln_g, ln_b, w_in, q_norm_g, w_uq, kv_norm_g, w_ukv,
                lam_q1, lam_k1, lam_q2, lam_k2, subln_g, w_out, lambda_init, need_ctx):
    n = x.shape[1]
    sh_x, sc_x, g_x = _adaln(c[:, None, :], ada_w, ada_b)
    sh_c, sc_c, g_c = _adaln(c_ctx[None, None, :], ada_w, ada_b)
    qx, knx, krx, vax, dqx, dkx, dvx, gx = _project_even(_modulate(x, sh_x, sc_x) @ w_in,
                                                          q_norm_g, w_uq, kv_norm_g, w_ukv)
    qc, knc, krc, vac, dqc, dkc, dvc, gc = _project_even(_modulate(ctx, sh_c, sc_c) @ w_in,
                                                          q_norm_g, w_uq, kv_norm_g, w_ukv)
    cos_a, sin_a = _rope_tables(n, MLA_ROPE)
    cos_b, sin_b = _rope_tables(n, DIFF_QK)
    qx = jnp.concatenate([qx[..., :MLA_NOPE], _apply_rope(qx[..., MLA_NOPE:], cos_a, sin_a)], -1)
    krx = _apply_rope(krx, cos_a, sin_a)
    dqx = _apply_rope(dqx, cos_b, sin_b)
    dkx = _apply_rope(dkx, cos_b, sin_b)
    kac = _mla_keys(knc, krc)
    ka_all = jnp.concatenate([_mla_keys(knx, krx), kac], 1)
    va_all = jnp.concatenate([vax, vac], 1)
    dk1_all = jnp.concatenate([dkx[..., 0, :], dkc[..., 0, :]], 1)
    dk2_all = jnp.concatenate([dkx[..., 1, :], dkc[..., 1, :]], 1)
    dv_all = jnp.concatenate([dvx, dvc], 1)
    lam = (jnp.exp(jnp.sum(lam_q1.astype(jnp.float32) * lam_k1.astype(jnp.float32)))
           - jnp.exp(jnp.sum(lam_q2.astype(jnp.float32) * lam_k2.astype(jnp.float32)))
           + lambda_init)
    o_a = _sweep_query_blocks(lambda qb: _softmax_attn(qb, ka_all, va_all, MLA_SCALE), (qx,))
    o_b = _sweep_query_blocks(
        lambda q1b, q2b: _diff_attn(q1b, q2b, dk1_all, dk2_all, dv_all, lam, DIFF_SCALE),
        (dqx[..., 0, :], dqx[..., 1, :]))
    y = _even_merge(o_a, o_b, gx, subln_g, lambda_init, w_out)
    x_new = _layer_norm(DEEPNORM_ALPHA * x + g_x * y, ln_g, ln_b)
    if not need_ctx:
        return x_new, None
    o_ac = _softmax_attn(qc, kac, vac, MLA_SCALE)
    o_bc = _diff_attn(dqc[..., 0, :], dqc[..., 1, :], dkc[..., 0, :], dkc[..., 1, :], dvc, lam, DIFF_SCALE)
    yc = _even_merge(o_ac, o_bc, gc, subln_g, lambda_init, w_out)
    ctx_new = _layer_norm(DEEPNORM_ALPHA * ctx + g_c * yc, ln_g, ln_b)
    return x_new, ctx_new


def _conv_centred(u, w, b):
    out = lax.conv_general_dilated(u, w[:, None, :].astype(u.dtype), window_strides=(1,),
                                   padding=[(CONV_PAD_L, CONV_PAD_R)],
                                   dimension_numbers=('NWC', 'WIO', 'NWC'),
                                   feature_group_count=u.shape[-1])
    return out + b


def _rglru_coeffs(u, wa, ba, wx, bx, lam):
    b, n, _ = u.shape
    ub = u.reshape(b, n, LRU_BLOCKS, LRU_BLOCK_W)
    r = jax.nn.sigmoid(jnp.einsum('bnki,kij->bnkj', ub, wa) + ba).reshape(b, n, LRU_WIDTH).astype(jnp.float32)
    i = jax.nn.sigmoid(jnp.einsum('bnki,kij->bnkj', ub, wx) + bx).reshape(b, n, LRU_WIDTH).astype(jnp.float32)
    log_a = -LRU_C * r * jax.nn.softplus(-lam.astype(jnp.float32))
    a = jnp.exp(log_a)
    gated = jnp.sqrt(-jnp.expm1(2.0 * log_a)) * (i * u.astype(jnp.float32))
    return a, gated


def _linear_scan(a, bx, h0):
    bx = bx.at[:, 0].add(a[:, 0] * h0)
    def combine(l, r):
        return (l[0] * r[0], r[0] * l[1] + r[1])
    return lax.associative_scan(combine, (a, bx), axis=1)[1]


def _rglru_direction(ux, uc, wa, ba, wx, bx, lam):
    a_c, b_c = _rglru_coeffs(uc, wa, ba, wx, bx, lam)
    a_x, b_x = _rglru_coeffs(ux, wa, ba, wx, bx, lam)
    h_c = _linear_scan(a_c, b_c, jnp.zeros((uc.shape[0], LRU_WIDTH), jnp.float32))
    h_x = _linear_scan(a_x, b_x, h_c[:, -1])
    return h_x, h_c


def _odd_layer(x, ctx, c, c_ctx, ada_w, ada_b, ln_g, ln_b, w_in, conv_w, conv_b,
               ga_w, ga_b, gx_w, gx_b, lru_lambda, w_out, need_ctx):
    sh_x, sc_x, g_x = _adaln(c[:, None, :], ada_w, ada_b)
    sh_c, sc_c, g_c = _adaln(c_ctx[None, None, :], ada_w, ada_b)
    ux, gate_x = jnp.split(_modulate(x, sh_x, sc_x) @ w_in, 2, axis=-1)
    uc, gate_c = jnp.split(_modulate(ctx, sh_c, sc_c) @ w_in, 2, axis=-1)
    ux = _conv_centred(ux, conv_w, conv_b)
    uc = _conv_centred(uc, conv_w, conv_b)
    hx_f, hc_f = _rglru_direction(ux, uc, ga_w[0], ga_b[0], gx_w[0], gx_b[0], lru_lambda[0])
    hx_b, hc_b = _rglru_direction(ux[:, ::-1], uc[:, ::-1], ga_w[1], ga_b[1], gx_w[1], gx_b[1], lru_lambda[1])
    hx = (hx_f + hx_b[:, ::-1]).astype(x.dtype)
    y = (hx * jax.nn.silu(gate_x)) @ w_out
    x_new = _layer_norm(DEEPNORM_ALPHA * x + g_x * y, ln_g, ln_b)
    if not need_ctx:
        return x_new, None
    hc = (hc_f + hc_b[:, ::-1]).astype(ctx.dtype)
    yc = (hc * jax.nn.silu(gate_c)) @ w_out
    ctx_new = _layer_norm(DEEPNORM_ALPHA * ctx + g_c * yc, ln_g, ln_b)
    return x_new, ctx_new


def setup_inputs(seed: int = 0) -> dict:
    key = jax.random.key(seed)
    ks = jax.random.split(key, 32)
    n_even = (DEPTH + 1) // 2
    n_odd = DEPTH // 2

    def nrm(k, shape, scale):
        return jax.random.normal(k, shape, jnp.float32) * scale

    u = jax.random.uniform(ks[27], (n_odd, 2, LRU_WIDTH), jnp.float32, 0.9, 0.999)
    a0 = u ** (1.0 / LRU_C)
    return {
        'x': nrm(ks[0], (BATCH, SEQ, D_MODEL), 1.0),
        'c': nrm(ks[1], (BATCH, D_MODEL), 1.0),
        'ctx': nrm(ks[2], (BATCH, CTX_LEN, D_MODEL), 1.0),
        'c_ctx': nrm(ks[3], (D_MODEL,), 1.0),
        'ada_w': nrm(ks[4], (DEPTH, D_MODEL, 3 * D_MODEL), 0.5 * D_MODEL ** -0.5),
        'ada_b': nrm(ks[5], (DEPTH, 3 * D_MODEL), 0.02),
        'post_ln_g': 1.0 + nrm(ks[6], (DEPTH, D_MODEL), 0.02),
        'post_ln_b': nrm(ks[7], (DEPTH, D_MODEL), 0.02),
        'e_w_in': nrm(ks[8], (n_even, D_MODEL, EVEN_IN_WIDTH), D_MODEL ** -0.5),
        'e_q_norm_g': 1.0 + nrm(ks[9], (n_even, MLA_Q_LORA), 0.02),
        'e_w_uq': nrm(ks[10], (n_even, MLA_Q_LORA, MLA_HEADS * (MLA_NOPE + MLA_ROPE)), MLA_Q_LORA ** -0.5),
        'e_kv_norm_g': 1.0 + nrm(ks[11], (n_even, MLA_KV_LORA), 0.02),
        'e_w_ukv': nrm(ks[12], (n_even, MLA_KV_LORA, MLA_HEADS * (MLA_NOPE + MLA_V)), MLA_KV_LORA ** -0.5),
        'e_lam_q1': nrm(ks[13], (n_even, DIFF_QK), 0.1),
        'e_lam_k1': nrm(ks[14], (n_even, DIFF_QK), 0.1),
        'e_lam_q2': nrm(ks[15], (n_even, DIFF_QK), 0.1),
        'e_lam_k2': nrm(ks[16], (n_even, DIFF_QK), 0.1),
        'e_subln_g': 1.0 + nrm(ks[17], (n_even, DIFF_V), 0.02),
        'e_w_out': nrm(ks[18], (n_even, ATTN_WIDTH, D_MODEL), ATTN_WIDTH ** -0.5 * DEEPNORM_BETA),
        'o_w_in': nrm(ks[19], (n_odd, D_MODEL, 2 * LRU_WIDTH), D_MODEL ** -0.5),
        'o_conv_w': nrm(ks[20], (n_odd, CONV_W, LRU_WIDTH), CONV_W ** -0.5),
        'o_conv_b': nrm(ks[21], (n_odd, LRU_WIDTH), 0.02),
        'o_gate_a_w': nrm(ks[22], (n_odd, 2, LRU_BLOCKS, LRU_BLOCK_W, LRU_BLOCK_W), LRU_BLOCK_W ** -0.5),
        'o_gate_a_b': nrm(ks[23], (n_odd, 2, LRU_BLOCKS, LRU_BLOCK_W), 0.02),
        'o_gate_x_w': nrm(ks[24], (n_odd, 2, LRU_BLOCKS, LRU_BLOCK_W, LRU_BLOCK_W), LRU_BLOCK_W ** -0.5),
        'o_gate_x_b': nrm(ks[25], (n_odd, 2, LRU_BLOCKS, LRU_BLOCK_W), 0.02),
        'o_lru_lambda': jnp.log(a0) - jnp.log1p(-a0),
        'o_w_out': nrm(ks[26], (n_odd, LRU_WIDTH, D_MODEL), LRU_WIDTH ** -0.5 * DEEPNORM_BETA),
    }


def reference(x, c, ctx, c_ctx, ada_w, ada_b, post_ln_g, post_ln_b,
              e_w_in, e_q_norm_g, e_w_uq, e_kv_norm_g, e_w_ukv,
              e_lam_q1, e_lam_k1, e_lam_q2, e_lam_k2, e_subln_g, e_w_out,
              o_w_in, o_conv_w, o_conv_b, o_gate_a_w, o_gate_a_b, o_gate_x_w, o_gate_x_b,
              o_lru_lambda, o_w_out):
    for i in range(DEPTH):
        need_ctx = i < DEPTH - 1
        j = i // 2
        if i % 2 == 0:
            lambda_init = 0.8 - 0.6 * math.exp(-0.3 * i)
            x, ctx = _even_layer(x, ctx, c, c_ctx, ada_w[i], ada_b[i], post_ln_g[i], post_ln_b[i],
                                 e_w_in[j], e_q_norm_g[j], e_w_uq[j], e_kv_norm_g[j], e_w_ukv[j],
                                 e_lam_q1[j], e_lam_k1[j], e_lam_q2[j], e_lam_k2[j], e_subln_g[j],
                                 e_w_out[j], lambda_init, need_ctx)
        else:
            x, ctx = _odd_layer(x, ctx, c, c_ctx, ada_w[i], ada_b[i], post_ln_g[i], post_ln_b[i],
                                o_w_in[j], o_conv_w[j], o_conv_b[j], o_gate_a_w[j], o_gate_a_b[j],
                                o_gate_x_w[j], o_gate_x_b[j], o_lru_lambda[j], o_w_out[j], need_ctx)
    return x
```

```python
import math
import numpy as np
from contextlib import ExitStack
import concourse.bass as bass
import concourse.mybir as mybir
from concourse.bass_utils import run_bass_kernel_spmd

F32 = mybir.dt.float32
BF16 = mybir.dt.bfloat16
AF = mybir.ActivationFunctionType
ALU = mybir.AluOpType
AX = mybir.AxisListType

D = 1024
SEQ = 8192
NOWN = 4096
CTX = 256
T = SEQ + CTX
NQ = NOWN + CTX
NKC = T // 128
ALPHA = 4 ** 0.25
LN_EPS = 1e-6
RMS_EPS = 1e-6
MLA_SCALE = 96 ** -0.5
DIFF_SCALE = 64 ** -0.5
LAMBDA_INIT0 = 0.8 - 0.6 * math.exp(-0.3 * 0)
NWA = 3232 + 96 + 96 + 512 + 512


class Buf:
    def __init__(self, name):
        self.name = name
        self.w = None
        self.r = []
        self.wsem = None
        self.rsem = None


class Sched:
    ENG = ("pe", "act", "dve", "pool", "sp")
    ENGATTR = {"pe": "tensor", "act": "scalar", "dve": "vector", "pool": "gpsimd", "sp": "sync"}
    CE = ("pe", "act", "dve", "pool")

    def __init__(self, nc, stack):
        self.nc = nc
        self.stack = stack
        self.ops = {e: [] for e in self.ENG}
        self.esem = {}
        self.ecount = {}
        self.known = {e: {} for e in self.ENG}
        self.pending = {e: [] for e in self.ENG}
        self.nsem = 0
        self.dsems = []
        self.final_tokens = []
        self.free_dsems = {"sp": [], "pool": [], "act": []}
        self.sem_owners = []
        self._fresh_engine_sems()

    def _sem(self, name):
        self.nsem += 1
        return self.stack.enter_context(self.nc.semaphore(f"q{self.nsem}_{name}"))

    def _fresh_engine_sems(self):
        for e in self.CE:
            self.esem[e] = self._sem("e_" + e)
            self.ecount[e] = 0

    def new_dsem(self, name, q):
        if self.free_dsems[q]:
            return self.free_dsems[q].pop()
        ent = [self._sem(name), 0, q]
        self.dsems.append(ent)
        return ent

    def barrier(self):
        toks = [(self.esem[e], self.ecount[e], "x") for e in self.CE if self.ecount[e] > 0]
        toks += [(ent[0], ent[1], "dma") for ent in self.dsems if ent[1] > 0]
        for e in self.ENG:
            self.pending[e] = self.pending[e] + list(toks)
        for e in self.CE:
            if self.ecount[e] > 20000:
                self.esem[e] = self._sem("e_" + e)
                self.ecount[e] = 0
        for (b, attr) in self.sem_owners:
            ent = getattr(b, attr)
            if ent is not None:
                self.free_dsems[ent[2]].append(ent)
                setattr(b, attr, None)
        self.sem_owners = []

    def _waits(self, eng, reads, writes):
        toks = list(self.pending[eng])
        self.pending[eng] = []
        for b in reads:
            if b.w is not None:
                toks.append(b.w)
        for b in writes:
            if b.w is not None:
                toks.append(b.w)
            toks.extend(b.r)
        need = {}
        for (sem, val, src) in toks:
            if src == "pe" and eng == "pe":
                continue
            k = id(sem)
            if self.known[eng].get(k, 0) >= val:
                continue
            if k not in need or need[k][1] < val:
                need[k] = (sem, val)
        for k, (sem, val) in need.items():
            self.known[eng][k] = val
        return list(need.values())

    def _mark(self, tok, reads, writes):
        for b in writes:
            b.w = tok
            b.r = []
        for b in reads:
            if b not in writes:
                b.r.append(tok)
                if len(b.r) > 16:
                    best = {}
                    for t in b.r:
                        k = id(t[0])
                        if k not in best or best[k][1] < t[1]:
                            best[k] = t
                    b.r = list(best.values())

    def op(self, eng, name, reads, writes, **kw):
        waits = self._waits(eng, reads, writes)
        self.ecount[eng] += 1
        tok = (self.esem[eng], self.ecount[eng], eng)
        self.ops[eng].append((waits, name, kw, (self.esem[eng], 1)))
        self._mark(tok, reads, writes)
        return tok

    def dma(self, q, out, in_, reads=(), writes=(), final=False, **kw):
        waits = self._waits(q, reads, writes)
        tgt = writes[0] if writes else reads[0]
        attr = ("wsem_" if writes else "rsem_") + q
        cur = getattr(tgt, attr, None)
        if cur is None:
            cur = self.new_dsem(tgt.name, q)
            setattr(tgt, attr, cur)
            self.sem_owners.append((tgt, attr))
        cur[1] += 16
        tok = (cur[0], cur[1], "dma")
        kw = dict(kw)
        kw.update(out=out, in_=in_)
        self.ops[q].append((waits, "dma_start", kw, (cur[0], 16)))
        self._mark(tok, reads, writes)
        if final:
            self.final_tokens.append(tok)
        return tok

    def emit(self, last=False):
        nc = self.nc
        fin = {}
        if last:
            toks = list(self.final_tokens) + [(ent[0], ent[1], "dma") for ent in self.dsems if ent[1] > 0]
            for (sem, val, _) in toks:
                k = id(sem)
                if k not in fin or fin[k][1] < val:
                    fin[k] = (sem, val)
        with nc.Block() as block:
            for e in self.ENG:
                ops = self.ops[e]
                is_last = last and e == "sp"
                if not ops and not is_last:
                    continue

                def body(engobj, ops=ops, is_last=is_last):
                    for (waits, name, kw, inc) in ops:
                        for (sem, val) in waits:
                            engobj.wait_ge(sem, val)
                        getattr(engobj, name)(**kw).then_inc(inc[0], inc[1])
                    if is_last:
                        for (sem, val) in fin.values():
                            engobj.wait_ge(sem, val)

                getattr(block, self.ENGATTR[e])(body)
        self.ops = {e: [] for e in self.ENG}


class Tl:
    def __init__(self, t, name):
        self.t = t
        self.b = Buf(name)


class Ring:
    def __init__(self, items):
        self.items = items
        self.i = 0

    def next(self):
        it = self.items[self.i % len(self.items)]
        self.i += 1
        return it


class Prog:
    def __init__(self, nc, stack):
        self.nc = nc
        self.S = Sched(nc, stack)
        self.debug = False
        self.stop_after = 99

    def din(self, name, shape, dt=F32):
        return self.nc.dram_tensor(name, list(shape), dt, kind="ExternalInput").ap()

    def dout(self, name, shape, dt=F32):
        return self.nc.dram_tensor(name, list(shape), dt, kind="ExternalOutput").ap()

    def dscr(self, name, shape, dt):
        kind = "ExternalOutput" if self.debug else "Internal"
        return self.nc.dram_tensor(name, list(shape), dt, kind=kind).ap()

    def sb(self, st, name, shape, dt):
        self.uid = getattr(self, "uid", 0) + 1
        name = f"{name}_{self.uid}"
        return Tl(st.enter_context(self.nc.sbuf_tensor(name, list(shape), dt)), name)

    def ps(self, st, name):
        return Tl(st.enter_context(self.nc.psum_tensor(name, [128, 512], F32)), name)


O_CQ, O_CKV, O_DQ, O_DK, O_DV, O_GATE = 0, 384, 672, 1184, 1696, 2208
O_KR, O_KRR, O_DQR, O_DKR = 3232, 3328, 3424, 3936
NT = 17


def build_program(debug=False, stop_after=99):
    nc = bass.Bass("TRN2", target_bir_lowering=False)
    with ExitStack() as top:
        P = Prog(nc, top)
        P.debug = debug
        P.stop_after = stop_after
        S = P.S
        G = {}
        G["xk"] = P.din("xk", [T, D])
        cvec = P.din("cvec", [2, D])
        G["tabA"] = P.din("tabA", [2, 32, T])
        G["tabB"] = P.din("tabB", [2, 64, T])
        ada_w = P.din("ada_w", [2, D, 3 * D])
        ada_b = P.din("ada_b", [2, 3 * D])
        ln_g = P.din("ln_g", [2, D])
        ln_b = P.din("ln_b", [2, D])
        identd = P.din("identd", [128, 128])
        G["wA"] = P.din("wA", [D, NWA])
        G["w_uq"] = P.din("w_uq", [384, 1536])
        G["w_ukv"] = P.din("w_ukv", [256, 1024])
        G["qg"] = P.din("qg", [384])
        G["kvg"] = P.din("kvg", [256])
        G["lamv"] = P.din("lamv", [4, 64])
        G["subg"] = P.din("subg", [128])
        G["e_w_out"] = P.din("e_w_out", [D, D])
        G["o_w_in"] = P.din("o_w_in", [D, 2 * D])
        G["o_conv_w"] = P.din("o_conv_w", [4, D])
        G["o_conv_b"] = P.din("o_conv_b", [D])
        G["o_gw"] = P.din("o_gw", [2, 2, 8, 128, 128])
        G["o_gb"] = P.din("o_gb", [2, 2, 8, 128])
        G["o_lam"] = P.din("o_lam", [2, D])
        G["o_w_out"] = P.din("o_w_out", [D, D])
        G["out"] = P.dout("out", [SEQ, D])
        G["B_out"] = Buf("out")
        G["X1"] = P.dscr("X1", [T, D], F32)
        G["B_X1"] = Buf("X1")

        ident = P.sb(top, "ident", [128, 128], BF16)
        ones = P.sb(top, "ones", [128, 128], BF16)
        identf = P.sb(top, "identf", [128, 128], F32)
        banks = [P.ps(top, f"pb{i}") for i in range(8)]
        mod = P.sb(top, "mod", [128, 2, 24, 2], F32)
        gbc = [[P.sb(top, f"gbc{l}{c}", [128, D], F32) for c in range(2 - l)] for l in range(2)]
        lng = [P.sb(top, f"lng{l}", [128, D], F32) for l in range(2)]
        lnb = [P.sb(top, f"lnb{l}", [128, D], F32) for l in range(2)]
        G.update(ident=ident, ones=ones, banks=banks, mod=mod, gbc=gbc, lng=lng, lnb=lnb)

        S.dma("sp", identf.t[:], identd[:, :], writes=[identf.b])
        S.op("dve", "tensor_copy", [identf.b], [ident.b], out=ident.t[:], in_=identf.t[:])
        S.op("pool", "memset", [], [ones.b], ap=ones.t[:], constant=1.0)

        with ExitStack() as ph:
            adw = P.sb(ph, "adw", [128, 8, 3 * D], BF16)
            adb = P.sb(ph, "adb", [128, 24], F32)
            cT = P.sb(ph, "cT", [128, 2, 8], F32)
            sc = P.sb(ph, "sc", [128, 8, 2], BF16)
            scb = [P.sb(ph, f"scb{c}", [128, 8, 128], BF16) for c in range(2)]
            gb_b = P.sb(ph, "gb_b", [128, D], F32)
            S.dma("sp", cT.t[:], cvec.rearrange("c (k p) -> p c k", p=128), writes=[cT.b], allow_slow_non_contiguous=True)
            S.op("act", "activation", [cT.b], [sc.b], out=sc.t[:].rearrange("p k c -> p c k"), in_=cT.t[:], func=AF.Silu)
            for c in range(2):
                S.op("dve", "tensor_copy", [sc.b], [scb[c].b], out=scb[c].t[:], in_=sc.t[:, :, c:c + 1].to_broadcast([128, 8, 128]))
            for l in range(2):
                for j in range(3):
                    S.dma("pool", adw.t[:, :, j * D:(j + 1) * D], ada_w[l, :, j * D:(j + 1) * D].rearrange("(k p) n -> p k n", p=128),
                          writes=[adw.b])
                S.dma("sp", adb.t[:], ada_b[l].rearrange("(j p) -> p j", p=128), writes=[adb.b], allow_slow_non_contiguous=True)
                S.dma("sp", gb_b.t[:], ada_b[l:l + 1, 2 * D:3 * D].broadcast_to([128, D]), writes=[gb_b.b])
                S.dma("sp", lng[l].t[:], ln_g[l:l + 1, :].broadcast_to([128, D]), writes=[lng[l].b])
                S.dma("sp", lnb[l].t[:], ln_b[l:l + 1, :].broadcast_to([128, D]), writes=[lnb[l].b])
                pb = banks[l]
                for j in range(24):
                    for k in range(8):
                        S.op("pe", "matmul", [adw.b, sc.b], [pb.b], out=pb.t[:, j * 2:j * 2 + 2], lhsT=adw.t[:, k, j * 128:(j + 1) * 128],
                             rhs=sc.t[:, k, :], start=(k == 0), stop=(k == 7))
                S.op("dve", "tensor_tensor", [pb.b, adb.b], [mod.b], out=mod.t[:, l, :, :], in0=pb.t[:, 0:48].rearrange("p (j c) -> p j c", c=2),
                     in1=adb.t[:].unsqueeze(2).to_broadcast([128, 24, 2]), op=ALU.add)
                S.op("dve", "tensor_scalar", [mod.b], [mod.b], out=mod.t[:, l, 8:16, :], in0=mod.t[:, l, 8:16, :], scalar1=1.0, scalar2=None, op0=ALU.add)
                for c in range(2 - l):
                    for hh in range(2):
                        pg = banks[2 + c * 2 + hh]
                        for k in range(8):
                            S.op("pe", "matmul", [adw.b, scb[c].b], [pg.b], out=pg.t[:, :], lhsT=scb[c].t[:, k, :],
                                 rhs=adw.t[:, k, 2 * D + hh * 512:2 * D + (hh + 1) * 512], start=(k == 0), stop=(k == 7))
                        S.op("dve", "tensor_tensor", [pg.b, gb_b.b], [gbc[l][c].b], out=gbc[l][c].t[:, hh * 512:(hh + 1) * 512], in0=pg.t[:, :],
                             in1=gb_b.t[:, hh * 512:(hh + 1) * 512], op=ALU.add)
            S.barrier()
            S.emit()

        if stop_after >= 1:
            layer0(P, G)
        if stop_after >= 4:
            layer1(P, G)
        S.emit(last=True)
    return nc


def load_xm_tile(P, G, L, src, B_src, t, xb, xm, ts_r, cond):
    S = P.S
    ident, mod = G["ident"], G["mod"]
    W = 512 if t < 16 else 256
    ns = W // 128
    for fp in range(4):
        tsv, tsb = ts_r.next()
        for hf in range(2):
            fc = fp * 2 + hf
            for s in range(ns):
                S.op("pe", "transpose", [xb.b, ident.b], [tsb], out=tsv[:, hf * 512 + s * 128:hf * 512 + (s + 1) * 128],
                     in_=xb.t[:, s, fc * 128:(fc + 1) * 128], identity=ident.t[:])
        fc = fp * 2
        S.op("act", "activation", [tsb, mod.b], [xm.b], out=xm.t[:, fc, 0:W], in_=tsv[:, 0:W], func=AF.Identity,
             bias=mod.t[:, L, fc, cond:cond + 1], scale=mod.t[:, L, 8 + fc, cond:cond + 1])
        fc = fp * 2 + 1
        S.op("dve", "tensor_scalar", [tsb, mod.b], [xm.b], out=xm.t[:, fc, 0:W], in0=tsv[:, 512:512 + W],
             scalar1=mod.t[:, L, 8 + fc, cond:cond + 1], scalar2=mod.t[:, L, fc, cond:cond + 1], op0=ALU.mult, op1=ALU.add)


def layer0(P, G):
    S = P.S
    xk, tabA, tabB, wA, w_uq, w_ukv, qg, kvg, lamv, subg, e_w_out = (G[k] for k in
        "xk tabA tabB wA w_uq w_ukv qg kvg lamv subg e_w_out".split())
    ident, ones, banks, mod = (G[k] for k in "ident ones banks mod".split())
    L = 0
    KTm = P.dscr("KTm", [8, 96, T], BF16)
    Vm = P.dscr("Vm", [8, 128, NKC, 64], BF16)
    QTm = P.dscr("QTm", [8, 96, T], BF16)
    KTd = P.dscr("KTd", [4, 128, T], BF16)
    Vd = P.dscr("Vd", [4, 2, 128, NKC, 64], BF16)
    QTd = P.dscr("QTd", [4, 128, T], BF16)
    SG = P.dscr("SG", [8, 128, T], BF16)
    B_KTm = [Buf(f"KTm{h}") for h in range(8)]
    B_Vm = Buf("Vm")
    B_QTm = [Buf(f"QTm{h}") for h in range(8)]
    B_KTd = [Buf(f"KTd{h}") for h in range(4)]
    B_Vd = Buf("Vd")
    B_QTd = [Buf(f"QTd{h}") for h in range(4)]
    B_SG = Buf("SG")

    with ExitStack() as ph:
        wa = P.sb(ph, "wa", [128, 8, NWA], BF16)
        wuq = P.sb(ph, "wuq", [128, 3, 1536], BF16)
        wukv = P.sb(ph, "wukv", [128, 2, 1024], BF16)
        qgc = P.sb(ph, "qgc", [128, 3], F32)
        kvgc = P.sb(ph, "kvgc", [128, 2], F32)
        epsc = P.sb(ph, "epsc", [128, 1], F32)
        S.op("dve", "memset", [], [epsc.b], ap=epsc.t[:], constant=RMS_EPS)
        for j0 in range(0, NWA, 1112):
            S.dma("pool", wa.t[:, :, j0:j0 + 1112], wA[:, j0:j0 + 1112].rearrange("(k p) n -> p k n", p=128), writes=[wa.b])
        S.dma("pool", wuq.t[:], w_uq.rearrange("(k p) n -> p k n", p=128), writes=[wuq.b])
        S.dma("pool", wukv.t[:], w_ukv.rearrange("(k p) n -> p k n", p=128), writes=[wukv.b])
        S.dma("sp", qgc.t[:], qg.rearrange("(k p) -> p k", p=128), writes=[qgc.b], allow_slow_non_contiguous=True)
        S.dma("sp", kvgc.t[:], kvg.rearrange("(k p) -> p k", p=128), writes=[kvgc.b], allow_slow_non_contiguous=True)

        xb_r = Ring([P.sb(ph, f"xb{i}", [128, 4, D], BF16) for i in range(2)])
        xm_r = Ring([P.sb(ph, f"xm{i}", [128, 8, 512], BF16) for i in range(2)])
        tA_r = Ring([P.sb(ph, f"tA{i}", [128, 2, 512], F32) for i in range(2)])
        tB_r = Ring([P.sb(ph, f"tB{i}", [128, 2, 512], F32) for i in range(2)])
        ckvn_r = Ring([P.sb(ph, f"ckvn{i}", [128, 2, 512], BF16) for i in range(2)])
        cqn_r = Ring([P.sb(ph, f"cqn{i}", [128, 3, 512], BF16) for i in range(2)])
        sq_r = Ring([P.sb(ph, f"sq{i}", [128, 512], BF16) for i in range(3)])
        rs_r = Ring([P.sb(ph, f"rs{i}", [128, 512], F32) for i in range(2)])
        f1_r = Ring([P.sb(ph, f"f1_{i}", [128, 512], F32) for i in range(3)])
        f2_r = Ring([P.sb(ph, f"f2_{i}", [128, 512], F32) for i in range(3)])
        ob_r = Ring([P.sb(ph, f"ob{i}", [128, 512], BF16) for i in range(6)])
        pbr = Ring(banks[2:8])
        ts_r = Ring([(banks[bi].t[:].bitcast(BF16), banks[bi].b) for bi in range(2)])

        def mm_group(pb, M, W, lhs, rhs, rd):
            nk = len(lhs)
            for k in range(nk):
                S.op("pe", "matmul", rd, [pb.b], out=pb.t[0:M, 0:W], lhsT=lhs[k], rhs=rhs[k], start=(k == 0), stop=(k == nk - 1))

        def rstd_bc(src_tiles, W, nfeat):
            sqs = []
            for pbs in src_tiles:
                sq = sq_r.next()
                S.op("act", "activation", [pbs.b], [sq.b], out=sq.t[:, 0:W], in_=pbs.t[:, 0:W], func=AF.Square)
                sqs.append(sq)
            pss = pbr.next()
            for i, sq in enumerate(sqs):
                S.op("pe", "matmul", [ones.b, sq.b], [pss.b], out=pss.t[:, 0:W], lhsT=ones.t[:, :], rhs=sq.t[:, 0:W],
                     start=(i == 0), stop=(i == len(sqs) - 1))
            rs = rs_r.next()
            S.op("act", "activation", [pss.b, epsc.b], [rs.b], out=rs.t[:, 0:W], in_=pss.t[:, 0:W], func=AF.Ln, scale=1.0 / nfeat, bias=epsc.t[:, 0:1])
            S.op("act", "activation", [rs.b], [rs.b], out=rs.t[:, 0:W], in_=rs.t[:, 0:W], func=AF.Exp, scale=-0.5)
            return rs

        def rope(pm, pr, tab, r0, r1, W, outap, outb):
            f1 = f1_r.next()
            f2 = f2_r.next()
            S.op("dve", "tensor_tensor", [pm.b, tab.b], [f1.b], out=f1.t[r0:r1, 0:W], in0=pm.t[r0:r1, 0:W], in1=tab.t[r0:r1, 0, 0:W], op=ALU.mult)
            S.op("dve", "tensor_tensor", [pr.b, tab.b], [f2.b], out=f2.t[r0:r1, 0:W], in0=pr.t[r0:r1, 0:W], in1=tab.t[r0:r1, 1, 0:W], op=ALU.mult)
            S.op("pool", "tensor_tensor", [f1.b, f2.b], [outb], out=outap, in0=f1.t[r0:r1, 0:W], in1=f2.t[r0:r1, 0:W], op=ALU.add)

        def loads(t):
            W = 512 if t < 16 else 256
            ns = W // 128
            t0 = t * 512
            xb = xb_r.next()
            S.dma("pool", xb.t[:, 0:ns, :], xk[t0:t0 + W, :].rearrange("(s p) f -> p s f", p=128), writes=[xb.b])
            tA = tA_r.next()
            tB = tB_r.next()
            S.dma("sp", tA.t[64:96, :, 0:W], tabA[:, :, t0:t0 + W].rearrange("c p w -> p c w"), writes=[tA.b])
            for hh in range(2):
                S.dma("sp", tB.t[hh * 64:(hh + 1) * 64, :, 0:W], tabB[:, :, t0:t0 + W].rearrange("c p w -> p c w"), writes=[tB.b])
            return xb, tA, tB

        nxt = loads(0)
        for t in range(NT):
            W = 512 if t < 16 else 256
            ns = W // 128
            t0 = t * 512
            cond = 1 if t == 16 else 0
            xb, tA, tB = nxt
            if t + 1 < NT:
                nxt = loads(t + 1)
            xm = xm_r.next()
            load_xm_tile(P, G, L, xk, None, t, xb, xm, ts_r, cond)

            def proj(pb, M, col0):
                mm_group(pb, M, W, [wa.t[:, k, col0:col0 + M] for k in range(8)], [xm.t[:, k, 0:W] for k in range(8)], [wa.b, xm.b])

            pc = [pbr.next() for _ in range(2)]
            for i in range(2):
                proj(pc[i], 128, O_CKV + i * 128)
            rs = rstd_bc(pc, W, 256)
            ckvn = ckvn_r.next()
            for i in range(2):
                S.op("dve", "scalar_tensor_tensor", [pc[i].b, kvgc.b, rs.b], [ckvn.b], out=ckvn.t[:, i, 0:W], in0=pc[i].t[:, 0:W],
                     scalar=kvgc.t[:, i:i + 1], in1=rs.t[:, 0:W], op0=ALU.mult, op1=ALU.mult)
            for j in range(4):
                pb = pbr.next()
                mm_group(pb, 128, W, [wukv.t[:, k, j * 128:(j + 1) * 128] for k in range(2)], [ckvn.t[:, k, 0:W] for k in range(2)], [wukv.b, ckvn.b])
                ob = ob_r.next()
                S.op("act", "activation", [pb.b], [ob.b], out=ob.t[:, 0:W], in_=pb.t[:, 0:W], func=AF.Identity)
                for hh in range(2):
                    S.dma("sp", KTm[2 * j + hh, 0:64, t0:t0 + W], ob.t[hh * 64:(hh + 1) * 64, 0:W], reads=[ob.b], writes=[B_KTm[2 * j + hh]])
            for s in range(ns):
                pb = pbr.next()
                mm_group(pb, 128, 512, [ckvn.t[:, k, s * 128:(s + 1) * 128] for k in range(2)], [wukv.t[:, k, 512:1024] for k in range(2)], [wukv.b, ckvn.b])
                ob = ob_r.next()
                S.op("dve", "tensor_copy", [pb.b], [ob.b], out=ob.t[:, :], in_=pb.t[:, :])
                S.dma("sp", Vm[:, :, t0 // 128 + s, :].rearrange("h p d -> p h d"), ob.t[:, :].rearrange("p (h d) -> p h d", h=8),
                      reads=[ob.b], writes=[B_Vm])
            pm = pbr.next()
            pr = pbr.next()
            proj(pm, 96, O_KR)
            proj(pr, 96, O_KRR)
            ob = ob_r.next()
            rope(pm, pr, tA, 64, 96, W, ob.t[64:96, 0:W], ob.b)
            for h in range(8):
                S.dma("sp", KTm[h, 64:96, t0:t0 + W], ob.t[64:96, 0:W], reads=[ob.b], writes=[B_KTm[h]])
            for h in range(4):
                pm = pbr.next()
                pr = pbr.next()
                proj(pm, 128, O_DK + h * 128)
                proj(pr, 128, O_DKR + h * 128)
                ob = ob_r.next()
                rope(pm, pr, tB, 0, 128, W, ob.t[:, 0:W], ob.b)
                S.dma("sp", KTd[h, :, t0:t0 + W], ob.t[:, 0:W], reads=[ob.b], writes=[B_KTd[h]])
            for s in range(ns):
                pb = pbr.next()
                mm_group(pb, 128, 512, [xm.t[:, k, s * 128:(s + 1) * 128] for k in range(8)], [wa.t[:, k, O_DV:O_DV + 512] for k in range(8)], [wa.b, xm.b])
                ob = ob_r.next()
                S.op("act", "activation", [pb.b], [ob.b], out=ob.t[:, :], in_=pb.t[:, :], func=AF.Identity)
                S.dma("sp", Vd[:, :, :, t0 // 128 + s, :].rearrange("h j p d -> p (h j) d"), ob.t[:, :].rearrange("p (g d) -> p g d", g=8),
                      reads=[ob.b], writes=[B_Vd])
            pq = [pbr.next() for _ in range(3)]
            for i in range(3):
                proj(pq[i], 128, O_CQ + i * 128)
            rs = rstd_bc(pq, W, 384)
            cqn = cqn_r.next()
            for i in range(3):
                S.op("dve", "scalar_tensor_tensor", [pq[i].b, qgc.b, rs.b], [cqn.b], out=cqn.t[:, i, 0:W], in0=pq[i].t[:, 0:W],
                     scalar=qgc.t[:, i:i + 1], in1=rs.t[:, 0:W], op0=ALU.mult, op1=ALU.mult)
            for h in range(8):
                pm = pbr.next()
                pr = pbr.next()
                mm_group(pm, 96, W, [wuq.t[:, k, h * 96:(h + 1) * 96] for k in range(3)], [cqn.t[:, k, 0:W] for k in range(3)], [wuq.b, cqn.b])
                mm_group(pr, 96, W, [wuq.t[:, k, 768 + h * 96:768 + (h + 1) * 96] for k in range(3)], [cqn.t[:, k, 0:W] for k in range(3)], [wuq.b, cqn.b])
                ob = ob_r.next()
                S.op("act", "activation", [pm.b], [ob.b], out=ob.t[0:64, 0:W], in_=pm.t[0:64, 0:W], func=AF.Identity)
                rope(pm, pr, tA, 64, 96, W, ob.t[64:96, 0:W], ob.b)
                S.dma("sp", QTm[h, :, t0:t0 + W], ob.t[0:96, 0:W], reads=[ob.b], writes=[B_QTm[h]])
            for h in range(4):
                pm = pbr.next()
                pr = pbr.next()
                proj(pm, 128, O_DQ + h * 128)
                proj(pr, 128, O_DQR + h * 128)
                ob = ob_r.next()
                rope(pm, pr, tB, 0, 128, W, ob.t[:, 0:W], ob.b)
                S.dma("sp", QTd[h, :, t0:t0 + W], ob.t[:, 0:W], reads=[ob.b], writes=[B_QTd[h]])
            for c in range(8):
                pb = pbr.next()
                proj(pb, 128, O_GATE + c * 128)
                ob = ob_r.next()
                S.op("act", "activation", [pb.b], [ob.b], out=ob.t[:, 0:W], in_=pb.t[:, 0:W], func=AF.Silu)
                S.dma("sp", SG[c, :, t0:t0 + W], ob.t[:, 0:W], reads=[ob.b], writes=[B_SG])
        S.barrier()
        S.emit()
    if P.stop_after == 1:
        return

    qtiles = [(i * 512, 512, 0) for i in range(16)] + [(SEQ, 256, 64)]
    OB = P.dscr("OB", [8, 128, T], BF16)
    B_OB = Buf("OB")
    with ExitStack() as ph:
        KT2 = [P.sb(ph, f"KT{i}", [128, T], BF16) for i in range(2)]
        VA2 = [P.sb(ph, f"VA{i}", [128, NKC, 2, 128], BF16) for i in range(2)]
        qm_r = Ring([P.sb(ph, f"qm{i}", [128, 512], BF16) for i in range(3)])
        qd_r = Ring([(P.sb(ph, f"qa{i}", [128, 512], BF16), P.sb(ph, f"qb{i}", [128, 512], BF16)) for i in range(3)])
        for (qa, qb) in qd_r.items:
            S.op("pool", "memset", [], [qa.b], ap=qa.t[:], constant=0.0)
            S.op("pool", "memset", [], [qb.b], ap=qb.t[:], constant=0.0)
        pt_r = Ring([P.sb(ph, f"pt{i}", [128, 512], BF16) for i in range(14)])
        tmp_r = Ring([P.sb(ph, f"ptsum{i}", [128, 512], BF16) for i in range(6)])
        rl_r = Ring([P.sb(ph, f"rl{i}", [128, 512], F32) for i in range(3)])
        onesf = P.sb(ph, "onesf", [128, 128], F32)
        S.op("pool", "memset", [], [onesf.b], ap=onesf.t[:], constant=1.0)
        ot_r = Ring([P.sb(ph, f"ot{i}", [64, 512], BF16) for i in range(6)])
        ev_r = Ring([[P.sb(ph, f"ev{i}_{k}", [128, 512], F32) for k in range(4)] for i in range(1)])
        dd_r = Ring([P.sb(ph, f"dd{i}", [128, 512], F32) for i in range(2)])
        sq2_r = Ring([P.sb(ph, f"sqd{i}", [128, 512], BF16) for i in range(2)])
        rs2_r = Ring([P.sb(ph, f"rs2{i}", [128, 512], F32) for i in range(2)])
        otd_r = Ring([P.sb(ph, f"otd{i}", [128, 512], BF16) for i in range(2)])
        lam4 = P.sb(ph, "lam4", [128, 4, 64], F32)
        lamp = P.sb(ph, "lamp", [128, 2, 64], F32)
        lams = P.sb(ph, "lams", [128, 2], F32)
        neglam = P.sb(ph, "neglam", [128, 1], F32)
        sgc = P.sb(ph, "sgc", [128, 1], F32)
        eps2 = P.sb(ph, "eps2", [128, 1], F32)
        S.op("dve", "memset", [], [eps2.b], ap=eps2.t[:], constant=RMS_EPS)
        for i in range(2):
            S.op("pool", "memset", [], [VA2[i].b], ap=VA2[i].t[:, :, :, 64:128], constant=1.0)
        S.dma("sp", lam4.t[:].rearrange("p a d -> p (a d)"), lamv.rearrange("(o a) d -> o (a d)", o=1).broadcast_to([128, 256]), writes=[lam4.b])
        S.op("dve", "tensor_tensor", [lam4.b], [lamp.b], out=lamp.t[:], in0=lam4.t[:, 0:4:2, :], in1=lam4.t[:, 1:4:2, :], op=ALU.mult)
        S.op("dve", "tensor_reduce", [lamp.b], [lams.b], out=lams.t[:], in_=lamp.t[:], axis=AX.X, op=ALU.add)
        S.op("act", "activation", [lams.b], [lams.b], out=lams.t[:], in_=lams.t[:], func=AF.Exp)
        S.op("dve", "tensor_tensor", [lams.b], [neglam.b], out=neglam.t[:], in0=lams.t[:, 1:2], in1=lams.t[:, 0:1], op=ALU.subtract)
        S.op("dve", "tensor_scalar", [neglam.b], [neglam.b], out=neglam.t[:], in0=neglam.t[:], scalar1=-LAMBDA_INIT0, scalar2=None, op0=ALU.add)
        S.dma("sp", sgc.t[:], subg.rearrange("(p o) -> p o", o=1), writes=[sgc.b], allow_slow_non_contiguous=True)
        S.op("dve", "tensor_scalar", [sgc.b], [sgc.b], out=sgc.t[:], in0=sgc.t[:], scalar1=1.0 - LAMBDA_INIT0, scalar2=None, op0=ALU.mult)

        def load_head(hh, slot):
            KT, VA = KT2[slot], VA2[slot]
            if hh < 8:
                S.dma("sp", KT.t[0:96, :], KTm[hh], reads=[B_KTm[hh]], writes=[KT.b])
                for c0 in range(0, NKC, 11):
                    S.dma("sp", VA.t[:, c0:c0 + 11, 0, 0:64], Vm[hh, :, c0:c0 + 11, :], reads=[B_Vm], writes=[VA.b])
            else:
                h = hh - 8
                S.dma("sp", KT.t[:, :], KTd[h], reads=[B_KTd[h]], writes=[KT.b])
                VDv = VA.t[:].rearrange("p c j d -> p (c j) d")
                for j in range(2):
                    for c0 in range(0, NKC, 11):
                        S.dma("sp", VDv[:, c0:c0 + 11, j * 64:(j + 1) * 64], Vd[h, j, :, c0:c0 + 11, :], reads=[B_Vd], writes=[VA.b])

        qjobs = {}

        def load_q(hh, qi):
            if hh >= 12:
                return
            gq, W, kc0 = qtiles[qi]
            if hh < 8:
                qm = qm_r.next()
                S.dma("sp", qm.t[0:96, 0:W], QTm[hh, :, gq:gq + W], reads=[B_QTm[hh]], writes=[qm.b])
                qjobs[(hh, qi)] = (qm, qm)
            else:
                h = hh - 8
                qa, qb = qd_r.next()
                S.dma("sp", qa.t[0:64, 0:W], QTd[h, 0:64, gq:gq + W], reads=[B_QTd[h]], writes=[qa.b])
                S.dma("sp", qb.t[64:128, 0:W], QTd[h, 64:128, gq:gq + W], reads=[B_QTd[h]], writes=[qb.b])
                qjobs[(hh, qi)] = (qa, qb)

        LOOK = 3
        load_head(0, 0)
        load_q(0, 0)
        for hh in range(12):
            slot = hh % 2
            KT, VA = KT2[slot], VA2[slot]
            if hh + 1 < 12:
                load_head(hh + 1, 1 - slot)
            mla = hh < 8
            nmap, nv = (1, 1) if mla else (2, 1)
            VDv = VA.t[:].rearrange("p c j d -> p (c j) d")
            scale = MLA_SCALE if mla else DIFF_SCALE
            sc_r = Ring(banks[0:4]) if mla else Ring(banks[0:3])
            accr = Ring(banks[4:8])
            units = []
            for qi, (gq, W, kc0) in enumerate(qtiles):
                accs = [[accr.next()]] if mla else [[banks[4], banks[6]], [banks[5], banks[7]]]
                for kc in range(kc0, NKC):
                    for m in range(nmap):
                        units.append((qi, kc, m, accs, kc == kc0, kc == NKC - 1, kc == NKC - 1 and m == nmap - 1))
            pts = {}
            pair_pend = {}
            lacc_started = {}
            deferred = []

            def post_mla(qi, accs):
                gq, W, kc0 = qtiles[qi]
                acc = accs[0][0]
                rl = rl_r.next()
                S.op("dve", "reciprocal", [acc.b], [rl.b], out=rl.t[0:64, 0:W], in_=acc.t[64:128, 0:W])
                ot = ot_r.next()
                S.op("dve", "tensor_tensor", [acc.b, rl.b], [ot.b], out=ot.t[:, 0:W], in0=acc.t[0:64, 0:W], in1=rl.t[0:64, 0:W], op=ALU.mult)
                oc, op0 = hh // 2, (hh % 2) * 64
                S.dma("pool", OB[oc, op0:op0 + 64, gq:gq + W], ot.t[:, 0:W], reads=[ot.b], writes=[B_OB])

            def post_diff_stages(qi, accs):
                gq, W, kc0 = qtiles[qi]
                h = hh - 8
                ev = ev_r.next()
                aux = banks[3]
                rls = [rl_r.next(), rl_r.next()]
                dd = dd_r.next()
                sq = sq2_r.next()
                rs2 = rs2_r.next()
                ot = otd_r.next()

                def st_evac():
                    S.op("dve", "tensor_copy", [accs[0][0].b], [ev[0].b], out=ev[0].t[:, 0:W], in_=accs[0][0].t[:, 0:W])
                    S.op("act", "activation", [accs[1][0].b], [ev[1].b], out=ev[1].t[:, 0:W], in_=accs[1][0].t[:, 0:W], func=AF.Identity)
                    S.op("act", "activation", [accs[0][1].b], [ev[2].b], out=ev[2].t[:, 0:W], in_=accs[0][1].t[:, 0:W], func=AF.Identity)
                    S.op("dve", "tensor_copy", [accs[1][1].b], [ev[3].b], out=ev[3].t[:, 0:W], in_=accs[1][1].t[:, 0:W])

                def st_lbc(m):
                    S.op("pe", "matmul", [onesf.b, ev[2 + m].b], [aux.b], out=aux.t[:, 0:W], lhsT=onesf.t[:, :], rhs=ev[2 + m].t[:, 0:W], start=True, stop=True)

                def st_recip(m):
                    S.op("act", "activation", [aux.b], [rls[m].b], out=rls[m].t[:, 0:W], in_=aux.t[:, 0:W], func=AF.Ln)
                    S.op("act", "activation", [rls[m].b], [rls[m].b], out=rls[m].t[:, 0:W], in_=rls[m].t[:, 0:W], func=AF.Exp, scale=-1.0)

                def st_norm():
                    S.op("dve", "tensor_tensor", [ev[0].b, rls[0].b], [ev[0].b], out=ev[0].t[:, 0:W], in0=ev[0].t[:, 0:W], in1=rls[0].t[:, 0:W], op=ALU.mult)
                    S.op("pool", "tensor_tensor", [ev[1].b, rls[1].b], [ev[1].b], out=ev[1].t[:, 0:W], in0=ev[1].t[:, 0:W], in1=rls[1].t[:, 0:W], op=ALU.mult)

                def st_diff():
                    S.op("dve", "scalar_tensor_tensor", [ev[0].b, ev[1].b, neglam.b], [dd.b], out=dd.t[:, 0:W], in0=ev[1].t[:, 0:W], scalar=neglam.t[:, 0:1],
                         in1=ev[0].t[:, 0:W], op0=ALU.mult, op1=ALU.add)
                    S.op("pool", "tensor_tensor", [dd.b], [sq.b], out=sq.t[:, 0:W], in0=dd.t[:, 0:W], in1=dd.t[:, 0:W], op=ALU.mult)

                def st_ss():
                    S.op("pe", "matmul", [ones.b, sq.b], [aux.b], out=aux.t[:, 0:W], lhsT=ones.t[:, :], rhs=sq.t[:, 0:W], start=True, stop=True)

                def st_rs():
                    S.op("act", "activation", [aux.b, eps2.b], [rs2.b], out=rs2.t[:, 0:W], in_=aux.t[:, 0:W], func=AF.Ln, scale=1.0 / 128, bias=eps2.t[:, 0:1])
                    S.op("act", "activation", [rs2.b], [rs2.b], out=rs2.t[:, 0:W], in_=rs2.t[:, 0:W], func=AF.Exp, scale=-0.5)

                def st_out():
                    S.op("dve", "scalar_tensor_tensor", [dd.b, sgc.b, rs2.b], [ot.b], out=ot.t[:, 0:W],
                         in0=dd.t[:, 0:W], scalar=sgc.t[:, 0:1], in1=rs2.t[:, 0:W], op0=ALU.mult, op1=ALU.mult)
                    S.dma("pool", OB[4 + h, :, gq:gq + W], ot.t[:, 0:W], reads=[ot.b], writes=[B_OB])

                return [(0, st_evac), (2, lambda: st_lbc(0)), (3, lambda: st_recip(0)), (2, lambda: st_lbc(1)), (3, lambda: st_recip(1)),
                        (3, st_norm), (4, st_diff), (4, st_ss), (3, st_rs), (3, st_out)]

            nun = len(units)
            for i in range(nun + LOOK):
                if i < nun:
                    qi, kc, m, accs, first, last, qlast = units[i]
                    gq, W, kc0 = qtiles[qi]
                    if first and m == 0:
                        nq = (hh, qi + 1) if qi + 1 < len(qtiles) else (hh + 1, 0)
                        load_q(*nq)
                    qt = qjobs[(hh, qi)][m]
                    r1 = 96 if mla else 128
                    sb_ = sc_r.next()
                    S.op("pe", "matmul", [KT.b, qt.b], [sb_.b], out=sb_.t[:, 0:W], lhsT=KT.t[0:r1, kc * 128:(kc + 1) * 128],
                         rhs=qt.t[0:r1, 0:W], start=True, stop=True)
                    pt = pt_r.next()
                    S.op("act", "activation", [sb_.b], [pt.b], out=pt.t[:, 0:W], in_=sb_.t[:, 0:W], func=AF.Exp, scale=scale)
                    pts[i] = pt
                j_ = i - LOOK
                if j_ >= 0:
                    qi, kc, m, accs, first, last, qlast = units[j_]
                    gq, W, kc0 = qtiles[qi]
                    pt = pts.pop(j_)
                    acc = accs[m][0]
                    if mla:
                        S.op("pe", "matmul", [VA.b, pt.b], [acc.b], out=acc.t[:, 0:W], lhsT=VA.t[:, kc, 0, :], rhs=pt.t[:, 0:W], start=first, stop=last)
                    else:
                        S.op("pe", "matmul", [VA.b, pt.b], [acc.b], out=acc.t[:, 0:W], lhsT=VDv[:, kc, :], rhs=pt.t[:, 0:W], start=first, stop=last)
                        lacc = accs[m][1]
                        pend = pair_pend.setdefault(m, [])
                        pend.append(pt)
                        if len(pend) == 4 or last:
                            pair_pend[m] = []

                            def bsum(x, y):
                                t = tmp_r.next()
                                S.op("dve", "tensor_tensor", [x.b, y.b], [t.b], out=t.t[:, 0:W], in0=x.t[:, 0:W], in1=y.t[:, 0:W], op=ALU.add)
                                return t

                            src = pend[0]
                            if len(pend) >= 2:
                                src = bsum(pend[0], pend[1])
                            if len(pend) == 3:
                                src = bsum(src, pend[2])
                            elif len(pend) == 4:
                                src = bsum(src, bsum(pend[2], pend[3]))
                            if not lacc_started.get((qi, m)):
                                lacc_started[(qi, m)] = True
                                S.op("dve", "tensor_copy", [src.b], [lacc.b], out=lacc.t[:, 0:W], in_=src.t[:, 0:W])
                            else:
                                S.op("dve", "tensor_tensor", [lacc.b, src.b], [lacc.b], out=lacc.t[:, 0:W], in0=lacc.t[:, 0:W], in1=src.t[:, 0:W], op=ALU.add)
                    if qlast:
                        if mla:
                            post_mla(qi, accs)
                        else:
                            while deferred:
                                deferred.pop(0)[1]()
                            deferred.extend([list(x) for x in post_diff_stages(qi, accs)])
                    if deferred:
                        if deferred[0][0] <= 0:
                            deferred.pop(0)[1]()
                        else:
                            deferred[0][0] -= 1
            while deferred:
                deferred.pop(0)[1]()
        S.barrier()
        S.emit()

    with ExitStack() as p3:
        wo = P.sb(p3, "wo", [128, 8, D], BF16)
        S.dma("pool", wo.t[:], e_w_out.rearrange("(k p) n -> p k n", p=128), writes=[wo.b])
        out_proj_ln(P, G, L, qtiles, OB, B_OB, None, wo, SG, B_SG, xk, None, G["X1"], G["B_X1"], p3)
        S.barrier()
        S.emit()


def out_proj_ln(P, G, L, qtiles, OB, B_OB, HG, wo, SG, B_SG, xres, B_res, xout, B_out, st):
    S = P.S
    banks, gbc, lng, lnb = G["banks"], G["gbc"], G["lng"], G["lnb"]
    sg_r = Ring([P.sb(st, f"sg{i}", [128, 8, 512], BF16) for i in range(3)])
    og_r = Ring([P.sb(st, f"og{i}", [128, 8, 512], BF16) for i in range(2)]) if OB is not None else None
    ob_r = Ring([P.sb(st, f"obt{i}", [128, 8, 512], BF16) for i in range(3)]) if OB is not None else None
    xr_r = Ring([P.sb(st, f"xr{i}", [128, D], F32) for i in range(2)])
    v_r = Ring([P.sb(st, f"vv{i}", [128, D], F32) for i in range(3)])
    o_r = Ring([P.sb(st, f"oo{i}", [128, D], F32) for i in range(2)])
    st_r = Ring([P.sb(st, f"bst{i}", [128, 2, 6], F32) for i in range(2)])
    mv_r = Ring([P.sb(st, f"mv{i}", [128, 4], F32) for i in range(3)])
    epsl = P.sb(st, "epsl", [128, 1], F32)
    S.op("dve", "memset", [], [epsl.b], ap=epsl.t[:], constant=LN_EPS)
    yb = Ring(banks[0:8])
    pend = None

    def ln_apply(v, mv, r0):
        o = o_r.next()
        S.op("dve", "scalar_tensor_tensor", [v.b, mv.b, lng[L].b], [o.b], out=o.t[:], in0=v.t[:], scalar=mv.t[:, 0:1], in1=lng[L].t[:],
             op0=ALU.subtract, op1=ALU.mult)
        S.op("dve", "scalar_tensor_tensor", [o.b, mv.b, lnb[L].b], [o.b], out=o.t[:], in0=o.t[:], scalar=mv.t[:, 2:3], in1=lnb[L].t[:],
             op0=ALU.mult, op1=ALU.add)
        S.dma("pool", xout[r0:r0 + 128, :], o.t[:], reads=[o.b], writes=[B_out], final=True)

    def prep(qi):
        gq, W, kc0 = qtiles[qi]
        sg = sg_r.next()
        if OB is not None:
            S.dma("sp", sg.t[:, :, 0:W], SG[:, :, gq:gq + W].rearrange("c p w -> p c w"), reads=[B_SG], writes=[sg.b])
            obt = ob_r.next()
            S.dma("sp", obt.t[:, :, 0:W], OB[:, :, gq:gq + W].rearrange("c p w -> p c w"), reads=[B_OB], writes=[obt.b])
            return (sg, obt)
        S.dma("sp", sg.t[:, :, 0:W], HG[:, :, gq:gq + W].rearrange("c p w -> p c w"), reads=[B_SG], writes=[sg.b])
        return (sg, None)

    def make_og(qi):
        gq, W, kc0 = qtiles[qi]
        sg, obt = loaded.pop(qi)
        if obt is None:
            return sg
        og = og_r.next()
        S.op("dve", "tensor_tensor", [obt.b, sg.b], [og.b], out=og.t[:, :, 0:W], in0=obt.t[:, :, 0:W], in1=sg.t[:, :, 0:W], op=ALU.mult)
        return og

    nq = len(qtiles)
    loaded = {qi: prep(qi) for qi in range(min(2, nq))}
    ogs = {0: make_og(0)}
    for qi, (gq, W, kc0) in enumerate(qtiles):
        cond = 1 if gq >= SEQ else 0
        if qi + 2 < nq:
            loaded[qi + 2] = prep(qi + 2)
        if qi + 1 < nq:
            ogs[qi + 1] = make_og(qi + 1)
        og = ogs.pop(qi)
        for s in range(W // 128):
            r0 = gq + s * 128
            xr = xr_r.next()
            S.dma("sp", xr.t[:], xres[r0:r0 + 128, :], reads=([B_res] if B_res is not None else []), writes=[xr.b])
            ys = [yb.next() for _ in range(2)]
            for hh in range(2):
                for k in range(8):
                    S.op("pe", "matmul", [og.b, wo.b], [ys[hh].b], out=ys[hh].t[:, :], lhsT=og.t[:, k, s * 128:(s + 1) * 128],
                         rhs=wo.t[:, k, hh * 512:(hh + 1) * 512], start=(k == 0), stop=(k == 7))
            v = v_r.next()
            for hh in range(2):
                S.op("dve", "tensor_tensor", [ys[hh].b, gbc[L][cond].b], [v.b], out=v.t[:, hh * 512:(hh + 1) * 512], in0=ys[hh].t[:, :],
                     in1=gbc[L][cond].t[:, hh * 512:(hh + 1) * 512], op=ALU.mult)
            S.op("dve", "scalar_tensor_tensor", [xr.b, v.b], [v.b], out=v.t[:], in0=xr.t[:], scalar=ALPHA, in1=v.t[:], op0=ALU.mult, op1=ALU.add)
            bst = st_r.next()
            for hh in range(2):
                S.op("dve", "bn_stats", [v.b], [bst.b], out=bst.t[:, hh, :], in_=v.t[:, hh * 512:(hh + 1) * 512])
            mv = mv_r.next()
            S.op("dve", "bn_aggr", [bst.b], [mv.b], out=mv.t[:, 0:2], in_=bst.t[:].rearrange("p a b -> p (a b)"))
            S.op("act", "activation", [mv.b, epsl.b], [mv.b], out=mv.t[:, 2:3], in_=mv.t[:, 1:2], func=AF.Ln, bias=epsl.t[:, 0:1])
            S.op("act", "activation", [mv.b], [mv.b], out=mv.t[:, 2:3], in_=mv.t[:, 2:3], func=AF.Exp, scale=-0.5)
            if pend is not None:
                ln_apply(*pend)
            pend = (v, mv, r0)
    if pend is not None:
        ln_apply(*pend)


BLK = 1024


def layer1(P, G):
    S = P.S
    X1, B_X1, banks, mod = G["X1"], G["B_X1"], G["banks"], G["mod"]
    o_w_in, o_conv_w, o_conv_b, o_gw, o_gb, o_lam, o_w_out = (G[k] for k in "o_w_in o_conv_w o_conv_b o_gw o_gb o_lam o_w_out".split())
    L = 1
    UX = P.dscr("UX", [8, 128, T], F32)
    SGL = P.dscr("SGL", [8, 128, SEQ], BF16)
    HG = P.dscr("HG", [8, 128, SEQ], BF16)
    B_UX = [Buf(f"UX{c}") for c in range(8)]
    B_SGL = [Buf(f"SGL{c}") for c in range(8)]
    B_HG = Buf("HG")

    with ExitStack() as ph:
        w1 = P.sb(ph, "w1", [128, 8, 2 * D], BF16)
        for j0 in range(0, 2 * D, 1024):
            S.dma("pool", w1.t[:, :, j0:j0 + 1024], o_w_in[:, j0:j0 + 1024].rearrange("(k p) n -> p k n", p=128), writes=[w1.b])
        xb_r = Ring([P.sb(ph, f"l1xb{i}", [128, 4, D], BF16) for i in range(2)])
        xm_r = Ring([P.sb(ph, f"l1xm{i}", [128, 8, 512], BF16) for i in range(2)])
        uf_r = Ring([P.sb(ph, f"l1uf{i}", [128, 512], F32) for i in range(4)])
        gb_r = Ring([P.sb(ph, f"l1gb{i}", [128, 512], BF16) for i in range(4)])
        pbr = Ring(banks[2:8])
        ts_r = Ring([(banks[bi].t[:].bitcast(BF16), banks[bi].b) for bi in range(2)])

        def loads(t):
            W = 512 if t < 16 else 256
            xb = xb_r.next()
            S.dma("pool", xb.t[:, 0:W // 128, :], X1[t * 512:t * 512 + W, :].rearrange("(s p) f -> p s f", p=128), reads=[B_X1], writes=[xb.b])
            return xb

        nxt = loads(0)
        for t in range(NT):
            W = 512 if t < 16 else 256
            t0 = t * 512
            xb = nxt
            if t + 1 < NT:
                nxt = loads(t + 1)
            xm = xm_r.next()
            load_xm_tile(P, G, L, X1, B_X1, t, xb, xm, ts_r, 1 if t == 16 else 0)
            for m in range(16 if t < 16 else 8):
                pb = pbr.next()
                for k in range(8):
                    S.op("pe", "matmul", [w1.b, xm.b], [pb.b], out=pb.t[:, 0:W], lhsT=w1.t[:, k, m * 128:(m + 1) * 128], rhs=xm.t[:, k, 0:W],
                         start=(k == 0), stop=(k == 7))
                if m < 8:
                    uf = uf_r.next()
                    S.op("dve", "tensor_copy", [pb.b], [uf.b], out=uf.t[:, 0:W], in_=pb.t[:, 0:W])
                    S.dma("sp", UX[m, :, t0:t0 + W], uf.t[:, 0:W], reads=[uf.b], writes=[B_UX[m]])
                else:
                    gb = gb_r.next()
                    S.op("act", "activation", [pb.b], [gb.b], out=gb.t[:, 0:W], in_=pb.t[:, 0:W], func=AF.Silu)
                    S.dma("sp", SGL[m - 8, :, t0:t0 + W], gb.t[:, 0:W], reads=[gb.b], writes=[B_SGL[m - 8]])
        S.barrier()
        S.emit()

    with ExitStack() as ph:
        gw = P.sb(ph, "gw", [128, 32, 128], BF16)
        gbias = P.sb(ph, "gbias", [128, 32], F32)
        lamc = P.sb(ph, "lamc", [128, 16], F32)
        sc8 = P.sb(ph, "sc8", [128, 16], F32)
        sc16 = P.sb(ph, "sc16", [128, 16], F32)
        cw = P.sb(ph, "cw", [128, 4, 8], F32)
        cb = P.sb(ph, "cb", [128, 8], F32)
        one1 = P.sb(ph, "one1", [128, 1], F32)
        S.op("dve", "memset", [], [one1.b], ap=one1.t[:], constant=1.0)
        S.dma("pool", gw.t[:], o_gw.rearrange("g d k i j -> i (g d k) j"), writes=[gw.b])
        S.dma("sp", gbias.t[:], o_gb.rearrange("g d k j -> j (g d k)"), writes=[gbias.b], allow_slow_non_contiguous=True)
        S.dma("sp", lamc.t[:], o_lam.rearrange("d (k p) -> p (d k)", p=128), writes=[lamc.b], allow_slow_non_contiguous=True)
        S.dma("sp", cw.t[:], o_conv_w.rearrange("t (k p) -> p t k", p=128), writes=[cw.b], allow_slow_non_contiguous=True)
        S.dma("sp", cb.t[:], o_conv_b.rearrange("(k p) -> p k", p=128), writes=[cb.b], allow_slow_non_contiguous=True)
        S.op("act", "activation", [lamc.b], [lamc.b], out=lamc.t[:], in_=lamc.t[:], func=AF.Exp, scale=-1.0)
        S.op("act", "activation", [lamc.b, one1.b], [lamc.b], out=lamc.t[:], in_=lamc.t[:], func=AF.Ln, bias=one1.t[:, 0:1])
        h8, h16 = sc8, sc16
        S.op("dve", "tensor_scalar", [lamc.b], [h8.b], out=h8.t[:], in0=lamc.t[:], scalar1=-4.0, scalar2=None, op0=ALU.mult)
        S.op("dve", "tensor_scalar", [lamc.b], [h16.b], out=h16.t[:], in0=lamc.t[:], scalar1=-8.0, scalar2=None, op0=ALU.mult)
        hbias = P.sb(ph, "hbias", [128, 32], F32)
        S.op("dve", "tensor_scalar", [gbias.b], [hbias.b], out=hbias.t[:], in0=gbias.t[:], scalar1=0.5, scalar2=None, op0=ALU.mult)

        uh = P.sb(ph, "uh", [128, SEQ + 4], F32)
        uxc = P.sb(ph, "uxc", [128, CTX + 4], F32)
        ua = P.sb(ph, "ua", [128, T], F32)
        ubf = P.sb(ph, "ubf", [128, T], BF16)
        hc = [P.sb(ph, f"hc{d}", [128, CTX], F32) for d in range(2)]
        r_r = Ring([P.sb(ph, f"r{i}", [128, BLK], F32) for i in range(2)])
        i_r = Ring([P.sb(ph, f"i{i}", [128, BLK], F32) for i in range(4)])
        a_r = Ring([P.sb(ph, f"a{i}", [128, BLK], F32) for i in range(4)])
        s_r = Ring([P.sb(ph, f"s{i}", [128, BLK], F32) for i in range(4)])
        q25 = P.sb(ph, "q25", [128, 1], F32)
        S.op("dve", "memset", [], [q25.b], ap=q25.t[:], constant=0.25)
        g_r = Ring([P.sb(ph, f"g{i}", [128, BLK], F32) for i in range(2)])
        hb_r = Ring([P.sb(ph, f"hb{i}", [128, BLK], F32) for i in range(2)])
        sgl_r = Ring([P.sb(ph, f"sgl{i}", [128, BLK], BF16) for i in range(2)])
        hg_r = Ring([P.sb(ph, f"hg{i}", [128, BLK], BF16) for i in range(2)])
        gp_r = Ring(banks[0:8])
        S.op("pool", "memset", [], [uh.b], ap=uh.t[:], constant=0.0)
        S.op("pool", "memset", [], [uxc.b], ap=uxc.t[:], constant=0.0)

        for c in range(8):
            if c > 0:
                S.op("pool", "memset", [], [uh.b], ap=uh.t[:, 0:2], constant=0.0)
                S.op("pool", "memset", [], [uh.b], ap=uh.t[:, SEQ + 2:SEQ + 4], constant=0.0)
            S.dma("sp", uh.t[:, 2:2 + SEQ], UX[c, :, 0:SEQ], reads=[B_UX[c]], writes=[uh.b])
            S.dma("sp", uxc.t[:, 2:2 + CTX], UX[c, :, SEQ:T], reads=[B_UX[c]], writes=[uxc.b])
            for (src, n, o0) in ((uh, SEQ, 0), (uxc, CTX, SEQ)):
                S.op("dve", "tensor_scalar", [src.b, cw.b, cb.b], [ua.b], out=ua.t[:, o0:o0 + n], in0=src.t[:, 0:n], scalar1=cw.t[:, 0, c:c + 1],
                     scalar2=cb.t[:, c:c + 1], op0=ALU.mult, op1=ALU.add)
                for k in range(1, 4):
                    S.op("dve", "scalar_tensor_tensor", [src.b, cw.b, ua.b], [ua.b], out=ua.t[:, o0:o0 + n], in0=src.t[:, k:k + n],
                         scalar=cw.t[:, k, c:c + 1], in1=ua.t[:, o0:o0 + n], op0=ALU.mult, op1=ALU.add)
            S.op("act", "activation", [ua.b], [ubf.b], out=ubf.t[:], in_=ua.t[:], func=AF.Identity)
            for d in range(2):
                ia, ix = (0 * 2 + d) * 8 + c, (1 * 2 + d) * 8 + c
                dk = d * 8 + c
                lat = [(b * BLK, BLK) for b in range(SEQ // BLK)]
                blocks = [(SEQ, CTX, True)] + [(t0, n, False) for (t0, n) in (lat if d == 0 else lat[::-1])]
                st1 = {}
                chain = {"prev": None}

                def stage1(k):
                    t0, n, is_ctx = blocks[k]
                    r, it, a, s_ = r_r.next(), i_r.next(), a_r.next(), s_r.next()
                    for q0 in range(0, n, 512):
                        w = min(512, n - q0)
                        for (gi, dst) in ((ia, r), (ix, it)):
                            pb = gp_r.next()
                            S.op("pe", "matmul", [gw.b, ubf.b], [pb.b], out=pb.t[:, 0:w], lhsT=gw.t[:, gi, :], rhs=ubf.t[:, t0 + q0:t0 + q0 + w],
                                 start=True, stop=True)
                            S.op("act", "activation", [pb.b, hbias.b], [dst.b], out=dst.t[:, q0:q0 + w], in_=pb.t[:, 0:w], func=AF.Tanh,
                                 scale=0.5, bias=hbias.t[:, gi:gi + 1])
                    S.op("act", "activation", [r.b, h8.b], [a.b], out=a.t[:, 0:n], in_=r.t[:, 0:n], func=AF.Exp, scale=h8.t[:, dk:dk + 1], bias=h8.t[:, dk:dk + 1])
                    S.op("act", "activation", [r.b, h16.b], [s_.b], out=s_.t[:, 0:n], in_=r.t[:, 0:n], func=AF.Exp, scale=h16.t[:, dk:dk + 1], bias=h16.t[:, dk:dk + 1])
                    S.op("pool", "tensor_scalar", [s_.b], [s_.b], out=s_.t[:, 0:n], in0=s_.t[:, 0:n], scalar1=1.0, scalar2=0.0, op0=ALU.min, op1=ALU.max)
                    st1[k] = (it, a, s_)

                def stage2(k):
                    t0, n, is_ctx = blocks[k]
                    it, a, s_ = st1.pop(k)
                    g = g_r.next()
                    S.op("act", "activation", [s_.b, q25.b], [s_.b], out=s_.t[:, 0:n], in_=s_.t[:, 0:n], func=AF.Sqrt, scale=-0.25, bias=q25.t[:, 0:1])
                    S.op("dve", "scalar_tensor_tensor", [it.b, ua.b], [g.b], out=g.t[:, 0:n], in0=it.t[:, 0:n], scalar=1.0, in1=ua.t[:, t0:t0 + n],
                         op0=ALU.add, op1=ALU.mult)
                    S.op("dve", "tensor_tensor", [g.b, s_.b], [g.b], out=g.t[:, 0:n], in0=g.t[:, 0:n], in1=s_.t[:, 0:n], op=ALU.mult)
                    if is_ctx:
                        dst, dap, dbuf = hc[d], hc[d].t[:, 0:n], hc[d].b
                    elif d == 0:
                        dst, dap, dbuf = uh, uh.t[:, 2 + t0:2 + t0 + n], uh.b
                    else:
                        dst = hb_r.next()
                        dap, dbuf = dst.t[:, 0:n], dst.b
                    prev_init = chain["prev"]
                    init = 0.0 if prev_init is None else prev_init[0]
                    rd = [a.b, g.b] + ([] if prev_init is None else [prev_init[1]])
                    if d == 0:
                        S.op("dve", "tensor_tensor_scan", rd, [dbuf], out=dap, data0=a.t[:, 0:n], data1=g.t[:, 0:n], initial=init, op0=ALU.mult, op1=ALU.add)
                        chain["prev"] = (dap[:, n - 1:n], dbuf)
                    else:
                        S.op("dve", "tensor_tensor_scan", rd, [dbuf], out=dap[:, ::-1], data0=a.t[:, 0:n][:, ::-1], data1=g.t[:, 0:n][:, ::-1],
                             initial=init, op0=ALU.mult, op1=ALU.add)
                        chain["prev"] = (dap[:, 0:1], dbuf)
                    if d == 1 and not is_ctx:
                        sgl = sgl_r.next()
                        S.dma("sp", sgl.t[:, 0:n], SGL[c, :, t0:t0 + n], reads=[B_SGL[c]], writes=[sgl.b])
                        S.op("dve", "tensor_tensor", [dst.b, uh.b], [g.b], out=g.t[:, 0:n], in0=dst.t[:, 0:n], in1=uh.t[:, 2 + t0:2 + t0 + n], op=ALU.add)
                        hg = hg_r.next()
                        S.op("pool", "tensor_tensor", [g.b, sgl.b], [hg.b], out=hg.t[:, 0:n], in0=g.t[:, 0:n], in1=sgl.t[:, 0:n], op=ALU.mult)
                        S.dma("pool", HG[c, :, t0:t0 + n], hg.t[:, 0:n], reads=[hg.b], writes=[B_HG])

                for p0 in range(0, len(blocks), 2):
                    ks = [k for k in (p0, p0 + 1) if k < len(blocks)]
                    for k in ks:
                        stage1(k)
                    for k in ks:
                        stage2(k)
        S.barrier()
        S.emit()

    with ExitStack() as ph:
        wo = P.sb(ph, "wo1", [128, 8, D], BF16)
        S.dma("pool", wo.t[:], o_w_out.rearrange("(k p) n -> p k n", p=128), writes=[wo.b])
        qtiles = [(i * 512, 512, 0) for i in range(16)]
        out_proj_ln(P, G, L, qtiles, None, None, HG, wo, None, B_HG, X1, B_X1, G["out"], G["B_out"], ph)
        S.barrier()
        S.emit()


def _rope_tabs(pos, rot_dim):
    rows = (pos // 64).astype(np.float32)
    cols = (pos % 64).astype(np.float32)
    n_freq = rot_dim // 4
    freqs = (np.float32(10000.0) ** (-np.arange(n_freq, dtype=np.float32) / np.float32(n_freq))).astype(np.float32)
    ang = np.concatenate([rows[:, None] * freqs, cols[:, None] * freqs], -1).astype(np.float32)
    return np.cos(ang).T.astype(np.float32), np.sin(ang).T.astype(np.float32)


def _host_layout(inputs):
    f = lambda k: np.asarray(inputs[k], np.float32)
    x, ctx, c, c_ctx = f("x"), f("ctx"), f("c"), f("c_ctx")
    w_in, w_uq, w_ukv = f("e_w_in")[0], f("e_w_uq")[0], f("e_w_ukv")[0]
    kr = w_in[:, 640:672]
    junk = w_in[:, 384:448]
    blk_kr = np.concatenate([junk, kr], 1)
    blk_krr = np.concatenate([junk, kr[:, 16:32], kr[:, 0:16]], 1)

    def rot64(w):
        w4 = w.reshape(w.shape[0], -1, 2, 32)
        return np.concatenate([w4[:, :, 1, :], w4[:, :, 0, :]], -1).reshape(w.shape[0], -1)

    wA = np.ascontiguousarray(np.concatenate([w_in, blk_kr, blk_krr, rot64(w_in[:, 672:1184]), rot64(w_in[:, 1184:1696])], 1))
    uq3 = w_uq.reshape(384, 8, 96)
    uq_rot = np.concatenate([uq3[:, :, 0:64], uq3[:, :, 80:96], uq3[:, :, 64:80]], -1).reshape(384, 768)
    w_uq2 = np.ascontiguousarray(np.concatenate([w_uq, uq_rot], 1))
    kv3 = w_ukv.reshape(256, 8, 128)
    w_ukv2 = np.ascontiguousarray(np.concatenate([kv3[:, :, 0:64].reshape(256, 512), kv3[:, :, 64:128].reshape(256, 512)], 1))
    lamv = np.stack([f(k)[0] for k in ("e_lam_q1", "e_lam_k1", "e_lam_q2", "e_lam_k2")])
    pos = np.arange(SEQ)
    cA, sA = _rope_tabs(pos, 32)
    cB, sB = _rope_tabs(pos, 64)

    def padctx(cs, sn):
        return (np.concatenate([cs, np.ones((cs.shape[0], CTX), np.float32)], 1),
                np.concatenate([sn, np.zeros((sn.shape[0], CTX), np.float32)], 1))

    cA, sA = padctx(cA, sA)
    cB, sB = padctx(cB, sB)
    shared = dict(
        ada_w=f("ada_w"), ada_b=f("ada_b"), ln_g=f("post_ln_g"), ln_b=f("post_ln_b"), identd=np.eye(128, dtype=np.float32),
        wA=wA, w_uq=w_uq2, w_ukv=w_ukv2, qg=f("e_q_norm_g")[0], kvg=f("e_kv_norm_g")[0], lamv=np.ascontiguousarray(lamv),
        subg=f("e_subln_g")[0], e_w_out=f("e_w_out")[0],
        tabA=np.ascontiguousarray(np.stack([np.concatenate([cA, cA], 0), np.concatenate([-sA, sA], 0)])),
        tabB=np.ascontiguousarray(np.stack([np.concatenate([cB, cB], 0), np.concatenate([-sB, sB], 0)])),
        o_w_in=f("o_w_in")[0], o_conv_w=f("o_conv_w")[0], o_conv_b=f("o_conv_b")[0],
        o_gw=np.ascontiguousarray(np.stack([f("o_gate_a_w")[0], f("o_gate_x_w")[0]])),
        o_gb=np.ascontiguousarray(np.stack([f("o_gate_a_b")[0], f("o_gate_x_b")[0]])),
        o_lam=f("o_lru_lambda")[0], o_w_out=f("o_w_out")[0],
    )
    in_maps = []
    for b in range(4):
        m = dict(shared)
        m.update(xk=np.ascontiguousarray(np.concatenate([x[b], ctx[b]], 0)), cvec=np.ascontiguousarray(np.stack([c[b], c_ctx])))
        in_maps.append(m)
    return in_maps


_NC_CACHE = {}
CORE_IDS = [0, 2, 4, 6]


def kernel(**inputs):
    in_maps = _host_layout(inputs)
    if "nc" not in _NC_CACHE:
        _NC_CACHE["nc"] = build_program()
    nc = _NC_CACHE["nc"]
    res = run_bass_kernel_spmd(nc, in_maps, core_ids=CORE_IDS)
    return np.stack([res.results[b]["out"] for b in range(4)], 0).astype(np.float32)
```

```python
import math
import numpy as np
from contextlib import ExitStack
import concourse.bass as bass
import concourse.mybir as mybir
from concourse.bass_utils import run_bass_kernel_spmd

F32 = mybir.dt.float32
BF16 = mybir.dt.bfloat16
AF = mybir.ActivationFunctionType
ALU = mybir.AluOpType
AX = mybir.AxisListType

D = 1024
SEQ = 8192
NOWN = 4096
CTX = 256
T = SEQ + CTX
NQ = NOWN + CTX
NKC = T // 128
ALPHA = 4 ** 0.25
LN_EPS = 1e-6
RMS_EPS = 1e-6
MLA_SCALE = 96 ** -0.5
DIFF_SCALE = 64 ** -0.5
LAMBDA_INIT0 = 0.8 - 0.6 * math.exp(-0.3 * 0)
NWA = 3232 + 96 + 96 + 512 + 512


class Buf:
    def __init__(self, name):
        self.name = name
        self.w = None
        self.r = []
        self.wsem = None
        self.rsem = None


class Sched:
    ENG = ("pe", "act", "dve", "pool", "sp")
    ENGATTR = {"pe": "tensor", "act": "scalar", "dve": "vector", "pool": "gpsimd", "sp": "sync"}
    CE = ("pe", "act", "dve", "pool")

    def __init__(self, nc, stack):
        self.nc = nc
        self.stack = stack
        self.ops = {e: [] for e in self.ENG}
        self.esem = {}
        self.ecount = {}
        self.known = {e: {} for e in self.ENG}
        self.pending = {e: [] for e in self.ENG}
        self.nsem = 0
        self.dsems = []
        self.final_tokens = []
        self.free_dsems = {"sp": [], "pool": [], "act": []}
        self.sem_owners = []
        self._fresh_engine_sems()

    def _sem(self, name):
        self.nsem += 1
        return self.stack.enter_context(self.nc.semaphore(f"q{self.nsem}_{name}"))

    def _fresh_engine_sems(self):
        for e in self.CE:
            self.esem[e] = self._sem("e_" + e)
            self.ecount[e] = 0

    def new_dsem(self, name, q):
        if self.free_dsems[q]:
            return self.free_dsems[q].pop()
        ent = [self._sem(name), 0, q]
        self.dsems.append(ent)
        return ent

    def barrier(self):
        toks = [(self.esem[e], self.ecount[e], "x") for e in self.CE if self.ecount[e] > 0]
        toks += [(ent[0], ent[1], "dma") for ent in self.dsems if ent[1] > 0]
        for e in self.ENG:
            self.pending[e] = self.pending[e] + list(toks)
        for e in self.CE:
            if self.ecount[e] > 20000:
                self.esem[e] = self._sem("e_" + e)
                self.ecount[e] = 0
        for (b, attr) in self.sem_owners:
            ent = getattr(b, attr)
            if ent is not None:
                self.free_dsems[ent[2]].append(ent)
                setattr(b, attr, None)
        self.sem_owners = []

    def _waits(self, eng, reads, writes):
        toks = list(self.pending[eng])
        self.pending[eng] = []
        for b in reads:
            if b.w is not None:
                toks.append(b.w)
        for b in writes:
            if b.w is not None:
                toks.append(b.w)
            toks.extend(b.r)
        need = {}
        for (sem, val, src) in toks:
            if src == "pe" and eng == "pe":
                continue
            k = id(sem)
            if self.known[eng].get(k, 0) >= val:
                continue
            if k not in need or need[k][1] < val:
                need[k] = (sem, val)
        for k, (sem, val) in need.items():
            self.known[eng][k] = val
        return list(need.values())

    def _mark(self, tok, reads, writes):
        for b in writes:
            b.w = tok
            b.r = []
        for b in reads:
            if b not in writes:
                b.r.append(tok)
                if len(b.r) > 16:
                    best = {}
                    for t in b.r:
                        k = id(t[0])
                        if k not in best or best[k][1] < t[1]:
                            best[k] = t
                    b.r = list(best.values())

    def op(self, eng, name, reads, writes, **kw):
        waits = self._waits(eng, reads, writes)
        self.ecount[eng] += 1
        tok = (self.esem[eng], self.ecount[eng], eng)
        self.ops[eng].append((waits, name, kw, (self.esem[eng], 1)))
        self._mark(tok, reads, writes)
        return tok

    def dma(self, q, out, in_, reads=(), writes=(), final=False, **kw):
        waits = self._waits(q, reads, writes)
        tgt = writes[0] if writes else reads[0]
        attr = ("wsem_" if writes else "rsem_") + q
        cur = getattr(tgt, attr, None)
        if cur is None:
            cur = self.new_dsem(tgt.name, q)
            setattr(tgt, attr, cur)
            self.sem_owners.append((tgt, attr))
        cur[1] += 16
        tok = (cur[0], cur[1], "dma")
        kw = dict(kw)
        kw.update(out=out, in_=in_)
        self.ops[q].append((waits, "dma_start", kw, (cur[0], 16)))
        self._mark(tok, reads, writes)
        if final:
            self.final_tokens.append(tok)
        return tok

    def emit(self, last=False):
        nc = self.nc
        fin = {}
        if last:
            toks = list(self.final_tokens) + [(ent[0], ent[1], "dma") for ent in self.dsems if ent[1] > 0]
            for (sem, val, _) in toks:
                k = id(sem)
                if k not in fin or fin[k][1] < val:
                    fin[k] = (sem, val)
        with nc.Block() as block:
            for e in self.ENG:
                ops = self.ops[e]
                is_last = last and e == "sp"
                if not ops and not is_last:
                    continue

                def body(engobj, ops=ops, is_last=is_last):
                    for (waits, name, kw, inc) in ops:
                        for (sem, val) in waits:
                            engobj.wait_ge(sem, val)
                        getattr(engobj, name)(**kw).then_inc(inc[0], inc[1])
                    if is_last:
                        for (sem, val) in fin.values():
                            engobj.wait_ge(sem, val)

                getattr(block, self.ENGATTR[e])(body)
        self.ops = {e: [] for e in self.ENG}


class Tl:
    def __init__(self, t, name):
        self.t = t
        self.b = Buf(name)


class Ring:
    def __init__(self, items):
        self.items = items
        self.i = 0

    def next(self):
        it = self.items[self.i % len(self.items)]
        self.i += 1
        return it


class Prog:
    def __init__(self, nc, stack):
        self.nc = nc
        self.S = Sched(nc, stack)
        self.debug = False
        self.stop_after = 99

    def din(self, name, shape, dt=F32):
        return self.nc.dram_tensor(name, list(shape), dt, kind="ExternalInput").ap()

    def dout(self, name, shape, dt=F32):
        return self.nc.dram_tensor(name, list(shape), dt, kind="ExternalOutput").ap()

    def dscr(self, name, shape, dt):
        kind = "ExternalOutput" if self.debug else "Internal"
        return self.nc.dram_tensor(name, list(shape), dt, kind=kind).ap()

    def sb(self, st, name, shape, dt):
        self.uid = getattr(self, "uid", 0) + 1
        name = f"{name}_{self.uid}"
        return Tl(st.enter_context(self.nc.sbuf_tensor(name, list(shape), dt)), name)

    def ps(self, st, name):
        return Tl(st.enter_context(self.nc.psum_tensor(name, [128, 512], F32)), name)


O_CQ, O_CKV, O_DQ, O_DK, O_DV, O_GATE = 0, 384, 672, 1184, 1696, 2208
O_KR, O_KRR, O_DQR, O_DKR = 3232, 3328, 3424, 3936
NT = 17


def build_program(debug=False, stop_after=99):
    nc = bass.Bass("TRN2", target_bir_lowering=False)
    with ExitStack() as top:
        P = Prog(nc, top)
        P.debug = debug
        P.stop_after = stop_after
        S = P.S
        G = {}
        G["xk"] = P.din("xk", [T, D])
        cvec = P.din("cvec", [2, D])
        G["tabA"] = P.din("tabA", [2, 32, T])
        G["tabB"] = P.din("tabB", [2, 64, T])
        ada_w = P.din("ada_w", [2, D, 3 * D])
        ada_b = P.din("ada_b", [2, 3 * D])
        ln_g = P.din("ln_g", [2, D])
        ln_b = P.din("ln_b", [2, D])
        identd = P.din("identd", [128, 128])
        G["wA"] = P.din("wA", [D, NWA])
        G["w_uq"] = P.din("w_uq", [384, 1536])
        G["w_ukv"] = P.din("w_ukv", [256, 1024])
        G["qg"] = P.din("qg", [384])
        G["kvg"] = P.din("kvg", [256])
        G["lamv"] = P.din("lamv", [4, 64])
        G["subg"] = P.din("subg", [128])
        G["e_w_out"] = P.din("e_w_out", [D, D])
        G["o_w_in"] = P.din("o_w_in", [D, 2 * D])
        G["o_conv_w"] = P.din("o_conv_w", [4, D])
        G["o_conv_b"] = P.din("o_conv_b", [D])
        G["o_gw"] = P.din("o_gw", [2, 2, 8, 128, 128])
        G["o_gb"] = P.din("o_gb", [2, 2, 8, 128])
        G["o_lam"] = P.din("o_lam", [2, D])
        G["o_w_out"] = P.din("o_w_out", [D, D])
        G["out"] = P.dout("out", [SEQ, D])
        G["B_out"] = Buf("out")
        G["X1"] = P.dscr("X1", [T, D], F32)
        G["B_X1"] = Buf("X1")

        ident = P.sb(top, "ident", [128, 128], BF16)
        ones = P.sb(top, "ones", [128, 128], BF16)
        identf = P.sb(top, "identf", [128, 128], F32)
        banks = [P.ps(top, f"pb{i}") for i in range(8)]
        mod = P.sb(top, "mod", [128, 2, 24, 2], F32)
        gbc = [[P.sb(top, f"gbc{l}{c}", [128, D], F32) for c in range(2 - l)] for l in range(2)]
        lng = [P.sb(top, f"lng{l}", [128, D], F32) for l in range(2)]
        lnb = [P.sb(top, f"lnb{l}", [128, D], F32) for l in range(2)]
        G.update(ident=ident, ones=ones, banks=banks, mod=mod, gbc=gbc, lng=lng, lnb=lnb)

        S.dma("sp", identf.t[:], identd[:, :], writes=[identf.b])
        S.op("dve", "tensor_copy", [identf.b], [ident.b], out=ident.t[:], in_=identf.t[:])
        S.op("pool", "memset", [], [ones.b], ap=ones.t[:], constant=1.0)

        wst = ExitStack()
        G["wa"] = P.sb(wst, "wa", [128, 8, NWA], BF16)
        G["wuq"] = P.sb(wst, "wuq", [128, 3, 1536], BF16)
        G["wukv"] = P.sb(wst, "wukv", [128, 2, 1024], BF16)
        G["wst"] = wst

        with ExitStack() as ph:
            adw = P.sb(ph, "adw", [128, 8, 3 * D], BF16)
            adb = P.sb(ph, "adb", [128, 24], F32)
            cT = P.sb(ph, "cT", [128, 2, 8], F32)
            sc = P.sb(ph, "sc", [128, 8, 2], BF16)
            scb = [P.sb(ph, f"scb{c}", [128, 8, 128], BF16) for c in range(2)]
            gb_b = P.sb(ph, "gb_b", [128, D], F32)
            S.dma("sp", cT.t[:], cvec.rearrange("c (k p) -> p c k", p=128), writes=[cT.b], allow_slow_non_contiguous=True)
            S.op("act", "activation", [cT.b], [sc.b], out=sc.t[:].rearrange("p k c -> p c k"), in_=cT.t[:], func=AF.Silu)
            for c in range(2):
                S.op("dve", "tensor_copy", [sc.b], [scb[c].b], out=scb[c].t[:], in_=sc.t[:, :, c:c + 1].to_broadcast([128, 8, 128]))
            for l in range(2):
                for j in range(3):
                    S.dma("pool", adw.t[:, :, j * D:(j + 1) * D], ada_w[l, :, j * D:(j + 1) * D].rearrange("(k p) n -> p k n", p=128),
                          writes=[adw.b])
                S.dma("sp", adb.t[:], ada_b[l].rearrange("(j p) -> p j", p=128), writes=[adb.b], allow_slow_non_contiguous=True)
                S.dma("sp", gb_b.t[:], ada_b[l:l + 1, 2 * D:3 * D].broadcast_to([128, D]), writes=[gb_b.b])
                S.dma("sp", lng[l].t[:], ln_g[l:l + 1, :].broadcast_to([128, D]), writes=[lng[l].b])
                S.dma("sp", lnb[l].t[:], ln_b[l:l + 1, :].broadcast_to([128, D]), writes=[lnb[l].b])
                pb = banks[l]
                for j in range(24):
                    for k in range(8):
                        S.op("pe", "matmul", [adw.b, sc.b], [pb.b], out=pb.t[:, j * 2:j * 2 + 2], lhsT=adw.t[:, k, j * 128:(j + 1) * 128],
                             rhs=sc.t[:, k, :], start=(k == 0), stop=(k == 7))
                S.op("dve", "tensor_tensor", [pb.b, adb.b], [mod.b], out=mod.t[:, l, :, :], in0=pb.t[:, 0:48].rearrange("p (j c) -> p j c", c=2),
                     in1=adb.t[:].unsqueeze(2).to_broadcast([128, 24, 2]), op=ALU.add)
                S.op("dve", "tensor_scalar", [mod.b], [mod.b], out=mod.t[:, l, 8:16, :], in0=mod.t[:, l, 8:16, :], scalar1=1.0, scalar2=None, op0=ALU.add)
                for c in range(2 - l):
                    for hh in range(2):
                        pg = banks[2 + c * 2 + hh]
                        for k in range(8):
                            S.op("pe", "matmul", [adw.b, scb[c].b], [pg.b], out=pg.t[:, :], lhsT=scb[c].t[:, k, :],
                                 rhs=adw.t[:, k, 2 * D + hh * 512:2 * D + (hh + 1) * 512], start=(k == 0), stop=(k == 7))
                        S.op("dve", "tensor_tensor", [pg.b, gb_b.b], [gbc[l][c].b], out=gbc[l][c].t[:, hh * 512:(hh + 1) * 512], in0=pg.t[:, :],
                             in1=gb_b.t[:, hh * 512:(hh + 1) * 512], op=ALU.add)
            wa_, wuq_, wukv_ = G["wa"], G["wuq"], G["wukv"]
            for j0 in range(0, NWA, 1112):
                S.dma("pool", wa_.t[:, :, j0:j0 + 1112], G["wA"][:, j0:j0 + 1112].rearrange("(k p) n -> p k n", p=128), writes=[wa_.b])
            S.dma("pool", wuq_.t[:], G["w_uq"].rearrange("(k p) n -> p k n", p=128), writes=[wuq_.b])
            S.dma("pool", wukv_.t[:], G["w_ukv"].rearrange("(k p) n -> p k n", p=128), writes=[wukv_.b])
            S.barrier()
            S.emit()

        if stop_after >= 1:
            layer0(P, G)
        if stop_after >= 4:
            layer1(P, G)
        S.emit(last=True)
    return nc


def load_xm_tile(P, G, L, src, B_src, t, xb, xm, ts_r, cond):
    S = P.S
    ident, mod = G["ident"], G["mod"]
    W = 512 if t < 16 else 256
    ns = W // 128
    for fp in range(4):
        tsv, tsb = ts_r.next()
        for hf in range(2):
            fc = fp * 2 + hf
            for s in range(ns):
                S.op("pe", "transpose", [xb.b, ident.b], [tsb], out=tsv[:, hf * 512 + s * 128:hf * 512 + (s + 1) * 128],
                     in_=xb.t[:, s, fc * 128:(fc + 1) * 128], identity=ident.t[:])
        fc = fp * 2
        S.op("act", "activation", [tsb, mod.b], [xm.b], out=xm.t[:, fc, 0:W], in_=tsv[:, 0:W], func=AF.Identity,
             bias=mod.t[:, L, fc, cond:cond + 1], scale=mod.t[:, L, 8 + fc, cond:cond + 1])
        fc = fp * 2 + 1
        S.op("dve", "tensor_scalar", [tsb, mod.b], [xm.b], out=xm.t[:, fc, 0:W], in0=tsv[:, 512:512 + W],
             scalar1=mod.t[:, L, 8 + fc, cond:cond + 1], scalar2=mod.t[:, L, fc, cond:cond + 1], op0=ALU.mult, op1=ALU.add)


def layer0(P, G):
    S = P.S
    xk, tabA, tabB, wA, w_uq, w_ukv, qg, kvg, lamv, subg, e_w_out = (G[k] for k in
        "xk tabA tabB wA w_uq w_ukv qg kvg lamv subg e_w_out".split())
    ident, ones, banks, mod = (G[k] for k in "ident ones banks mod".split())
    L = 0
    KTm = P.dscr("KTm", [8, 96, T], BF16)
    Vm = P.dscr("Vm", [8, 128, NKC, 64], BF16)
    QTm = P.dscr("QTm", [8, 96, T], BF16)
    KTd = P.dscr("KTd", [4, 128, T], BF16)
    Vd = P.dscr("Vd", [4, 2, 128, NKC, 64], BF16)
    QTd = P.dscr("QTd", [4, 128, T], BF16)
    SG = P.dscr("SG", [8, 128, T], BF16)
    B_KTm = [Buf(f"KTm{h}") for h in range(8)]
    B_Vm = Buf("Vm")
    B_QTm = [Buf(f"QTm{h}") for h in range(8)]
    B_KTd = [Buf(f"KTd{h}") for h in range(4)]
    B_Vd = Buf("Vd")
    B_QTd = [Buf(f"QTd{h}") for h in range(4)]
    B_SG = Buf("SG")

    with ExitStack() as ph:
        wa, wuq, wukv = G["wa"], G["wuq"], G["wukv"]
        qgc = P.sb(ph, "qgc", [128, 3], F32)
        kvgc = P.sb(ph, "kvgc", [128, 2], F32)
        epsc = P.sb(ph, "epsc", [128, 1], F32)
        S.op("dve", "memset", [], [epsc.b], ap=epsc.t[:], constant=RMS_EPS)
        S.dma("sp", qgc.t[:], qg.rearrange("(k p) -> p k", p=128), writes=[qgc.b], allow_slow_non_contiguous=True)
        S.dma("sp", kvgc.t[:], kvg.rearrange("(k p) -> p k", p=128), writes=[kvgc.b], allow_slow_non_contiguous=True)

        xb_r = Ring([P.sb(ph, f"xb{i}", [128, 4, D], BF16) for i in range(2)])
        xm_r = Ring([P.sb(ph, f"xm{i}", [128, 8, 512], BF16) for i in range(2)])
        tA_r = Ring([P.sb(ph, f"tA{i}", [128, 2, 512], F32) for i in range(2)])
        tB_r = Ring([P.sb(ph, f"tB{i}", [128, 2, 512], F32) for i in range(2)])
        ckvn_r = Ring([P.sb(ph, f"ckvn{i}", [128, 2, 512], BF16) for i in range(2)])
        cqn_r = Ring([P.sb(ph, f"cqn{i}", [128, 3, 512], BF16) for i in range(2)])
        sq_r = Ring([P.sb(ph, f"sq{i}", [128, 512], BF16) for i in range(3)])
        rs_r = Ring([P.sb(ph, f"rs{i}", [128, 512], F32) for i in range(2)])
        f1_r = Ring([P.sb(ph, f"f1_{i}", [128, 512], F32) for i in range(3)])
        f2_r = Ring([P.sb(ph, f"f2_{i}", [128, 512], F32) for i in range(3)])
        ob_r = Ring([P.sb(ph, f"ob{i}", [128, 512], BF16) for i in range(6)])
        pbr = Ring(banks[2:8])
        ts_r = Ring([(banks[bi].t[:].bitcast(BF16), banks[bi].b) for bi in range(2)])

        def mm_group(pb, M, W, lhs, rhs, rd):
            nk = len(lhs)
            for k in range(nk):
                S.op("pe", "matmul", rd, [pb.b], out=pb.t[0:M, 0:W], lhsT=lhs[k], rhs=rhs[k], start=(k == 0), stop=(k == nk - 1))

        def rstd_bc(src_tiles, W, nfeat):
            sqs = []
            for pbs in src_tiles:
                sq = sq_r.next()
                S.op("act", "activation", [pbs.b], [sq.b], out=sq.t[:, 0:W], in_=pbs.t[:, 0:W], func=AF.Square)
                sqs.append(sq)
            pss = pbr.next()
            for i, sq in enumerate(sqs):
                S.op("pe", "matmul", [ones.b, sq.b], [pss.b], out=pss.t[:, 0:W], lhsT=ones.t[:, :], rhs=sq.t[:, 0:W],
                     start=(i == 0), stop=(i == len(sqs) - 1))
            rs = rs_r.next()
            S.op("act", "activation", [pss.b, epsc.b], [rs.b], out=rs.t[:, 0:W], in_=pss.t[:, 0:W], func=AF.Ln, scale=1.0 / nfeat, bias=epsc.t[:, 0:1])
            S.op("act", "activation", [rs.b], [rs.b], out=rs.t[:, 0:W], in_=rs.t[:, 0:W], func=AF.Exp, scale=-0.5)
            return rs

        def rope(pm, pr, tab, r0, r1, W, outap, outb):
            f1 = f1_r.next()
            f2 = f2_r.next()
            S.op("dve", "tensor_tensor", [pm.b, tab.b], [f1.b], out=f1.t[r0:r1, 0:W], in0=pm.t[r0:r1, 0:W], in1=tab.t[r0:r1, 0, 0:W], op=ALU.mult)
            S.op("dve", "tensor_tensor", [pr.b, tab.b], [f2.b], out=f2.t[r0:r1, 0:W], in0=pr.t[r0:r1, 0:W], in1=tab.t[r0:r1, 1, 0:W], op=ALU.mult)
            S.op("pool", "tensor_tensor", [f1.b, f2.b], [outb], out=outap, in0=f1.t[r0:r1, 0:W], in1=f2.t[r0:r1, 0:W], op=ALU.add)

        def loads(t):
            W = 512 if t < 16 else 256
            ns = W // 128
            t0 = t * 512
            xb = xb_r.next()
            S.dma("pool", xb.t[:, 0:ns, :], xk[t0:t0 + W, :].rearrange("(s p) f -> p s f", p=128), writes=[xb.b])
            tA = tA_r.next()
            tB = tB_r.next()
            S.dma("sp", tA.t[64:96, :, 0:W], tabA[:, :, t0:t0 + W].rearrange("c p w -> p c w"), writes=[tA.b])
            for hh in range(2):
                S.dma("sp", tB.t[hh * 64:(hh + 1) * 64, :, 0:W], tabB[:, :, t0:t0 + W].rearrange("c p w -> p c w"), writes=[tB.b])
            return xb, tA, tB

        nxt = loads(0)
        for t in range(NT):
            W = 512 if t < 16 else 256
            ns = W // 128
            t0 = t * 512
            cond = 1 if t == 16 else 0
            xb, tA, tB = nxt
            if t + 1 < NT:
                nxt = loads(t + 1)
            xm = xm_r.next()
            load_xm_tile(P, G, L, xk, None, t, xb, xm, ts_r, cond)

            def proj(pb, M, col0):
                mm_group(pb, M, W, [wa.t[:, k, col0:col0 + M] for k in range(8)], [xm.t[:, k, 0:W] for k in range(8)], [wa.b, xm.b])

            pc = [pbr.next() for _ in range(2)]
            for i in range(2):
                proj(pc[i], 128, O_CKV + i * 128)
            rs = rstd_bc(pc, W, 256)
            ckvn = ckvn_r.next()
            for i in range(2):
                S.op("dve", "scalar_tensor_tensor", [pc[i].b, kvgc.b, rs.b], [ckvn.b], out=ckvn.t[:, i, 0:W], in0=pc[i].t[:, 0:W],
                     scalar=kvgc.t[:, i:i + 1], in1=rs.t[:, 0:W], op0=ALU.mult, op1=ALU.mult)
            for j in range(4):
                pb = pbr.next()
                mm_group(pb, 128, W, [wukv.t[:, k, j * 128:(j + 1) * 128] for k in range(2)], [ckvn.t[:, k, 0:W] for k in range(2)], [wukv.b, ckvn.b])
                ob = ob_r.next()
                S.op("act", "activation", [pb.b], [ob.b], out=ob.t[:, 0:W], in_=pb.t[:, 0:W], func=AF.Identity)
                for hh in range(2):
                    S.dma("sp", KTm[2 * j + hh, 0:64, t0:t0 + W], ob.t[hh * 64:(hh + 1) * 64, 0:W], reads=[ob.b], writes=[B_KTm[2 * j + hh]])
            for s in range(ns):
                pb = pbr.next()
                mm_group(pb, 128, 512, [ckvn.t[:, k, s * 128:(s + 1) * 128] for k in range(2)], [wukv.t[:, k, 512:1024] for k in range(2)], [wukv.b, ckvn.b])
                ob = ob_r.next()
                S.op("dve", "tensor_copy", [pb.b], [ob.b], out=ob.t[:, :], in_=pb.t[:, :])
                S.dma("sp", Vm[:, :, t0 // 128 + s, :].rearrange("h p d -> p h d"), ob.t[:, :].rearrange("p (h d) -> p h d", h=8),
                      reads=[ob.b], writes=[B_Vm])
            pm = pbr.next()
            pr = pbr.next()
            proj(pm, 96, O_KR)
            proj(pr, 96, O_KRR)
            ob = ob_r.next()
            rope(pm, pr, tA, 64, 96, W, ob.t[64:96, 0:W], ob.b)
            for h in range(8):
                S.dma("sp", KTm[h, 64:96, t0:t0 + W], ob.t[64:96, 0:W], reads=[ob.b], writes=[B_KTm[h]])
            for h in range(4):
                pm = pbr.next()
                pr = pbr.next()
                proj(pm, 128, O_DK + h * 128)
                proj(pr, 128, O_DKR + h * 128)
                ob = ob_r.next()
                rope(pm, pr, tB, 0, 128, W, ob.t[:, 0:W], ob.b)
                S.dma("sp", KTd[h, :, t0:t0 + W], ob.t[:, 0:W], reads=[ob.b], writes=[B_KTd[h]])
            for s in range(ns):
                pb = pbr.next()
                mm_group(pb, 128, 512, [xm.t[:, k, s * 128:(s + 1) * 128] for k in range(8)], [wa.t[:, k, O_DV:O_DV + 512] for k in range(8)], [wa.b, xm.b])
                ob = ob_r.next()
                S.op("act", "activation", [pb.b], [ob.b], out=ob.t[:, :], in_=pb.t[:, :], func=AF.Identity)
                S.dma("sp", Vd[:, :, :, t0 // 128 + s, :].rearrange("h j p d -> p (h j) d"), ob.t[:, :].rearrange("p (g d) -> p g d", g=8),
                      reads=[ob.b], writes=[B_Vd])
            pq = [pbr.next() for _ in range(3)]
            for i in range(3):
                proj(pq[i], 128, O_CQ + i * 128)
            rs = rstd_bc(pq, W, 384)
            cqn = cqn_r.next()
            for i in range(3):
                S.op("dve", "scalar_tensor_tensor", [pq[i].b, qgc.b, rs.b], [cqn.b], out=cqn.t[:, i, 0:W], in0=pq[i].t[:, 0:W],
                     scalar=qgc.t[:, i:i + 1], in1=rs.t[:, 0:W], op0=ALU.mult, op1=ALU.mult)
            for h in range(8):
                pm = pbr.next()
                pr = pbr.next()
                mm_group(pm, 96, W, [wuq.t[:, k, h * 96:(h + 1) * 96] for k in range(3)], [cqn.t[:, k, 0:W] for k in range(3)], [wuq.b, cqn.b])
                mm_group(pr, 96, W, [wuq.t[:, k, 768 + h * 96:768 + (h + 1) * 96] for k in range(3)], [cqn.t[:, k, 0:W] for k in range(3)], [wuq.b, cqn.b])
                ob = ob_r.next()
                S.op("act", "activation", [pm.b], [ob.b], out=ob.t[0:64, 0:W], in_=pm.t[0:64, 0:W], func=AF.Identity)
                rope(pm, pr, tA, 64, 96, W, ob.t[64:96, 0:W], ob.b)
                S.dma("sp", QTm[h, :, t0:t0 + W], ob.t[0:96, 0:W], reads=[ob.b], writes=[B_QTm[h]])
            for h in range(4):
                pm = pbr.next()
                pr = pbr.next()
                proj(pm, 128, O_DQ + h * 128)
                proj(pr, 128, O_DQR + h * 128)
                ob = ob_r.next()
                rope(pm, pr, tB, 0, 128, W, ob.t[:, 0:W], ob.b)
                S.dma("sp", QTd[h, :, t0:t0 + W], ob.t[:, 0:W], reads=[ob.b], writes=[B_QTd[h]])
            for c in range(8):
                pb = pbr.next()
                proj(pb, 128, O_GATE + c * 128)
                ob = ob_r.next()
                S.op("act", "activation", [pb.b], [ob.b], out=ob.t[:, 0:W], in_=pb.t[:, 0:W], func=AF.Silu)
                S.dma("sp", SG[c, :, t0:t0 + W], ob.t[:, 0:W], reads=[ob.b], writes=[B_SG])
        S.barrier()
        S.emit()
    G["wst"].close()
    if P.stop_after == 1:
        return

    qtiles = [(i * 512, 512, 0) for i in range(16)] + [(SEQ, 256, 64)]
    OB = P.dscr("OB", [8, 128, T], BF16)
    B_OB = Buf("OB")
    with ExitStack() as ph:
        KT2 = [P.sb(ph, f"KT{i}", [128, T], BF16) for i in range(2)]
        VA2 = [P.sb(ph, f"VA{i}", [128, NKC, 2, 128], BF16) for i in range(2)]
        qm_r = Ring([P.sb(ph, f"qm{i}", [128, 512], BF16) for i in range(3)])
        qd_r = Ring([(P.sb(ph, f"qa{i}", [128, 512], BF16), P.sb(ph, f"qb{i}", [128, 512], BF16)) for i in range(3)])
        for (qa, qb) in qd_r.items:
            S.op("pool", "memset", [], [qa.b], ap=qa.t[:], constant=0.0)
            S.op("pool", "memset", [], [qb.b], ap=qb.t[:], constant=0.0)
        pt_r = Ring([P.sb(ph, f"pt{i}", [128, 512], BF16) for i in range(14)])
        tmp_r = Ring([P.sb(ph, f"ptsum{i}", [128, 512], BF16) for i in range(6)])
        rl_r = Ring([P.sb(ph, f"rl{i}", [128, 512], F32) for i in range(3)])
        onesf = P.sb(ph, "onesf", [128, 128], F32)
        S.op("pool", "memset", [], [onesf.b], ap=onesf.t[:], constant=1.0)
        ot_r = Ring([P.sb(ph, f"ot{i}", [64, 512], BF16) for i in range(6)])
        ev_r = Ring([[P.sb(ph, f"ev{i}_{k}", [128, 512], F32) for k in range(4)] for i in range(1)])
        dd_r = Ring([P.sb(ph, f"dd{i}", [128, 512], F32) for i in range(2)])
        sq2_r = Ring([P.sb(ph, f"sqd{i}", [128, 512], BF16) for i in range(2)])
        rs2_r = Ring([P.sb(ph, f"rs2{i}", [128, 512], F32) for i in range(2)])
        otd_r = Ring([P.sb(ph, f"otd{i}", [128, 512], BF16) for i in range(2)])
        lam4 = P.sb(ph, "lam4", [128, 4, 64], F32)
        lamp = P.sb(ph, "lamp", [128, 2, 64], F32)
        lams = P.sb(ph, "lams", [128, 2], F32)
        neglam = P.sb(ph, "neglam", [128, 1], F32)
        sgc = P.sb(ph, "sgc", [128, 1], F32)
        eps2 = P.sb(ph, "eps2", [128, 1], F32)
        S.op("dve", "memset", [], [eps2.b], ap=eps2.t[:], constant=RMS_EPS)
        for i in range(2):
            S.op("pool", "memset", [], [VA2[i].b], ap=VA2[i].t[:, :, :, 64:128], constant=1.0)
        S.dma("sp", lam4.t[:].rearrange("p a d -> p (a d)"), lamv.rearrange("(o a) d -> o (a d)", o=1).broadcast_to([128, 256]), writes=[lam4.b])
        S.op("dve", "tensor_tensor", [lam4.b], [lamp.b], out=lamp.t[:], in0=lam4.t[:, 0:4:2, :], in1=lam4.t[:, 1:4:2, :], op=ALU.mult)
        S.op("dve", "tensor_reduce", [lamp.b], [lams.b], out=lams.t[:], in_=lamp.t[:], axis=AX.X, op=ALU.add)
        S.op("act", "activation", [lams.b], [lams.b], out=lams.t[:], in_=lams.t[:], func=AF.Exp)
        S.op("dve", "tensor_tensor", [lams.b], [neglam.b], out=neglam.t[:], in0=lams.t[:, 1:2], in1=lams.t[:, 0:1], op=ALU.subtract)
        S.op("dve", "tensor_scalar", [neglam.b], [neglam.b], out=neglam.t[:], in0=neglam.t[:], scalar1=-LAMBDA_INIT0, scalar2=None, op0=ALU.add)
        S.dma("sp", sgc.t[:], subg.rearrange("(p o) -> p o", o=1), writes=[sgc.b], allow_slow_non_contiguous=True)
        S.op("dve", "tensor_scalar", [sgc.b], [sgc.b], out=sgc.t[:], in0=sgc.t[:], scalar1=1.0 - LAMBDA_INIT0, scalar2=None, op0=ALU.mult)

        def load_head(hh, slot):
            KT, VA = KT2[slot], VA2[slot]
            if hh < 8:
                S.dma("sp", KT.t[0:96, :], KTm[hh], reads=[B_KTm[hh]], writes=[KT.b])
                for c0 in range(0, NKC, 11):
                    S.dma("sp", VA.t[:, c0:c0 + 11, 0, 0:64], Vm[hh, :, c0:c0 + 11, :], reads=[B_Vm], writes=[VA.b])
            else:
                h = hh - 8
                S.dma("sp", KT.t[:, :], KTd[h], reads=[B_KTd[h]], writes=[KT.b])
                VDv = VA.t[:].rearrange("p c j d -> p (c j) d")
                for j in range(2):
                    for c0 in range(0, NKC, 11):
                        S.dma("sp", VDv[:, c0:c0 + 11, j * 64:(j + 1) * 64], Vd[h, j, :, c0:c0 + 11, :], reads=[B_Vd], writes=[VA.b])

        qjobs = {}

        def load_q(hh, qi):
            if hh >= 12:
                return
            gq, W, kc0 = qtiles[qi]
            if hh < 8:
                qm = qm_r.next()
                S.dma("sp", qm.t[0:96, 0:W], QTm[hh, :, gq:gq + W], reads=[B_QTm[hh]], writes=[qm.b])
                qjobs[(hh, qi)] = (qm, qm)
            else:
                h = hh - 8
                qa, qb = qd_r.next()
                S.dma("sp", qa.t[0:64, 0:W], QTd[h, 0:64, gq:gq + W], reads=[B_QTd[h]], writes=[qa.b])
                S.dma("sp", qb.t[64:128, 0:W], QTd[h, 64:128, gq:gq + W], reads=[B_QTd[h]], writes=[qb.b])
                qjobs[(hh, qi)] = (qa, qb)

        LOOK = 3
        load_head(0, 0)
        load_q(0, 0)
        for hh in range(12):
            slot = hh % 2
            KT, VA = KT2[slot], VA2[slot]
            if hh + 1 < 12:
                load_head(hh + 1, 1 - slot)
            mla = hh < 8
            nmap, nv = (1, 1) if mla else (2, 1)
            VDv = VA.t[:].rearrange("p c j d -> p (c j) d")
            scale = MLA_SCALE if mla else DIFF_SCALE
            sc_r = Ring(banks[0:4]) if mla else Ring(banks[0:3])
            accr = Ring(banks[4:8])
            units = []
            for qi, (gq, W, kc0) in enumerate(qtiles):
                accs = [[accr.next()]] if mla else [[banks[4], banks[6]], [banks[5], banks[7]]]
                for kc in range(kc0, NKC):
                    for m in range(nmap):
                        units.append((qi, kc, m, accs, kc == kc0, kc == NKC - 1, kc == NKC - 1 and m == nmap - 1))
            pts = {}
            pair_pend = {}
            lacc_started = {}
            deferred = []

            def post_mla(qi, accs):
                gq, W, kc0 = qtiles[qi]
                acc = accs[0][0]
                rl = rl_r.next()
                S.op("dve", "reciprocal", [acc.b], [rl.b], out=rl.t[0:64, 0:W], in_=acc.t[64:128, 0:W])
                ot = ot_r.next()
                S.op("dve", "tensor_tensor", [acc.b, rl.b], [ot.b], out=ot.t[:, 0:W], in0=acc.t[0:64, 0:W], in1=rl.t[0:64, 0:W], op=ALU.mult)
                oc, op0 = hh // 2, (hh % 2) * 64
                S.dma("pool", OB[oc, op0:op0 + 64, gq:gq + W], ot.t[:, 0:W], reads=[ot.b], writes=[B_OB])

            def post_diff_stages(qi, accs):
                gq, W, kc0 = qtiles[qi]
                h = hh - 8
                ev = ev_r.next()
                aux = banks[3]
                rls = [rl_r.next(), rl_r.next()]
                dd = dd_r.next()
                sq = sq2_r.next()
                rs2 = rs2_r.next()
                ot = otd_r.next()

                def st_evac():
                    S.op("dve", "tensor_copy", [accs[0][0].b], [ev[0].b], out=ev[0].t[:, 0:W], in_=accs[0][0].t[:, 0:W])
                    S.op("act", "activation", [accs[1][0].b], [ev[1].b], out=ev[1].t[:, 0:W], in_=accs[1][0].t[:, 0:W], func=AF.Identity)
                    S.op("act", "activation", [accs[0][1].b], [ev[2].b], out=ev[2].t[:, 0:W], in_=accs[0][1].t[:, 0:W], func=AF.Identity)
                    S.op("dve", "tensor_copy", [accs[1][1].b], [ev[3].b], out=ev[3].t[:, 0:W], in_=accs[1][1].t[:, 0:W])

                def st_lbc(m):
                    S.op("pe", "matmul", [onesf.b, ev[2 + m].b], [aux.b], out=aux.t[:, 0:W], lhsT=onesf.t[:, :], rhs=ev[2 + m].t[:, 0:W], start=True, stop=True)

                def st_recip(m):
                    S.op("act", "activation", [aux.b], [rls[m].b], out=rls[m].t[:, 0:W], in_=aux.t[:, 0:W], func=AF.Ln)
                    S.op("act", "activation", [rls[m].b], [rls[m].b], out=rls[m].t[:, 0:W], in_=rls[m].t[:, 0:W], func=AF.Exp, scale=-1.0)

                def st_norm():
                    S.op("dve", "tensor_tensor", [ev[0].b, rls[0].b], [ev[0].b], out=ev[0].t[:, 0:W], in0=ev[0].t[:, 0:W], in1=rls[0].t[:, 0:W], op=ALU.mult)
                    S.op("pool", "tensor_tensor", [ev[1].b, rls[1].b], [ev[1].b], out=ev[1].t[:, 0:W], in0=ev[1].t[:, 0:W], in1=rls[1].t[:, 0:W], op=ALU.mult)

                def st_diff():
                    S.op("dve", "scalar_tensor_tensor", [ev[0].b, ev[1].b, neglam.b], [dd.b], out=dd.t[:, 0:W], in0=ev[1].t[:, 0:W], scalar=neglam.t[:, 0:1],
                         in1=ev[0].t[:, 0:W], op0=ALU.mult, op1=ALU.add)
                    S.op("pool", "tensor_tensor", [dd.b], [sq.b], out=sq.t[:, 0:W], in0=dd.t[:, 0:W], in1=dd.t[:, 0:W], op=ALU.mult)

                def st_ss():
                    S.op("pe", "matmul", [ones.b, sq.b], [aux.b], out=aux.t[:, 0:W], lhsT=ones.t[:, :], rhs=sq.t[:, 0:W], start=True, stop=True)

                def st_rs():
                    S.op("act", "activation", [aux.b, eps2.b], [rs2.b], out=rs2.t[:, 0:W], in_=aux.t[:, 0:W], func=AF.Ln, scale=1.0 / 128, bias=eps2.t[:, 0:1])
                    S.op("act", "activation", [rs2.b], [rs2.b], out=rs2.t[:, 0:W], in_=rs2.t[:, 0:W], func=AF.Exp, scale=-0.5)

                def st_out():
                    S.op("dve", "scalar_tensor_tensor", [dd.b, sgc.b, rs2.b], [ot.b], out=ot.t[:, 0:W],
                         in0=dd.t[:, 0:W], scalar=sgc.t[:, 0:1], in1=rs2.t[:, 0:W], op0=ALU.mult, op1=ALU.mult)
                    S.dma("pool", OB[4 + h, :, gq:gq + W], ot.t[:, 0:W], reads=[ot.b], writes=[B_OB])

                return [(0, st_evac), (2, lambda: st_lbc(0)), (3, lambda: st_recip(0)), (2, lambda: st_lbc(1)), (3, lambda: st_recip(1)),
                        (3, st_norm), (4, st_diff), (4, st_ss), (3, st_rs), (3, st_out)]

            nun = len(units)
            for i in range(nun + LOOK):
                if i < nun:
                    qi, kc, m, accs, first, last, qlast = units[i]
                    gq, W, kc0 = qtiles[qi]
                    if first and m == 0:
                        nq = (hh, qi + 1) if qi + 1 < len(qtiles) else (hh + 1, 0)
                        load_q(*nq)
                    qt = qjobs[(hh, qi)][m]
                    r1 = 96 if mla else 128
                    sb_ = sc_r.next()
                    S.op("pe", "matmul", [KT.b, qt.b], [sb_.b], out=sb_.t[:, 0:W], lhsT=KT.t[0:r1, kc * 128:(kc + 1) * 128],
                         rhs=qt.t[0:r1, 0:W], start=True, stop=True)
                    pt = pt_r.next()
                    S.op("act", "activation", [sb_.b], [pt.b], out=pt.t[:, 0:W], in_=sb_.t[:, 0:W], func=AF.Exp, scale=scale)
                    pts[i] = pt
                j_ = i - LOOK
                if j_ >= 0:
                    qi, kc, m, accs, first, last, qlast = units[j_]
                    gq, W, kc0 = qtiles[qi]
                    pt = pts.pop(j_)
                    acc = accs[m][0]
                    if mla:
                        S.op("pe", "matmul", [VA.b, pt.b], [acc.b], out=acc.t[:, 0:W], lhsT=VA.t[:, kc, 0, :], rhs=pt.t[:, 0:W], start=first, stop=last)
                    else:
                        S.op("pe", "matmul", [VA.b, pt.b], [acc.b], out=acc.t[:, 0:W], lhsT=VDv[:, kc, :], rhs=pt.t[:, 0:W], start=first, stop=last)
                        lacc = accs[m][1]
                        pend = pair_pend.setdefault(m, [])
                        pend.append(pt)
                        if len(pend) == 4 or last:
                            pair_pend[m] = []

                            def bsum(x, y):
                                t = tmp_r.next()
                                S.op("dve", "tensor_tensor", [x.b, y.b], [t.b], out=t.t[:, 0:W], in0=x.t[:, 0:W], in1=y.t[:, 0:W], op=ALU.add)
                                return t

                            src = pend[0]
                            if len(pend) >= 2:
                                src = bsum(pend[0], pend[1])
                            if len(pend) == 3:
                                src = bsum(src, pend[2])
                            elif len(pend) == 4:
                                src = bsum(src, bsum(pend[2], pend[3]))
                            if not lacc_started.get((qi, m)):
                                lacc_started[(qi, m)] = True
                                S.op("dve", "tensor_copy", [src.b], [lacc.b], out=lacc.t[:, 0:W], in_=src.t[:, 0:W])
                            else:
                                S.op("dve", "tensor_tensor", [lacc.b, src.b], [lacc.b], out=lacc.t[:, 0:W], in0=lacc.t[:, 0:W], in1=src.t[:, 0:W], op=ALU.add)
                    if qlast:
                        if mla:
                            post_mla(qi, accs)
                        else:
                            while deferred:
                                deferred.pop(0)[1]()
                            deferred.extend([list(x) for x in post_diff_stages(qi, accs)])
                    if deferred:
                        if deferred[0][0] <= 0:
                            deferred.pop(0)[1]()
                        else:
                            deferred[0][0] -= 1
            while deferred:
                deferred.pop(0)[1]()
        S.barrier()
        S.emit()

    with ExitStack() as p3:
        wo = P.sb(p3, "wo", [128, 8, D], BF16)
        S.dma("pool", wo.t[:], e_w_out.rearrange("(k p) n -> p k n", p=128), writes=[wo.b])
        out_proj_ln(P, G, L, qtiles, OB, B_OB, None, wo, SG, B_SG, xk, None, G["X1"], G["B_X1"], p3)
        S.barrier()
        S.emit()


def out_proj_ln(P, G, L, qtiles, OB, B_OB, HG, wo, SG, B_SG, xres, B_res, xout, B_out, st):
    S = P.S
    banks, gbc, lng, lnb = G["banks"], G["gbc"], G["lng"], G["lnb"]
    sg_r = Ring([P.sb(st, f"sg{i}", [128, 8, 512], BF16) for i in range(3)])
    og_r = Ring([P.sb(st, f"og{i}", [128, 8, 512], BF16) for i in range(2)]) if OB is not None else None
    ob_r = Ring([P.sb(st, f"obt{i}", [128, 8, 512], BF16) for i in range(3)]) if OB is not None else None
    xr_r = Ring([P.sb(st, f"xr{i}", [128, D], F32) for i in range(2)])
    v_r = Ring([P.sb(st, f"vv{i}", [128, D], F32) for i in range(3)])
    o_r = Ring([P.sb(st, f"oo{i}", [128, D], F32) for i in range(2)])
    st_r = Ring([P.sb(st, f"bst{i}", [128, 2, 6], F32) for i in range(2)])
    mv_r = Ring([P.sb(st, f"mv{i}", [128, 4], F32) for i in range(3)])
    epsl = P.sb(st, "epsl", [128, 1], F32)
    S.op("dve", "memset", [], [epsl.b], ap=epsl.t[:], constant=LN_EPS)
    yb = Ring(banks[0:8])
    pend = None

    def ln_apply(v, mv, r0):
        o = o_r.next()
        S.op("dve", "scalar_tensor_tensor", [v.b, mv.b, lng[L].b], [o.b], out=o.t[:], in0=v.t[:], scalar=mv.t[:, 0:1], in1=lng[L].t[:],
             op0=ALU.subtract, op1=ALU.mult)
        S.op("dve", "scalar_tensor_tensor", [o.b, mv.b, lnb[L].b], [o.b], out=o.t[:], in0=o.t[:], scalar=mv.t[:, 2:3], in1=lnb[L].t[:],
             op0=ALU.mult, op1=ALU.add)
        S.dma("pool", xout[r0:r0 + 128, :], o.t[:], reads=[o.b], writes=[B_out], final=True)

    def prep(qi):
        gq, W, kc0 = qtiles[qi]
        sg = sg_r.next()
        if OB is not None:
            S.dma("sp", sg.t[:, :, 0:W], SG[:, :, gq:gq + W].rearrange("c p w -> p c w"), reads=[B_SG], writes=[sg.b])
            obt = ob_r.next()
            S.dma("sp", obt.t[:, :, 0:W], OB[:, :, gq:gq + W].rearrange("c p w -> p c w"), reads=[B_OB], writes=[obt.b])
            return (sg, obt)
        S.dma("sp", sg.t[:, :, 0:W], HG[:, :, gq:gq + W].rearrange("c p w -> p c w"), reads=[B_SG], writes=[sg.b])
        return (sg, None)

    def make_og(qi):
        gq, W, kc0 = qtiles[qi]
        sg, obt = loaded.pop(qi)
        if obt is None:
            return sg
        og = og_r.next()
        S.op("dve", "tensor_tensor", [obt.b, sg.b], [og.b], out=og.t[:, :, 0:W], in0=obt.t[:, :, 0:W], in1=sg.t[:, :, 0:W], op=ALU.mult)
        return og

    nq = len(qtiles)
    loaded = {qi: prep(qi) for qi in range(min(2, nq))}
    ogs = {0: make_og(0)}
    for qi, (gq, W, kc0) in enumerate(qtiles):
        cond = 1 if gq >= SEQ else 0
        if qi + 2 < nq:
            loaded[qi + 2] = prep(qi + 2)
        if qi + 1 < nq:
            ogs[qi + 1] = make_og(qi + 1)
        og = ogs.pop(qi)
        for s in range(W // 128):
            r0 = gq + s * 128
            xr = xr_r.next()
            S.dma("sp", xr.t[:], xres[r0:r0 + 128, :], reads=([B_res] if B_res is not None else []), writes=[xr.b])
            ys = [yb.next() for _ in range(2)]
            for hh in range(2):
                for k in range(8):
                    S.op("pe", "matmul", [og.b, wo.b], [ys[hh].b], out=ys[hh].t[:, :], lhsT=og.t[:, k, s * 128:(s + 1) * 128],
                         rhs=wo.t[:, k, hh * 512:(hh + 1) * 512], start=(k == 0), stop=(k == 7))
            v = v_r.next()
            for hh in range(2):
                S.op("dve", "tensor_tensor", [ys[hh].b, gbc[L][cond].b], [v.b], out=v.t[:, hh * 512:(hh + 1) * 512], in0=ys[hh].t[:, :],
                     in1=gbc[L][cond].t[:, hh * 512:(hh + 1) * 512], op=ALU.mult)
            S.op("dve", "scalar_tensor_tensor", [xr.b, v.b], [v.b], out=v.t[:], in0=xr.t[:], scalar=ALPHA, in1=v.t[:], op0=ALU.mult, op1=ALU.add)
            bst = st_r.next()
            for hh in range(2):
                S.op("dve", "bn_stats", [v.b], [bst.b], out=bst.t[:, hh, :], in_=v.t[:, hh * 512:(hh + 1) * 512])
            mv = mv_r.next()
            S.op("dve", "bn_aggr", [bst.b], [mv.b], out=mv.t[:, 0:2], in_=bst.t[:].rearrange("p a b -> p (a b)"))
            S.op("act", "activation", [mv.b, epsl.b], [mv.b], out=mv.t[:, 2:3], in_=mv.t[:, 1:2], func=AF.Ln, bias=epsl.t[:, 0:1])
            S.op("act", "activation", [mv.b], [mv.b], out=mv.t[:, 2:3], in_=mv.t[:, 2:3], func=AF.Exp, scale=-0.5)
            if pend is not None:
                ln_apply(*pend)
            pend = (v, mv, r0)
    if pend is not None:
        ln_apply(*pend)


BLK = 1024


def layer1(P, G):
    S = P.S
    X1, B_X1, banks, mod = G["X1"], G["B_X1"], G["banks"], G["mod"]
    o_w_in, o_conv_w, o_conv_b, o_gw, o_gb, o_lam, o_w_out = (G[k] for k in "o_w_in o_conv_w o_conv_b o_gw o_gb o_lam o_w_out".split())
    L = 1
    UX = P.dscr("UX", [8, 128, T], F32)
    SGL = P.dscr("SGL", [8, 128, SEQ], BF16)
    HG = P.dscr("HG", [8, 128, SEQ], BF16)
    B_UX = [Buf(f"UX{c}") for c in range(8)]
    B_SGL = [Buf(f"SGL{c}") for c in range(8)]
    B_HG = Buf("HG")

    with ExitStack() as ph:
        w1 = P.sb(ph, "w1", [128, 8, 2 * D], BF16)
        for j0 in range(0, 2 * D, 1024):
            S.dma("pool", w1.t[:, :, j0:j0 + 1024], o_w_in[:, j0:j0 + 1024].rearrange("(k p) n -> p k n", p=128), writes=[w1.b])
        xb_r = Ring([P.sb(ph, f"l1xb{i}", [128, 4, D], BF16) for i in range(2)])
        xm_r = Ring([P.sb(ph, f"l1xm{i}", [128, 8, 512], BF16) for i in range(2)])
        uf_r = Ring([P.sb(ph, f"l1uf{i}", [128, 512], F32) for i in range(4)])
        gb_r = Ring([P.sb(ph, f"l1gb{i}", [128, 512], BF16) for i in range(4)])
        pbr = Ring(banks[2:8])
        ts_r = Ring([(banks[bi].t[:].bitcast(BF16), banks[bi].b) for bi in range(2)])

        def loads(t):
            W = 512 if t < 16 else 256
            xb = xb_r.next()
            S.dma("pool", xb.t[:, 0:W // 128, :], X1[t * 512:t * 512 + W, :].rearrange("(s p) f -> p s f", p=128), reads=[B_X1], writes=[xb.b])
            return xb

        nxt = loads(0)
        for t in range(NT):
            W = 512 if t < 16 else 256
            t0 = t * 512
            xb = nxt
            if t + 1 < NT:
                nxt = loads(t + 1)
            xm = xm_r.next()
            load_xm_tile(P, G, L, X1, B_X1, t, xb, xm, ts_r, 1 if t == 16 else 0)
            for m in range(16 if t < 16 else 8):
                pb = pbr.next()
                for k in range(8):
                    S.op("pe", "matmul", [w1.b, xm.b], [pb.b], out=pb.t[:, 0:W], lhsT=w1.t[:, k, m * 128:(m + 1) * 128], rhs=xm.t[:, k, 0:W],
                         start=(k == 0), stop=(k == 7))
                if m < 8:
                    uf = uf_r.next()
                    S.op("dve", "tensor_copy", [pb.b], [uf.b], out=uf.t[:, 0:W], in_=pb.t[:, 0:W])
                    S.dma("sp", UX[m, :, t0:t0 + W], uf.t[:, 0:W], reads=[uf.b], writes=[B_UX[m]])
                else:
                    gb = gb_r.next()
                    S.op("act", "activation", [pb.b], [gb.b], out=gb.t[:, 0:W], in_=pb.t[:, 0:W], func=AF.Silu)
                    S.dma("sp", SGL[m - 8, :, t0:t0 + W], gb.t[:, 0:W], reads=[gb.b], writes=[B_SGL[m - 8]])
        S.barrier()
        S.emit()

    with ExitStack() as ph:
        gw = P.sb(ph, "gw", [128, 32, 128], BF16)
        gbias = P.sb(ph, "gbias", [128, 32], F32)
        lamc = P.sb(ph, "lamc", [128, 16], F32)
        sc8 = P.sb(ph, "sc8", [128, 16], F32)
        sc16 = P.sb(ph, "sc16", [128, 16], F32)
        cw = P.sb(ph, "cw", [128, 4, 8], F32)
        cb = P.sb(ph, "cb", [128, 8], F32)
        one1 = P.sb(ph, "one1", [128, 1], F32)
        S.op("dve", "memset", [], [one1.b], ap=one1.t[:], constant=1.0)
        S.dma("pool", gw.t[:], o_gw.rearrange("g d k i j -> i (g d k) j"), writes=[gw.b])
        S.dma("sp", gbias.t[:], o_gb.rearrange("g d k j -> j (g d k)"), writes=[gbias.b], allow_slow_non_contiguous=True)
        S.dma("sp", lamc.t[:], o_lam.rearrange("d (k p) -> p (d k)", p=128), writes=[lamc.b], allow_slow_non_contiguous=True)
        S.dma("sp", cw.t[:], o_conv_w.rearrange("t (k p) -> p t k", p=128), writes=[cw.b], allow_slow_non_contiguous=True)
        S.dma("sp", cb.t[:], o_conv_b.rearrange("(k p) -> p k", p=128), writes=[cb.b], allow_slow_non_contiguous=True)
        S.op("act", "activation", [lamc.b], [lamc.b], out=lamc.t[:], in_=lamc.t[:], func=AF.Exp, scale=-1.0)
        S.op("act", "activation", [lamc.b, one1.b], [lamc.b], out=lamc.t[:], in_=lamc.t[:], func=AF.Ln, bias=one1.t[:, 0:1])
        h8, h16 = sc8, sc16
        S.op("dve", "tensor_scalar", [lamc.b], [h8.b], out=h8.t[:], in0=lamc.t[:], scalar1=-4.0, scalar2=None, op0=ALU.mult)
        S.op("dve", "tensor_scalar", [lamc.b], [h16.b], out=h16.t[:], in0=lamc.t[:], scalar1=-8.0, scalar2=None, op0=ALU.mult)
        hbias = P.sb(ph, "hbias", [128, 32], F32)
        S.op("dve", "tensor_scalar", [gbias.b], [hbias.b], out=hbias.t[:], in0=gbias.t[:], scalar1=0.5, scalar2=None, op0=ALU.mult)

        uh = P.sb(ph, "uh", [128, SEQ + 4], F32)
        uxc = P.sb(ph, "uxc", [128, CTX + 4], F32)
        ua = P.sb(ph, "ua", [128, T], F32)
        ubf = P.sb(ph, "ubf", [128, T], BF16)
        hc = [P.sb(ph, f"hc{d}", [128, CTX], F32) for d in range(2)]
        r_r = Ring([P.sb(ph, f"r{i}", [128, BLK], F32) for i in range(2)])
        i_r = Ring([P.sb(ph, f"i{i}", [128, BLK], F32) for i in range(4)])
        a_r = Ring([P.sb(ph, f"a{i}", [128, BLK], F32) for i in range(4)])
        s_r = Ring([P.sb(ph, f"s{i}", [128, BLK], F32) for i in range(4)])
        q25 = P.sb(ph, "q25", [128, 1], F32)
        S.op("dve", "memset", [], [q25.b], ap=q25.t[:], constant=0.25)
        g_r = Ring([P.sb(ph, f"g{i}", [128, BLK], F32) for i in range(2)])
        hb_r = Ring([P.sb(ph, f"hb{i}", [128, BLK], F32) for i in range(2)])
        sgl_r = Ring([P.sb(ph, f"sgl{i}", [128, BLK], BF16) for i in range(2)])
        hg_r = Ring([P.sb(ph, f"hg{i}", [128, BLK], BF16) for i in range(2)])
        gp_r = Ring(banks[0:8])
        S.op("pool", "memset", [], [uh.b], ap=uh.t[:], constant=0.0)
        S.op("pool", "memset", [], [uxc.b], ap=uxc.t[:], constant=0.0)

        for c in range(8):
            if c > 0:
                S.op("pool", "memset", [], [uh.b], ap=uh.t[:, 0:2], constant=0.0)
                S.op("pool", "memset", [], [uh.b], ap=uh.t[:, SEQ + 2:SEQ + 4], constant=0.0)
            S.dma("sp", uh.t[:, 2:2 + SEQ], UX[c, :, 0:SEQ], reads=[B_UX[c]], writes=[uh.b])
            S.dma("sp", uxc.t[:, 2:2 + CTX], UX[c, :, SEQ:T], reads=[B_UX[c]], writes=[uxc.b])
            for (src, n, o0) in ((uh, SEQ, 0), (uxc, CTX, SEQ)):
                S.op("dve", "tensor_scalar", [src.b, cw.b, cb.b], [ua.b], out=ua.t[:, o0:o0 + n], in0=src.t[:, 0:n], scalar1=cw.t[:, 0, c:c + 1],
                     scalar2=cb.t[:, c:c + 1], op0=ALU.mult, op1=ALU.add)
                for k in range(1, 4):
                    S.op("dve", "scalar_tensor_tensor", [src.b, cw.b, ua.b], [ua.b], out=ua.t[:, o0:o0 + n], in0=src.t[:, k:k + n],
                         scalar=cw.t[:, k, c:c + 1], in1=ua.t[:, o0:o0 + n], op0=ALU.mult, op1=ALU.add)
            S.op("act", "activation", [ua.b], [ubf.b], out=ubf.t[:], in_=ua.t[:], func=AF.Identity)
            for d in range(2):
                ia, ix = (0 * 2 + d) * 8 + c, (1 * 2 + d) * 8 + c
                dk = d * 8 + c
                lat = [(b * BLK, BLK) for b in range(SEQ // BLK)]
                blocks = [(SEQ, CTX, True)] + [(t0, n, False) for (t0, n) in (lat if d == 0 else lat[::-1])]
                st1 = {}
                chain = {"prev": None}

                def stage1(k):
                    t0, n, is_ctx = blocks[k]
                    r, it, a, s_ = r_r.next(), i_r.next(), a_r.next(), s_r.next()
                    for q0 in range(0, n, 512):
                        w = min(512, n - q0)
                        for (gi, dst) in ((ia, r), (ix, it)):
                            pb = gp_r.next()
                            S.op("pe", "matmul", [gw.b, ubf.b], [pb.b], out=pb.t[:, 0:w], lhsT=gw.t[:, gi, :], rhs=ubf.t[:, t0 + q0:t0 + q0 + w],
                                 start=True, stop=True)
                            S.op("act", "activation", [pb.b, hbias.b], [dst.b], out=dst.t[:, q0:q0 + w], in_=pb.t[:, 0:w], func=AF.Tanh,
                                 scale=0.5, bias=hbias.t[:, gi:gi + 1])
                    S.op("act", "activation", [r.b, h8.b], [a.b], out=a.t[:, 0:n], in_=r.t[:, 0:n], func=AF.Exp, scale=h8.t[:, dk:dk + 1], bias=h8.t[:, dk:dk + 1])
                    S.op("act", "activation", [r.b, h16.b], [s_.b], out=s_.t[:, 0:n], in_=r.t[:, 0:n], func=AF.Exp, scale=h16.t[:, dk:dk + 1], bias=h16.t[:, dk:dk + 1])
                    S.op("pool", "tensor_scalar", [s_.b], [s_.b], out=s_.t[:, 0:n], in0=s_.t[:, 0:n], scalar1=1.0, scalar2=0.0, op0=ALU.min, op1=ALU.max)
                    st1[k] = (it, a, s_)

                def stage2(k):
                    t0, n, is_ctx = blocks[k]
                    it, a, s_ = st1.pop(k)
                    g = g_r.next()
                    S.op("act", "activation", [s_.b, q25.b], [s_.b], out=s_.t[:, 0:n], in_=s_.t[:, 0:n], func=AF.Sqrt, scale=-0.25, bias=q25.t[:, 0:1])
                    S.op("dve", "scalar_tensor_tensor", [it.b, ua.b], [g.b], out=g.t[:, 0:n], in0=it.t[:, 0:n], scalar=1.0, in1=ua.t[:, t0:t0 + n],
                         op0=ALU.add, op1=ALU.mult)
                    S.op("dve", "tensor_tensor", [g.b, s_.b], [g.b], out=g.t[:, 0:n], in0=g.t[:, 0:n], in1=s_.t[:, 0:n], op=ALU.mult)
                    if is_ctx:
                        dst, dap, dbuf = hc[d], hc[d].t[:, 0:n], hc[d].b
                    elif d == 0:
                        dst, dap, dbuf = uh, uh.t[:, 2 + t0:2 + t0 + n], uh.b
                    else:
                        dst = hb_r.next()
                        dap, dbuf = dst.t[:, 0:n], dst.b
                    prev_init = chain["prev"]
                    init = 0.0 if prev_init is None else prev_init[0]
                    rd = [a.b, g.b] + ([] if prev_init is None else [prev_init[1]])
                    if d == 0:
                        S.op("dve", "tensor_tensor_scan", rd, [dbuf], out=dap, data0=a.t[:, 0:n], data1=g.t[:, 0:n], initial=init, op0=ALU.mult, op1=ALU.add)
                        chain["prev"] = (dap[:, n - 1:n], dbuf)
                    else:
                        S.op("dve", "tensor_tensor_scan", rd, [dbuf], out=dap[:, ::-1], data0=a.t[:, 0:n][:, ::-1], data1=g.t[:, 0:n][:, ::-1],
                             initial=init, op0=ALU.mult, op1=ALU.add)
                        chain["prev"] = (dap[:, 0:1], dbuf)
                    if d == 1 and not is_ctx:
                        sgl = sgl_r.next()
                        S.dma("sp", sgl.t[:, 0:n], SGL[c, :, t0:t0 + n], reads=[B_SGL[c]], writes=[sgl.b])
                        S.op("dve", "tensor_tensor", [dst.b, uh.b], [g.b], out=g.t[:, 0:n], in0=dst.t[:, 0:n], in1=uh.t[:, 2 + t0:2 + t0 + n], op=ALU.add)
                        hg = hg_r.next()
                        S.op("pool", "tensor_tensor", [g.b, sgl.b], [hg.b], out=hg.t[:, 0:n], in0=g.t[:, 0:n], in1=sgl.t[:, 0:n], op=ALU.mult)
                        S.dma("pool", HG[c, :, t0:t0 + n], hg.t[:, 0:n], reads=[hg.b], writes=[B_HG])

                for p0 in range(0, len(blocks), 2):
                    ks = [k for k in (p0, p0 + 1) if k < len(blocks)]
                    for k in ks:
                        stage1(k)
                    for k in ks:
                        stage2(k)
        S.barrier()
        S.emit()

    with ExitStack() as ph:
        wo = P.sb(ph, "wo1", [128, 8, D], BF16)
        S.dma("pool", wo.t[:], o_w_out.rearrange("(k p) n -> p k n", p=128), writes=[wo.b])
        qtiles = [(i * 512, 512, 0) for i in range(16)]
        out_proj_ln(P, G, L, qtiles, None, None, HG, wo, None, B_HG, X1, B_X1, G["out"], G["B_out"], ph)
        S.barrier()
        S.emit()


def _rope_tabs(pos, rot_dim):
    rows = (pos // 64).astype(np.float32)
    cols = (pos % 64).astype(np.float32)
    n_freq = rot_dim // 4
    freqs = (np.float32(10000.0) ** (-np.arange(n_freq, dtype=np.float32) / np.float32(n_freq))).astype(np.float32)
    ang = np.concatenate([rows[:, None] * freqs, cols[:, None] * freqs], -1).astype(np.float32)
    return np.cos(ang).T.astype(np.float32), np.sin(ang).T.astype(np.float32)


def _host_layout(inputs):
    f = lambda k: np.asarray(inputs[k], np.float32)
    x, ctx, c, c_ctx = f("x"), f("ctx"), f("c"), f("c_ctx")
    w_in, w_uq, w_ukv = f("e_w_in")[0], f("e_w_uq")[0], f("e_w_ukv")[0]
    kr = w_in[:, 640:672]
    junk = w_in[:, 384:448]
    blk_kr = np.concatenate([junk, kr], 1)
    blk_krr = np.concatenate([junk, kr[:, 16:32], kr[:, 0:16]], 1)

    def rot64(w):
        w4 = w.reshape(w.shape[0], -1, 2, 32)
        return np.concatenate([w4[:, :, 1, :], w4[:, :, 0, :]], -1).reshape(w.shape[0], -1)

    wA = np.ascontiguousarray(np.concatenate([w_in, blk_kr, blk_krr, rot64(w_in[:, 672:1184]), rot64(w_in[:, 1184:1696])], 1))
    uq3 = w_uq.reshape(384, 8, 96)
    uq_rot = np.concatenate([uq3[:, :, 0:64], uq3[:, :, 80:96], uq3[:, :, 64:80]], -1).reshape(384, 768)
    w_uq2 = np.ascontiguousarray(np.concatenate([w_uq, uq_rot], 1))
    kv3 = w_ukv.reshape(256, 8, 128)
    w_ukv2 = np.ascontiguousarray(np.concatenate([kv3[:, :, 0:64].reshape(256, 512), kv3[:, :, 64:128].reshape(256, 512)], 1))
    lamv = np.stack([f(k)[0] for k in ("e_lam_q1", "e_lam_k1", "e_lam_q2", "e_lam_k2")])
    pos = np.arange(SEQ)
    cA, sA = _rope_tabs(pos, 32)
    cB, sB = _rope_tabs(pos, 64)

    def padctx(cs, sn):
        return (np.concatenate([cs, np.ones((cs.shape[0], CTX), np.float32)], 1),
                np.concatenate([sn, np.zeros((sn.shape[0], CTX), np.float32)], 1))

    cA, sA = padctx(cA, sA)
    cB, sB = padctx(cB, sB)
    shared = dict(
        ada_w=f("ada_w"), ada_b=f("ada_b"), ln_g=f("post_ln_g"), ln_b=f("post_ln_b"), identd=np.eye(128, dtype=np.float32),
        wA=wA, w_uq=w_uq2, w_ukv=w_ukv2, qg=f("e_q_norm_g")[0], kvg=f("e_kv_norm_g")[0], lamv=np.ascontiguousarray(lamv),
        subg=f("e_subln_g")[0], e_w_out=f("e_w_out")[0],
        tabA=np.ascontiguousarray(np.stack([np.concatenate([cA, cA], 0), np.concatenate([-sA, sA], 0)])),
        tabB=np.ascontiguousarray(np.stack([np.concatenate([cB, cB], 0), np.concatenate([-sB, sB], 0)])),
        o_w_in=f("o_w_in")[0], o_conv_w=f("o_conv_w")[0], o_conv_b=f("o_conv_b")[0],
        o_gw=np.ascontiguousarray(np.stack([f("o_gate_a_w")[0], f("o_gate_x_w")[0]])),
        o_gb=np.ascontiguousarray(np.stack([f("o_gate_a_b")[0], f("o_gate_x_b")[0]])),
        o_lam=f("o_lru_lambda")[0], o_w_out=f("o_w_out")[0],
    )
    in_maps = []
    for b in range(4):
        m = dict(shared)
        m.update(xk=np.ascontiguousarray(np.concatenate([x[b], ctx[b]], 0)), cvec=np.ascontiguousarray(np.stack([c[b], c_ctx])))
        in_maps.append(m)
    return in_maps


_NC_CACHE = {}
CORE_IDS = [0, 2, 4, 6]


def kernel(**inputs):
    in_maps = _host_layout(inputs)
    if "nc" not in _NC_CACHE:
        _NC_CACHE["nc"] = build_program()
    nc = _NC_CACHE["nc"]
    res = run_bass_kernel_spmd(nc, in_maps, core_ids=CORE_IDS)
    return np.stack([res.results[b]["out"] for b in range(4)], 0).astype(np.float32)
```

```python
import math
import numpy as np
from contextlib import ExitStack
import concourse.bass as bass
import concourse.mybir as mybir
from concourse.bass_utils import run_bass_kernel_spmd

F32 = mybir.dt.float32
BF16 = mybir.dt.bfloat16
AF = mybir.ActivationFunctionType
ALU = mybir.AluOpType
AX = mybir.AxisListType

D = 1024
SEQ = 8192
NOWN = 4096
CTX = 256
T = SEQ + CTX
NQ = NOWN + CTX
NKC = T // 128
ALPHA = 4 ** 0.25
LN_EPS = 1e-6
RMS_EPS = 1e-6
MLA_SCALE = 96 ** -0.5
DIFF_SCALE = 64 ** -0.5
LAMBDA_INIT0 = 0.8 - 0.6 * math.exp(-0.3 * 0)
NWA = 3232 + 96 + 96 + 512 + 512


class Buf:
    def __init__(self, name):
        self.name = name
        self.w = None
        self.r = []
        self.wsem = None
        self.rsem = None


class Sched:
    ENG = ("pe", "act", "dve", "pool", "sp")
    ENGATTR = {"pe": "tensor", "act": "scalar", "dve": "vector", "pool": "gpsimd", "sp": "sync"}
    CE = ("pe", "act", "dve", "pool")

    def __init__(self, nc, stack):
        self.nc = nc
        self.stack = stack
        self.ops = {e: [] for e in self.ENG}
        self.esem = {}
        self.ecount = {}
        self.known = {e: {} for e in self.ENG}
        self.pending = {e: [] for e in self.ENG}
        self.nsem = 0
        self.dsems = []
        self.final_tokens = []
        self.free_dsems = {"sp": [], "pool": [], "act": []}
        self.sem_owners = []
        self._fresh_engine_sems()

    def _sem(self, name):
        self.nsem += 1
        return self.stack.enter_context(self.nc.semaphore(f"q{self.nsem}_{name}"))

    def _fresh_engine_sems(self):
        for e in self.CE:
            self.esem[e] = self._sem("e_" + e)
            self.ecount[e] = 0

    def new_dsem(self, name, q):
        if self.free_dsems[q]:
            return self.free_dsems[q].pop()
        ent = [self._sem(name), 0, q]
        self.dsems.append(ent)
        return ent

    def barrier(self):
        toks = [(self.esem[e], self.ecount[e], "x") for e in self.CE if self.ecount[e] > 0]
        toks += [(ent[0], ent[1], "dma") for ent in self.dsems if ent[1] > 0]
        for e in self.ENG:
            self.pending[e] = self.pending[e] + list(toks)
        for e in self.CE:
            if self.ecount[e] > 20000:
                self.esem[e] = self._sem("e_" + e)
                self.ecount[e] = 0
        for (b, attr) in self.sem_owners:
            ent = getattr(b, attr)
            if ent is not None:
                self.free_dsems[ent[2]].append(ent)
                setattr(b, attr, None)
        self.sem_owners = []

    def _waits(self, eng, reads, writes):
        toks = list(self.pending[eng])
        self.pending[eng] = []
        for b in reads:
            if b.w is not None:
                toks.append(b.w)
        for b in writes:
            if b.w is not None:
                toks.append(b.w)
            toks.extend(b.r)
        need = {}
        for (sem, val, src) in toks:
            if src == "pe" and eng == "pe":
                continue
            k = id(sem)
            if self.known[eng].get(k, 0) >= val:
                continue
            if k not in need or need[k][1] < val:
                need[k] = (sem, val)
        for k, (sem, val) in need.items():
            self.known[eng][k] = val
        return list(need.values())

    def _mark(self, tok, reads, writes):
        for b in writes:
            b.w = tok
            b.r = []
        for b in reads:
            if b not in writes:
                b.r.append(tok)
                if len(b.r) > 16:
                    best = {}
                    for t in b.r:
                        k = id(t[0])
                        if k not in best or best[k][1] < t[1]:
                            best[k] = t
                    b.r = list(best.values())

    def op(self, eng, name, reads, writes, **kw):
        waits = self._waits(eng, reads, writes)
        self.ecount[eng] += 1
        tok = (self.esem[eng], self.ecount[eng], eng)
        self.ops[eng].append((waits, name, kw, (self.esem[eng], 1)))
        self._mark(tok, reads, writes)
        return tok

    def dma(self, q, out, in_, reads=(), writes=(), final=False, **kw):
        waits = self._waits(q, reads, writes)
        tgt = writes[0] if writes else reads[0]
        attr = ("wsem_" if writes else "rsem_") + q
        cur = getattr(tgt, attr, None)
        if cur is None:
            cur = self.new_dsem(tgt.name, q)
            setattr(tgt, attr, cur)
            self.sem_owners.append((tgt, attr))
        cur[1] += 16
        tok = (cur[0], cur[1], "dma")
        kw = dict(kw)
        kw.update(out=out, in_=in_)
        self.ops[q].append((waits, "dma_start", kw, (cur[0], 16)))
        self._mark(tok, reads, writes)
        if final:
            self.final_tokens.append(tok)
        return tok

    def emit(self, last=False):
        nc = self.nc
        fin = {}
        if last:
            toks = list(self.final_tokens) + [(ent[0], ent[1], "dma") for ent in self.dsems if ent[1] > 0]
            for (sem, val, _) in toks:
                k = id(sem)
                if k not in fin or fin[k][1] < val:
                    fin[k] = (sem, val)
        with nc.Block() as block:
            for e in self.ENG:
                ops = self.ops[e]
                is_last = last and e == "sp"
                if not ops and not is_last:
                    continue

                def body(engobj, ops=ops, is_last=is_last):
                    for (waits, name, kw, inc) in ops:
                        for (sem, val) in waits:
                            engobj.wait_ge(sem, val)
                        getattr(engobj, name)(**kw).then_inc(inc[0], inc[1])
                    if is_last:
                        for (sem, val) in fin.values():
                            engobj.wait_ge(sem, val)

                getattr(block, self.ENGATTR[e])(body)
        self.ops = {e: [] for e in self.ENG}


class Tl:
    def __init__(self, t, name):
        self.t = t
        self.b = Buf(name)


class Ring:
    def __init__(self, items):
        self.items = items
        self.i = 0

    def next(self):
        it = self.items[self.i % len(self.items)]
        self.i += 1
        return it


class Prog:
    def __init__(self, nc, stack):
        self.nc = nc
        self.S = Sched(nc, stack)
        self.debug = False
        self.stop_after = 99

    def din(self, name, shape, dt=F32):
        return self.nc.dram_tensor(name, list(shape), dt, kind="ExternalInput").ap()

    def dout(self, name, shape, dt=F32):
        return self.nc.dram_tensor(name, list(shape), dt, kind="ExternalOutput").ap()

    def dscr(self, name, shape, dt):
        kind = "ExternalOutput" if self.debug else "Internal"
        return self.nc.dram_tensor(name, list(shape), dt, kind=kind).ap()

    def sb(self, st, name, shape, dt):
        self.uid = getattr(self, "uid", 0) + 1
        name = f"{name}_{self.uid}"
        return Tl(st.enter_context(self.nc.sbuf_tensor(name, list(shape), dt)), name)

    def ps(self, st, name):
        return Tl(st.enter_context(self.nc.psum_tensor(name, [128, 512], F32)), name)


O_CQ, O_CKV, O_DQ, O_DK, O_DV, O_GATE = 0, 384, 672, 1184, 1696, 2208
O_KR, O_KRR, O_DQR, O_DKR = 3232, 3328, 3424, 3936
NT = 17


def build_program(debug=False, stop_after=99):
    nc = bass.Bass("TRN2", target_bir_lowering=False)
    with ExitStack() as top:
        P = Prog(nc, top)
        P.debug = debug
        P.stop_after = stop_after
        S = P.S
        G = {}
        G["xk"] = P.din("xk", [T, D])
        cvec = P.din("cvec", [2, D])
        G["tabA"] = P.din("tabA", [2, 32, T])
        G["tabB"] = P.din("tabB", [2, 64, T])
        ada_w = P.din("ada_w", [2, D, 3 * D])
        ada_b = P.din("ada_b", [2, 3 * D])
        ln_g = P.din("ln_g", [2, D])
        ln_b = P.din("ln_b", [2, D])
        identd = P.din("identd", [128, 128])
        G["wA"] = P.din("wA", [D, NWA])
        G["w_uq"] = P.din("w_uq", [384, 1536])
        G["w_ukv"] = P.din("w_ukv", [256, 1024])
        G["qg"] = P.din("qg", [384])
        G["kvg"] = P.din("kvg", [256])
        G["lamv"] = P.din("lamv", [4, 64])
        G["subg"] = P.din("subg", [128])
        G["e_w_out"] = P.din("e_w_out", [D, D])
        G["o_w_in"] = P.din("o_w_in", [D, 2 * D])
        G["o_conv_w"] = P.din("o_conv_w", [4, D])
        G["o_conv_b"] = P.din("o_conv_b", [D])
        G["o_gw"] = P.din("o_gw", [2, 2, 8, 128, 128])
        G["o_gb"] = P.din("o_gb", [2, 2, 8, 128])
        G["o_lam"] = P.din("o_lam", [2, D])
        G["o_w_out"] = P.din("o_w_out", [D, D])
        G["out"] = P.dout("out", [SEQ, D])
        G["B_out"] = Buf("out")
        G["X1"] = P.dscr("X1", [T, D], F32)
        G["B_X1"] = Buf("X1")

        ident = P.sb(top, "ident", [128, 128], BF16)
        ones = P.sb(top, "ones", [128, 128], BF16)
        identf = P.sb(top, "identf", [128, 128], F32)
        banks = [P.ps(top, f"pb{i}") for i in range(8)]
        mod = P.sb(top, "mod", [128, 2, 24, 2], F32)
        gbc = [[P.sb(top, f"gbc{l}{c}", [128, D], F32) for c in range(2 - l)] for l in range(2)]
        lng = [P.sb(top, f"lng{l}", [128, D], F32) for l in range(2)]
        lnb = [P.sb(top, f"lnb{l}", [128, D], F32) for l in range(2)]
        G.update(ident=ident, ones=ones, banks=banks, mod=mod, gbc=gbc, lng=lng, lnb=lnb)

        S.dma("sp", identf.t[:], identd[:, :], writes=[identf.b])
        S.op("dve", "tensor_copy", [identf.b], [ident.b], out=ident.t[:], in_=identf.t[:])
        S.op("pool", "memset", [], [ones.b], ap=ones.t[:], constant=1.0)

        wst = ExitStack()
        G["wa"] = P.sb(wst, "wa", [128, 8, NWA], BF16)
        G["wuq"] = P.sb(wst, "wuq", [128, 3, 1536], BF16)
        G["wukv"] = P.sb(wst, "wukv", [128, 2, 1024], BF16)
        G["wst"] = wst

        with ExitStack() as ph:
            adw = P.sb(ph, "adw", [128, 8, 3 * D], BF16)
            adb = P.sb(ph, "adb", [128, 24], F32)
            cT = P.sb(ph, "cT", [128, 2, 8], F32)
            sc = P.sb(ph, "sc", [128, 8, 2], BF16)
            scb = [P.sb(ph, f"scb{c}", [128, 8, 128], BF16) for c in range(2)]
            gb_b = P.sb(ph, "gb_b", [128, D], F32)
            S.dma("sp", cT.t[:], cvec.rearrange("c (k p) -> p c k", p=128), writes=[cT.b], allow_slow_non_contiguous=True)
            S.op("act", "activation", [cT.b], [sc.b], out=sc.t[:].rearrange("p k c -> p c k"), in_=cT.t[:], func=AF.Silu)
            for c in range(2):
                S.op("dve", "tensor_copy", [sc.b], [scb[c].b], out=scb[c].t[:], in_=sc.t[:, :, c:c + 1].to_broadcast([128, 8, 128]))
            for l in range(2):
                for j in range(3):
                    S.dma("pool", adw.t[:, :, j * D:(j + 1) * D], ada_w[l, :, j * D:(j + 1) * D].rearrange("(k p) n -> p k n", p=128),
                          writes=[adw.b])
                S.dma("sp", adb.t[:], ada_b[l].rearrange("(j p) -> p j", p=128), writes=[adb.b], allow_slow_non_contiguous=True)
                S.dma("sp", gb_b.t[:], ada_b[l:l + 1, 2 * D:3 * D].broadcast_to([128, D]), writes=[gb_b.b])
                S.dma("sp", lng[l].t[:], ln_g[l:l + 1, :].broadcast_to([128, D]), writes=[lng[l].b])
                S.dma("sp", lnb[l].t[:], ln_b[l:l + 1, :].broadcast_to([128, D]), writes=[lnb[l].b])
                pb = banks[l]
                for j in range(24):
                    for k in range(8):
                        S.op("pe", "matmul", [adw.b, sc.b], [pb.b], out=pb.t[:, j * 2:j * 2 + 2], lhsT=adw.t[:, k, j * 128:(j + 1) * 128],
                             rhs=sc.t[:, k, :], start=(k == 0), stop=(k == 7))
                S.op("dve", "tensor_tensor", [pb.b, adb.b], [mod.b], out=mod.t[:, l, :, :], in0=pb.t[:, 0:48].rearrange("p (j c) -> p j c", c=2),
                     in1=adb.t[:].unsqueeze(2).to_broadcast([128, 24, 2]), op=ALU.add)
                S.op("dve", "tensor_scalar", [mod.b], [mod.b], out=mod.t[:, l, 8:16, :], in0=mod.t[:, l, 8:16, :], scalar1=1.0, scalar2=None, op0=ALU.add)
                for c in range(2 - l):
                    for hh in range(2):
                        pg = banks[2 + c * 2 + hh]
                        for k in range(8):
                            S.op("pe", "matmul", [adw.b, scb[c].b], [pg.b], out=pg.t[:, :], lhsT=scb[c].t[:, k, :],
                                 rhs=adw.t[:, k, 2 * D + hh * 512:2 * D + (hh + 1) * 512], start=(k == 0), stop=(k == 7))
                        S.op("dve", "tensor_tensor", [pg.b, gb_b.b], [gbc[l][c].b], out=gbc[l][c].t[:, hh * 512:(hh + 1) * 512], in0=pg.t[:, :],
                             in1=gb_b.t[:, hh * 512:(hh + 1) * 512], op=ALU.add)
            wa_, wuq_, wukv_ = G["wa"], G["wuq"], G["wukv"]
            for j0 in range(0, NWA, 1112):
                S.dma("pool", wa_.t[:, :, j0:j0 + 1112], G["wA"][:, j0:j0 + 1112].rearrange("(k p) n -> p k n", p=128), writes=[wa_.b])
            S.dma("pool", wuq_.t[:], G["w_uq"].rearrange("(k p) n -> p k n", p=128), writes=[wuq_.b])
            S.dma("pool", wukv_.t[:], G["w_ukv"].rearrange("(k p) n -> p k n", p=128), writes=[wukv_.b])
            S.barrier()
            S.emit()

        if stop_after >= 1:
            layer0(P, G)
        if stop_after >= 4:
            layer1(P, G)
        S.emit(last=True)
    return nc


def load_xm_tile(P, G, L, src, B_src, t, xb, xm, ts_r, cond):
    S = P.S
    ident, mod = G["ident"], G["mod"]
    W = 512 if t < 16 else 256
    ns = W // 128
    for fp in range(4):
        tsv, tsb = ts_r.next()
        for hf in range(2):
            fc = fp * 2 + hf
            for s in range(ns):
                S.op("pe", "transpose", [xb.b, ident.b], [tsb], out=tsv[:, hf * 512 + s * 128:hf * 512 + (s + 1) * 128],
                     in_=xb.t[:, s, fc * 128:(fc + 1) * 128], identity=ident.t[:])
        fc = fp * 2
        S.op("act", "activation", [tsb, mod.b], [xm.b], out=xm.t[:, fc, 0:W], in_=tsv[:, 0:W], func=AF.Identity,
             bias=mod.t[:, L, fc, cond:cond + 1], scale=mod.t[:, L, 8 + fc, cond:cond + 1])
        fc = fp * 2 + 1
        S.op("dve", "tensor_scalar", [tsb, mod.b], [xm.b], out=xm.t[:, fc, 0:W], in0=tsv[:, 512:512 + W],
             scalar1=mod.t[:, L, 8 + fc, cond:cond + 1], scalar2=mod.t[:, L, fc, cond:cond + 1], op0=ALU.mult, op1=ALU.add)


def layer0(P, G):
    S = P.S
    xk, tabA, tabB, wA, w_uq, w_ukv, qg, kvg, lamv, subg, e_w_out = (G[k] for k in
        "xk tabA tabB wA w_uq w_ukv qg kvg lamv subg e_w_out".split())
    ident, ones, banks, mod = (G[k] for k in "ident ones banks mod".split())
    L = 0
    KTm = P.dscr("KTm", [8, 96, T], BF16)
    Vm = P.dscr("Vm", [8, 128, NKC, 64], BF16)
    QTm = P.dscr("QTm", [8, 96, T], BF16)
    KTd = P.dscr("KTd", [4, 128, T], BF16)
    Vd = P.dscr("Vd", [4, 2, 128, NKC, 64], BF16)
    QTd = P.dscr("QTd", [4, 128, T], BF16)
    SG = P.dscr("SG", [8, 128, T], BF16)
    B_KTm = [Buf(f"KTm{h}") for h in range(8)]
    B_Vm = Buf("Vm")
    B_QTm = [Buf(f"QTm{h}") for h in range(8)]
    B_KTd = [Buf(f"KTd{h}") for h in range(4)]
    B_Vd = Buf("Vd")
    B_QTd = [Buf(f"QTd{h}") for h in range(4)]
    B_SG = Buf("SG")

    with ExitStack() as ph:
        wa, wuq, wukv = G["wa"], G["wuq"], G["wukv"]
        qgc = P.sb(ph, "qgc", [128, 3], F32)
        kvgc = P.sb(ph, "kvgc", [128, 2], F32)
        epsc = P.sb(ph, "epsc", [128, 1], F32)
        S.op("dve", "memset", [], [epsc.b], ap=epsc.t[:], constant=RMS_EPS)
        S.dma("sp", qgc.t[:], qg.rearrange("(k p) -> p k", p=128), writes=[qgc.b], allow_slow_non_contiguous=True)
        S.dma("sp", kvgc.t[:], kvg.rearrange("(k p) -> p k", p=128), writes=[kvgc.b], allow_slow_non_contiguous=True)

        xb_r = Ring([P.sb(ph, f"xb{i}", [128, 4, D], BF16) for i in range(2)])
        xm_r = Ring([P.sb(ph, f"xm{i}", [128, 8, 512], BF16) for i in range(2)])
        tA_r = Ring([P.sb(ph, f"tA{i}", [128, 2, 512], F32) for i in range(2)])
        tB_r = Ring([P.sb(ph, f"tB{i}", [128, 2, 512], F32) for i in range(2)])
        ckvn_r = Ring([P.sb(ph, f"ckvn{i}", [128, 2, 512], BF16) for i in range(2)])
        cqn_r = Ring([P.sb(ph, f"cqn{i}", [128, 3, 512], BF16) for i in range(2)])
        sq_r = Ring([P.sb(ph, f"sq{i}", [128, 512], BF16) for i in range(3)])
        rs_r = Ring([P.sb(ph, f"rs{i}", [128, 512], F32) for i in range(2)])
        f1_r = Ring([P.sb(ph, f"f1_{i}", [128, 512], F32) for i in range(3)])
        f2_r = Ring([P.sb(ph, f"f2_{i}", [128, 512], F32) for i in range(3)])
        ob_r = Ring([P.sb(ph, f"ob{i}", [128, 512], BF16) for i in range(6)])
        pbr = Ring(banks[2:8])
        ts_r = Ring([(banks[bi].t[:].bitcast(BF16), banks[bi].b) for bi in range(2)])

        def mm_group(pb, M, W, lhs, rhs, rd):
            nk = len(lhs)
            for k in range(nk):
                S.op("pe", "matmul", rd, [pb.b], out=pb.t[0:M, 0:W], lhsT=lhs[k], rhs=rhs[k], start=(k == 0), stop=(k == nk - 1))

        def rstd_bc(src_tiles, W, nfeat):
            sqs = []
            for pbs in src_tiles:
                sq = sq_r.next()
                S.op("act", "activation", [pbs.b], [sq.b], out=sq.t[:, 0:W], in_=pbs.t[:, 0:W], func=AF.Square)
                sqs.append(sq)
            pss = pbr.next()
            for i, sq in enumerate(sqs):
                S.op("pe", "matmul", [ones.b, sq.b], [pss.b], out=pss.t[:, 0:W], lhsT=ones.t[:, :], rhs=sq.t[:, 0:W],
                     start=(i == 0), stop=(i == len(sqs) - 1))
            rs = rs_r.next()
            S.op("act", "activation", [pss.b, epsc.b], [rs.b], out=rs.t[:, 0:W], in_=pss.t[:, 0:W], func=AF.Ln, scale=1.0 / nfeat, bias=epsc.t[:, 0:1])
            S.op("act", "activation", [rs.b], [rs.b], out=rs.t[:, 0:W], in_=rs.t[:, 0:W], func=AF.Exp, scale=-0.5)
            return rs

        def rope(pm, pr, tab, r0, r1, W, outap, outb):
            f1 = f1_r.next()
            f2 = f2_r.next()
            S.op("dve", "tensor_tensor", [pm.b, tab.b], [f1.b], out=f1.t[r0:r1, 0:W], in0=pm.t[r0:r1, 0:W], in1=tab.t[r0:r1, 0, 0:W], op=ALU.mult)
            S.op("dve", "tensor_tensor", [pr.b, tab.b], [f2.b], out=f2.t[r0:r1, 0:W], in0=pr.t[r0:r1, 0:W], in1=tab.t[r0:r1, 1, 0:W], op=ALU.mult)
            S.op("pool", "tensor_tensor", [f1.b, f2.b], [outb], out=outap, in0=f1.t[r0:r1, 0:W], in1=f2.t[r0:r1, 0:W], op=ALU.add)

        def loads(t):
            W = 512 if t < 16 else 256
            ns = W // 128
            t0 = t * 512
            xb = xb_r.next()
            S.dma("pool", xb.t[:, 0:ns, :], xk[t0:t0 + W, :].rearrange("(s p) f -> p s f", p=128), writes=[xb.b])
            tA = tA_r.next()
            tB = tB_r.next()
            S.dma("sp", tA.t[64:96, :, 0:W], tabA[:, :, t0:t0 + W].rearrange("c p w -> p c w"), writes=[tA.b])
            for hh in range(2):
                S.dma("sp", tB.t[hh * 64:(hh + 1) * 64, :, 0:W], tabB[:, :, t0:t0 + W].rearrange("c p w -> p c w"), writes=[tB.b])
            return xb, tA, tB

        nxt = loads(0)
        for t in range(NT):
            W = 512 if t < 16 else 256
            ns = W // 128
            t0 = t * 512
            cond = 1 if t == 16 else 0
            xb, tA, tB = nxt
            if t + 1 < NT:
                nxt = loads(t + 1)
            xm = xm_r.next()
            load_xm_tile(P, G, L, xk, None, t, xb, xm, ts_r, cond)

            def proj(pb, M, col0):
                mm_group(pb, M, W, [wa.t[:, k, col0:col0 + M] for k in range(8)], [xm.t[:, k, 0:W] for k in range(8)], [wa.b, xm.b])

            pc = [pbr.next() for _ in range(2)]
            for i in range(2):
                proj(pc[i], 128, O_CKV + i * 128)
            rs = rstd_bc(pc, W, 256)
            ckvn = ckvn_r.next()
            for i in range(2):
                S.op("dve", "scalar_tensor_tensor", [pc[i].b, kvgc.b, rs.b], [ckvn.b], out=ckvn.t[:, i, 0:W], in0=pc[i].t[:, 0:W],
                     scalar=kvgc.t[:, i:i + 1], in1=rs.t[:, 0:W], op0=ALU.mult, op1=ALU.mult)
            for j in range(4):
                pb = pbr.next()
                mm_group(pb, 128, W, [wukv.t[:, k, j * 128:(j + 1) * 128] for k in range(2)], [ckvn.t[:, k, 0:W] for k in range(2)], [wukv.b, ckvn.b])
                ob = ob_r.next()
                S.op("act", "activation", [pb.b], [ob.b], out=ob.t[:, 0:W], in_=pb.t[:, 0:W], func=AF.Identity)
                for hh in range(2):
                    S.dma("sp", KTm[2 * j + hh, 0:64, t0:t0 + W], ob.t[hh * 64:(hh + 1) * 64, 0:W], reads=[ob.b], writes=[B_KTm[2 * j + hh]])
            for s in range(ns):
                pb = pbr.next()
                mm_group(pb, 128, 512, [ckvn.t[:, k, s * 128:(s + 1) * 128] for k in range(2)], [wukv.t[:, k, 512:1024] for k in range(2)], [wukv.b, ckvn.b])
                ob = ob_r.next()
                S.op("dve", "tensor_copy", [pb.b], [ob.b], out=ob.t[:, :], in_=pb.t[:, :])
                S.dma("sp", Vm[:, :, t0 // 128 + s, :].rearrange("h p d -> p h d"), ob.t[:, :].rearrange("p (h d) -> p h d", h=8),
                      reads=[ob.b], writes=[B_Vm])
            pm = pbr.next()
            pr = pbr.next()
            proj(pm, 96, O_KR)
            proj(pr, 96, O_KRR)
            ob = ob_r.next()
            rope(pm, pr, tA, 64, 96, W, ob.t[64:96, 0:W], ob.b)
            for h in range(8):
                S.dma("sp", KTm[h, 64:96, t0:t0 + W], ob.t[64:96, 0:W], reads=[ob.b], writes=[B_KTm[h]])
            for h in range(4):
                pm = pbr.next()
                pr = pbr.next()
                proj(pm, 128, O_DK + h * 128)
                proj(pr, 128, O_DKR + h * 128)
                ob = ob_r.next()
                rope(pm, pr, tB, 0, 128, W, ob.t[:, 0:W], ob.b)
                S.dma("sp", KTd[h, :, t0:t0 + W], ob.t[:, 0:W], reads=[ob.b], writes=[B_KTd[h]])
            for s in range(ns):
                pb = pbr.next()
                mm_group(pb, 128, 512, [xm.t[:, k, s * 128:(s + 1) * 128] for k in range(8)], [wa.t[:, k, O_DV:O_DV + 512] for k in range(8)], [wa.b, xm.b])
                ob = ob_r.next()
                S.op("act", "activation", [pb.b], [ob.b], out=ob.t[:, :], in_=pb.t[:, :], func=AF.Identity)
                S.dma("sp", Vd[:, :, :, t0 // 128 + s, :].rearrange("h j p d -> p (h j) d"), ob.t[:, :].rearrange("p (g d) -> p g d", g=8),
                      reads=[ob.b], writes=[B_Vd])
            pq = [pbr.next() for _ in range(3)]
            for i in range(3):
                proj(pq[i], 128, O_CQ + i * 128)
            rs = rstd_bc(pq, W, 384)
            cqn = cqn_r.next()
            for i in range(3):
                S.op("dve", "scalar_tensor_tensor", [pq[i].b, qgc.b, rs.b], [cqn.b], out=cqn.t[:, i, 0:W], in0=pq[i].t[:, 0:W],
                     scalar=qgc.t[:, i:i + 1], in1=rs.t[:, 0:W], op0=ALU.mult, op1=ALU.mult)
            for h in range(8):
                pm = pbr.next()
                pr = pbr.next()
                mm_group(pm, 96, W, [wuq.t[:, k, h * 96:(h + 1) * 96] for k in range(3)], [cqn.t[:, k, 0:W] for k in range(3)], [wuq.b, cqn.b])
                mm_group(pr, 96, W, [wuq.t[:, k, 768 + h * 96:768 + (h + 1) * 96] for k in range(3)], [cqn.t[:, k, 0:W] for k in range(3)], [wuq.b, cqn.b])
                ob = ob_r.next()
                S.op("act", "activation", [pm.b], [ob.b], out=ob.t[0:64, 0:W], in_=pm.t[0:64, 0:W], func=AF.Identity)
                rope(pm, pr, tA, 64, 96, W, ob.t[64:96, 0:W], ob.b)
                S.dma("sp", QTm[h, :, t0:t0 + W], ob.t[0:96, 0:W], reads=[ob.b], writes=[B_QTm[h]])
            for h in range(4):
                pm = pbr.next()
                pr = pbr.next()
                proj(pm, 128, O_DQ + h * 128)
                proj(pr, 128, O_DQR + h * 128)
                ob = ob_r.next()
                rope(pm, pr, tB, 0, 128, W, ob.t[:, 0:W], ob.b)
                S.dma("sp", QTd[h, :, t0:t0 + W], ob.t[:, 0:W], reads=[ob.b], writes=[B_QTd[h]])
            for c in range(8):
                pb = pbr.next()
                proj(pb, 128, O_GATE + c * 128)
                ob = ob_r.next()
                S.op("act", "activation", [pb.b], [ob.b], out=ob.t[:, 0:W], in_=pb.t[:, 0:W], func=AF.Silu)
                S.dma("sp", SG[c, :, t0:t0 + W], ob.t[:, 0:W], reads=[ob.b], writes=[B_SG])
        S.barrier()
        S.emit()
    G["wst"].close()
    if P.stop_after == 1:
        return

    qtiles = [(i * 512, 512, 0) for i in range(16)] + [(SEQ, 256, 64)]
    OB = P.dscr("OB", [8, 128, T], BF16)
    B_OB = Buf("OB")
    with ExitStack() as ph:
        KT2 = [P.sb(ph, f"KT{i}", [128, T], BF16) for i in range(2)]
        VA2 = [P.sb(ph, f"VA{i}", [128, NKC, 2, 128], BF16) for i in range(2)]
        qm_r = Ring([P.sb(ph, f"qm{i}", [128, 512], BF16) for i in range(3)])
        qd_r = Ring([(P.sb(ph, f"qa{i}", [128, 512], BF16), P.sb(ph, f"qb{i}", [128, 512], BF16)) for i in range(3)])
        for (qa, qb) in qd_r.items:
            S.op("pool", "memset", [], [qa.b], ap=qa.t[:], constant=0.0)
            S.op("pool", "memset", [], [qb.b], ap=qb.t[:], constant=0.0)
        pt_r = Ring([P.sb(ph, f"pt{i}", [128, 512], BF16) for i in range(14)])
        tmp_r = Ring([P.sb(ph, f"ptsum{i}", [128, 512], BF16) for i in range(6)])
        rl_r = Ring([P.sb(ph, f"rl{i}", [128, 512], F32) for i in range(3)])
        onesf = P.sb(ph, "onesf", [128, 128], F32)
        S.op("pool", "memset", [], [onesf.b], ap=onesf.t[:], constant=1.0)
        ot_r = Ring([P.sb(ph, f"ot{i}", [64, 512], BF16) for i in range(6)])
        ev_r = Ring([[P.sb(ph, f"ev{i}_{k}", [128, 512], F32) for k in range(4)] for i in range(1)])
        dd_r = Ring([P.sb(ph, f"dd{i}", [128, 512], F32) for i in range(2)])
        sq2_r = Ring([P.sb(ph, f"sqd{i}", [128, 512], BF16) for i in range(2)])
        rs2_r = Ring([P.sb(ph, f"rs2{i}", [128, 512], F32) for i in range(2)])
        otd_r = Ring([P.sb(ph, f"otd{i}", [128, 512], BF16) for i in range(2)])
        lam4 = P.sb(ph, "lam4", [128, 4, 64], F32)
        lamp = P.sb(ph, "lamp", [128, 2, 64], F32)
        lams = P.sb(ph, "lams", [128, 2], F32)
        neglam = P.sb(ph, "neglam", [128, 1], F32)
        sgc = P.sb(ph, "sgc", [128, 1], F32)
        eps2 = P.sb(ph, "eps2", [128, 1], F32)
        S.op("dve", "memset", [], [eps2.b], ap=eps2.t[:], constant=RMS_EPS)
        for i in range(2):
            S.op("pool", "memset", [], [VA2[i].b], ap=VA2[i].t[:, :, :, 64:128], constant=1.0)
        S.dma("sp", lam4.t[:].rearrange("p a d -> p (a d)"), lamv.rearrange("(o a) d -> o (a d)", o=1).broadcast_to([128, 256]), writes=[lam4.b])
        S.op("dve", "tensor_tensor", [lam4.b], [lamp.b], out=lamp.t[:], in0=lam4.t[:, 0:4:2, :], in1=lam4.t[:, 1:4:2, :], op=ALU.mult)
        S.op("dve", "tensor_reduce", [lamp.b], [lams.b], out=lams.t[:], in_=lamp.t[:], axis=AX.X, op=ALU.add)
        S.op("act", "activation", [lams.b], [lams.b], out=lams.t[:], in_=lams.t[:], func=AF.Exp)
        S.op("dve", "tensor_tensor", [lams.b], [neglam.b], out=neglam.t[:], in0=lams.t[:, 1:2], in1=lams.t[:, 0:1], op=ALU.subtract)
        S.op("dve", "tensor_scalar", [neglam.b], [neglam.b], out=neglam.t[:], in0=neglam.t[:], scalar1=-LAMBDA_INIT0, scalar2=None, op0=ALU.add)
        S.dma("sp", sgc.t[:], subg.rearrange("(p o) -> p o", o=1), writes=[sgc.b], allow_slow_non_contiguous=True)
        S.op("dve", "tensor_scalar", [sgc.b], [sgc.b], out=sgc.t[:], in0=sgc.t[:], scalar1=1.0 - LAMBDA_INIT0, scalar2=None, op0=ALU.mult)

        def load_head(hh, slot):
            KT, VA = KT2[slot], VA2[slot]
            if hh < 8:
                S.dma("sp", KT.t[0:96, :], KTm[hh], reads=[B_KTm[hh]], writes=[KT.b])
                for c0 in range(0, NKC, 11):
                    S.dma("sp", VA.t[:, c0:c0 + 11, 0, 0:64], Vm[hh, :, c0:c0 + 11, :], reads=[B_Vm], writes=[VA.b])
            else:
                h = hh - 8
                S.dma("sp", KT.t[:, :], KTd[h], reads=[B_KTd[h]], writes=[KT.b])
                VDv = VA.t[:].rearrange("p c j d -> p (c j) d")
                for j in range(2):
                    for c0 in range(0, NKC, 11):
                        S.dma("sp", VDv[:, c0:c0 + 11, j * 64:(j + 1) * 64], Vd[h, j, :, c0:c0 + 11, :], reads=[B_Vd], writes=[VA.b])

        qjobs = {}

        def load_q(hh, qi):
            if hh >= 12:
                return
            gq, W, kc0 = qtiles[qi]
            if hh < 8:
                qm = qm_r.next()
                S.dma("sp", qm.t[0:96, 0:W], QTm[hh, :, gq:gq + W], reads=[B_QTm[hh]], writes=[qm.b])
                qjobs[(hh, qi)] = (qm, qm)
            else:
                h = hh - 8
                qa, qb = qd_r.next()
                S.dma("sp", qa.t[0:64, 0:W], QTd[h, 0:64, gq:gq + W], reads=[B_QTd[h]], writes=[qa.b])
                S.dma("sp", qb.t[64:128, 0:W], QTd[h, 64:128, gq:gq + W], reads=[B_QTd[h]], writes=[qb.b])
                qjobs[(hh, qi)] = (qa, qb)

        LOOK = 4
        load_head(0, 0)
        load_q(0, 0)
        for hh in range(12):
            slot = hh % 2
            KT, VA = KT2[slot], VA2[slot]
            if hh + 1 < 12:
                load_head(hh + 1, 1 - slot)
            mla = hh < 8
            nmap, nv = (1, 1) if mla else (2, 1)
            VDv = VA.t[:].rearrange("p c j d -> p (c j) d")
            scale = MLA_SCALE if mla else DIFF_SCALE
            sc_r = Ring(banks[0:4]) if mla else Ring(banks[0:3])
            accr = Ring(banks[4:8])
            units = []
            for qi, (gq, W, kc0) in enumerate(qtiles):
                accs = [[accr.next()]] if mla else [[banks[4], banks[6]], [banks[5], banks[7]]]
                for kc in range(kc0, NKC):
                    for m in range(nmap):
                        units.append((qi, kc, m, accs, kc == kc0, kc == NKC - 1, kc == NKC - 1 and m == nmap - 1))
            pts = {}
            pair_pend = {}
            lacc_started = {}
            deferred = []

            def post_mla(qi, accs):
                gq, W, kc0 = qtiles[qi]
                acc = accs[0][0]
                rl = rl_r.next()
                S.op("dve", "reciprocal", [acc.b], [rl.b], out=rl.t[0:64, 0:W], in_=acc.t[64:128, 0:W])
                ot = ot_r.next()
                S.op("dve", "tensor_tensor", [acc.b, rl.b], [ot.b], out=ot.t[:, 0:W], in0=acc.t[0:64, 0:W], in1=rl.t[0:64, 0:W], op=ALU.mult)
                oc, op0 = hh // 2, (hh % 2) * 64
                S.dma("pool", OB[oc, op0:op0 + 64, gq:gq + W], ot.t[:, 0:W], reads=[ot.b], writes=[B_OB])

            def post_diff_stages(qi, accs):
                gq, W, kc0 = qtiles[qi]
                h = hh - 8
                ev = ev_r.next()
                aux = banks[3]
                rls = [rl_r.next(), rl_r.next()]
                dd = dd_r.next()
                sq = sq2_r.next()
                rs2 = rs2_r.next()
                ot = otd_r.next()

                def st_evac():
                    S.op("dve", "tensor_copy", [accs[0][0].b], [ev[0].b], out=ev[0].t[:, 0:W], in_=accs[0][0].t[:, 0:W])
                    S.op("act", "activation", [accs[1][0].b], [ev[1].b], out=ev[1].t[:, 0:W], in_=accs[1][0].t[:, 0:W], func=AF.Identity)
                    S.op("act", "activation", [accs[0][1].b], [ev[2].b], out=ev[2].t[:, 0:W], in_=accs[0][1].t[:, 0:W], func=AF.Identity)
                    S.op("dve", "tensor_copy", [accs[1][1].b], [ev[3].b], out=ev[3].t[:, 0:W], in_=accs[1][1].t[:, 0:W])

                def st_lbc(m):
                    S.op("pe", "matmul", [onesf.b, ev[2 + m].b], [aux.b], out=aux.t[:, 0:W], lhsT=onesf.t[:, :], rhs=ev[2 + m].t[:, 0:W], start=True, stop=True)

                def st_recip(m):
                    S.op("act", "activation", [aux.b], [rls[m].b], out=rls[m].t[:, 0:W], in_=aux.t[:, 0:W], func=AF.Ln)
                    S.op("act", "activation", [rls[m].b], [rls[m].b], out=rls[m].t[:, 0:W], in_=rls[m].t[:, 0:W], func=AF.Exp, scale=-1.0)

                def st_norm():
                    S.op("dve", "tensor_tensor", [ev[0].b, rls[0].b], [ev[0].b], out=ev[0].t[:, 0:W], in0=ev[0].t[:, 0:W], in1=rls[0].t[:, 0:W], op=ALU.mult)
                    S.op("pool", "tensor_tensor", [ev[1].b, rls[1].b], [ev[1].b], out=ev[1].t[:, 0:W], in0=ev[1].t[:, 0:W], in1=rls[1].t[:, 0:W], op=ALU.mult)

                def st_diff():
                    S.op("dve", "scalar_tensor_tensor", [ev[0].b, ev[1].b, neglam.b], [dd.b], out=dd.t[:, 0:W], in0=ev[1].t[:, 0:W], scalar=neglam.t[:, 0:1],
                         in1=ev[0].t[:, 0:W], op0=ALU.mult, op1=ALU.add)
                    S.op("pool", "tensor_tensor", [dd.b], [sq.b], out=sq.t[:, 0:W], in0=dd.t[:, 0:W], in1=dd.t[:, 0:W], op=ALU.mult)

                def st_ss():
                    S.op("pe", "matmul", [ones.b, sq.b], [aux.b], out=aux.t[:, 0:W], lhsT=ones.t[:, :], rhs=sq.t[:, 0:W], start=True, stop=True)

                def st_rs():
                    S.op("act", "activation", [aux.b, eps2.b], [rs2.b], out=rs2.t[:, 0:W], in_=aux.t[:, 0:W], func=AF.Ln, scale=1.0 / 128, bias=eps2.t[:, 0:1])
                    S.op("act", "activation", [rs2.b], [rs2.b], out=rs2.t[:, 0:W], in_=rs2.t[:, 0:W], func=AF.Exp, scale=-0.5)

                def st_out():
                    S.op("dve", "scalar_tensor_tensor", [dd.b, sgc.b, rs2.b], [ot.b], out=ot.t[:, 0:W],
                         in0=dd.t[:, 0:W], scalar=sgc.t[:, 0:1], in1=rs2.t[:, 0:W], op0=ALU.mult, op1=ALU.mult)
                    S.dma("pool", OB[4 + h, :, gq:gq + W], ot.t[:, 0:W], reads=[ot.b], writes=[B_OB])

                return [(0, st_evac), (2, lambda: st_lbc(0)), (3, lambda: st_recip(0)), (2, lambda: st_lbc(1)), (3, lambda: st_recip(1)),
                        (3, st_norm), (4, st_diff), (4, st_ss), (3, st_rs), (3, st_out)]

            nun = len(units)
            for i in range(nun + LOOK):
                if i < nun:
                    qi, kc, m, accs, first, last, qlast = units[i]
                    gq, W, kc0 = qtiles[qi]
                    if first and m == 0:
                        nq = (hh, qi + 1) if qi + 1 < len(qtiles) else (hh + 1, 0)
                        load_q(*nq)
                    qt = qjobs[(hh, qi)][m]
                    r1 = 96 if mla else 128
                    sb_ = sc_r.next()
                    S.op("pe", "matmul", [KT.b, qt.b], [sb_.b], out=sb_.t[:, 0:W], lhsT=KT.t[0:r1, kc * 128:(kc + 1) * 128],
                         rhs=qt.t[0:r1, 0:W], start=True, stop=True)
                    pt = pt_r.next()
                    S.op("act", "activation", [sb_.b], [pt.b], out=pt.t[:, 0:W], in_=sb_.t[:, 0:W], func=AF.Exp, scale=scale)
                    pts[i] = pt
                j_ = i - LOOK
                if j_ >= 0:
                    qi, kc, m, accs, first, last, qlast = units[j_]
                    gq, W, kc0 = qtiles[qi]
                    pt = pts.pop(j_)
                    acc = accs[m][0]
                    if mla:
                        S.op("pe", "matmul", [VA.b, pt.b], [acc.b], out=acc.t[:, 0:W], lhsT=VA.t[:, kc, 0, :], rhs=pt.t[:, 0:W], start=first, stop=last)
                    else:
                        S.op("pe", "matmul", [VA.b, pt.b], [acc.b], out=acc.t[:, 0:W], lhsT=VDv[:, kc, :], rhs=pt.t[:, 0:W], start=first, stop=last)
                        lacc = accs[m][1]
                        pend = pair_pend.setdefault(m, [])
                        pend.append(pt)
                        if len(pend) == 4 or last:
                            pair_pend[m] = []

                            def bsum(x, y):
                                t = tmp_r.next()
                                S.op("dve", "tensor_tensor", [x.b, y.b], [t.b], out=t.t[:, 0:W], in0=x.t[:, 0:W], in1=y.t[:, 0:W], op=ALU.add)
                                return t

                            src = pend[0]
                            if len(pend) >= 2:
                                src = bsum(pend[0], pend[1])
                            if len(pend) == 3:
                                src = bsum(src, pend[2])
                            elif len(pend) == 4:
                                src = bsum(src, bsum(pend[2], pend[3]))
                            if not lacc_started.get((qi, m)):
                                lacc_started[(qi, m)] = True
                                S.op("dve", "tensor_copy", [src.b], [lacc.b], out=lacc.t[:, 0:W], in_=src.t[:, 0:W])
                            else:
                                S.op("dve", "tensor_tensor", [lacc.b, src.b], [lacc.b], out=lacc.t[:, 0:W], in0=lacc.t[:, 0:W], in1=src.t[:, 0:W], op=ALU.add)
                    if qlast:
                        if mla:
                            post_mla(qi, accs)
                        else:
                            while deferred:
                                deferred.pop(0)[1]()
                            deferred.extend([list(x) for x in post_diff_stages(qi, accs)])
                    if deferred:
                        if deferred[0][0] <= 0:
                            deferred.pop(0)[1]()
                        else:
                            deferred[0][0] -= 1
            while deferred:
                deferred.pop(0)[1]()
        S.barrier()
        S.emit()

    with ExitStack() as p3:
        wo = P.sb(p3, "wo", [128, 8, D], BF16)
        S.dma("pool", wo.t[:], e_w_out.rearrange("(k p) n -> p k n", p=128), writes=[wo.b])
        out_proj_ln(P, G, L, qtiles, OB, B_OB, None, wo, SG, B_SG, xk, None, G["X1"], G["B_X1"], p3)
        S.barrier()
        S.emit()


def out_proj_ln(P, G, L, qtiles, OB, B_OB, HG, wo, SG, B_SG, xres, B_res, xout, B_out, st):
    S = P.S
    banks, gbc, lng, lnb = G["banks"], G["gbc"], G["lng"], G["lnb"]
    sg_r = Ring([P.sb(st, f"sg{i}", [128, 8, 512], BF16) for i in range(3)])
    og_r = Ring([P.sb(st, f"og{i}", [128, 8, 512], BF16) for i in range(2)]) if OB is not None else None
    ob_r = Ring([P.sb(st, f"obt{i}", [128, 8, 512], BF16) for i in range(3)]) if OB is not None else None
    xr_r = Ring([P.sb(st, f"xr{i}", [128, D], F32) for i in range(2)])
    v_r = Ring([P.sb(st, f"vv{i}", [128, D], F32) for i in range(3)])
    o_r = Ring([P.sb(st, f"oo{i}", [128, D], F32) for i in range(2)])
    st_r = Ring([P.sb(st, f"bst{i}", [128, 2, 6], F32) for i in range(2)])
    mv_r = Ring([P.sb(st, f"mv{i}", [128, 4], F32) for i in range(3)])
    epsl = P.sb(st, "epsl", [128, 1], F32)
    S.op("dve", "memset", [], [epsl.b], ap=epsl.t[:], constant=LN_EPS)
    yb = Ring(banks[0:8])
    pend = None

    def ln_apply(v, mv, r0):
        o = o_r.next()
        S.op("dve", "scalar_tensor_tensor", [v.b, mv.b, lng[L].b], [o.b], out=o.t[:], in0=v.t[:], scalar=mv.t[:, 0:1], in1=lng[L].t[:],
             op0=ALU.subtract, op1=ALU.mult)
        S.op("dve", "scalar_tensor_tensor", [o.b, mv.b, lnb[L].b], [o.b], out=o.t[:], in0=o.t[:], scalar=mv.t[:, 2:3], in1=lnb[L].t[:],
             op0=ALU.mult, op1=ALU.add)
        S.dma("pool", xout[r0:r0 + 128, :], o.t[:], reads=[o.b], writes=[B_out], final=True)

    def prep(qi):
        gq, W, kc0 = qtiles[qi]
        sg = sg_r.next()
        if OB is not None:
            S.dma("sp", sg.t[:, :, 0:W], SG[:, :, gq:gq + W].rearrange("c p w -> p c w"), reads=[B_SG], writes=[sg.b])
            obt = ob_r.next()
            S.dma("sp", obt.t[:, :, 0:W], OB[:, :, gq:gq + W].rearrange("c p w -> p c w"), reads=[B_OB], writes=[obt.b])
            return (sg, obt)
        S.dma("sp", sg.t[:, :, 0:W], HG[:, :, gq:gq + W].rearrange("c p w -> p c w"), reads=[B_SG], writes=[sg.b])
        return (sg, None)

    def make_og(qi):
        gq, W, kc0 = qtiles[qi]
        sg, obt = loaded.pop(qi)
        if obt is None:
            return sg
        og = og_r.next()
        S.op("dve", "tensor_tensor", [obt.b, sg.b], [og.b], out=og.t[:, :, 0:W], in0=obt.t[:, :, 0:W], in1=sg.t[:, :, 0:W], op=ALU.mult)
        return og

    nq = len(qtiles)
    loaded = {qi: prep(qi) for qi in range(min(2, nq))}
    ogs = {0: make_og(0)}
    for qi, (gq, W, kc0) in enumerate(qtiles):
        cond = 1 if gq >= SEQ else 0
        if qi + 2 < nq:
            loaded[qi + 2] = prep(qi + 2)
        if qi + 1 < nq:
            ogs[qi + 1] = make_og(qi + 1)
        og = ogs.pop(qi)
        for s in range(W // 128):
            r0 = gq + s * 128
            xr = xr_r.next()
            S.dma("sp", xr.t[:], xres[r0:r0 + 128, :], reads=([B_res] if B_res is not None else []), writes=[xr.b])
            ys = [yb.next() for _ in range(2)]
            for hh in range(2):
                for k in range(8):
                    S.op("pe", "matmul", [og.b, wo.b], [ys[hh].b], out=ys[hh].t[:, :], lhsT=og.t[:, k, s * 128:(s + 1) * 128],
                         rhs=wo.t[:, k, hh * 512:(hh + 1) * 512], start=(k == 0), stop=(k == 7))
            v = v_r.next()
            for hh in range(2):
                S.op("dve", "tensor_tensor", [ys[hh].b, gbc[L][cond].b], [v.b], out=v.t[:, hh * 512:(hh + 1) * 512], in0=ys[hh].t[:, :],
                     in1=gbc[L][cond].t[:, hh * 512:(hh + 1) * 512], op=ALU.mult)
            S.op("dve", "scalar_tensor_tensor", [xr.b, v.b], [v.b], out=v.t[:], in0=xr.t[:], scalar=ALPHA, in1=v.t[:], op0=ALU.mult, op1=ALU.add)
            bst = st_r.next()
            for hh in range(2):
                S.op("dve", "bn_stats", [v.b], [bst.b], out=bst.t[:, hh, :], in_=v.t[:, hh * 512:(hh + 1) * 512])
            mv = mv_r.next()
            S.op("dve", "bn_aggr", [bst.b], [mv.b], out=mv.t[:, 0:2], in_=bst.t[:].rearrange("p a b -> p (a b)"))
            S.op("act", "activation", [mv.b, epsl.b], [mv.b], out=mv.t[:, 2:3], in_=mv.t[:, 1:2], func=AF.Ln, bias=epsl.t[:, 0:1])
            S.op("act", "activation", [mv.b], [mv.b], out=mv.t[:, 2:3], in_=mv.t[:, 2:3], func=AF.Exp, scale=-0.5)
            if pend is not None:
                ln_apply(*pend)
            pend = (v, mv, r0)
    if pend is not None:
        ln_apply(*pend)


BLK = 1024


def layer1(P, G):
    S = P.S
    X1, B_X1, banks, mod = G["X1"], G["B_X1"], G["banks"], G["mod"]
    o_w_in, o_conv_w, o_conv_b, o_gw, o_gb, o_lam, o_w_out = (G[k] for k in "o_w_in o_conv_w o_conv_b o_gw o_gb o_lam o_w_out".split())
    L = 1
    UX = P.dscr("UX", [8, 128, T], F32)
    SGL = P.dscr("SGL", [8, 128, SEQ], BF16)
    HG = P.dscr("HG", [8, 128, SEQ], BF16)
    B_UX = [Buf(f"UX{c}") for c in range(8)]
    B_SGL = [Buf(f"SGL{c}") for c in range(8)]
    B_HG = Buf("HG")

    with ExitStack() as ph:
        w1 = P.sb(ph, "w1", [128, 8, 2 * D], BF16)
        for j0 in range(0, 2 * D, 1024):
            S.dma("pool", w1.t[:, :, j0:j0 + 1024], o_w_in[:, j0:j0 + 1024].rearrange("(k p) n -> p k n", p=128), writes=[w1.b])
        xb_r = Ring([P.sb(ph, f"l1xb{i}", [128, 4, D], BF16) for i in range(2)])
        xm_r = Ring([P.sb(ph, f"l1xm{i}", [128, 8, 512], BF16) for i in range(2)])
        uf_r = Ring([P.sb(ph, f"l1uf{i}", [128, 512], F32) for i in range(4)])
        gb_r = Ring([P.sb(ph, f"l1gb{i}", [128, 512], BF16) for i in range(4)])
        pbr = Ring(banks[2:8])
        ts_r = Ring([(banks[bi].t[:].bitcast(BF16), banks[bi].b) for bi in range(2)])

        def loads(t):
            W = 512 if t < 16 else 256
            xb = xb_r.next()
            S.dma("pool", xb.t[:, 0:W // 128, :], X1[t * 512:t * 512 + W, :].rearrange("(s p) f -> p s f", p=128), reads=[B_X1], writes=[xb.b])
            return xb

        nxt = loads(0)
        for t in range(NT):
            W = 512 if t < 16 else 256
            t0 = t * 512
            xb = nxt
            if t + 1 < NT:
                nxt = loads(t + 1)
            xm = xm_r.next()
            load_xm_tile(P, G, L, X1, B_X1, t, xb, xm, ts_r, 1 if t == 16 else 0)
            for m in range(16 if t < 16 else 8):
                pb = pbr.next()
                for k in range(8):
                    S.op("pe", "matmul", [w1.b, xm.b], [pb.b], out=pb.t[:, 0:W], lhsT=w1.t[:, k, m * 128:(m + 1) * 128], rhs=xm.t[:, k, 0:W],
                         start=(k == 0), stop=(k == 7))
                if m < 8:
                    uf = uf_r.next()
                    S.op("dve", "tensor_copy", [pb.b], [uf.b], out=uf.t[:, 0:W], in_=pb.t[:, 0:W])
                    S.dma("sp", UX[m, :, t0:t0 + W], uf.t[:, 0:W], reads=[uf.b], writes=[B_UX[m]])
                else:
                    gb = gb_r.next()
                    S.op("act", "activation", [pb.b], [gb.b], out=gb.t[:, 0:W], in_=pb.t[:, 0:W], func=AF.Silu)
                    S.dma("sp", SGL[m - 8, :, t0:t0 + W], gb.t[:, 0:W], reads=[gb.b], writes=[B_SGL[m - 8]])
        S.barrier()
        S.emit()

    with ExitStack() as ph:
        gw = P.sb(ph, "gw", [128, 32, 128], BF16)
        gbias = P.sb(ph, "gbias", [128, 32], F32)
        lamc = P.sb(ph, "lamc", [128, 16], F32)
        sc8 = P.sb(ph, "sc8", [128, 16], F32)
        sc16 = P.sb(ph, "sc16", [128, 16], F32)
        cw = P.sb(ph, "cw", [128, 4, 8], F32)
        cb = P.sb(ph, "cb", [128, 8], F32)
        one1 = P.sb(ph, "one1", [128, 1], F32)
        S.op("dve", "memset", [], [one1.b], ap=one1.t[:], constant=1.0)
        S.dma("pool", gw.t[:], o_gw.rearrange("g d k i j -> i (g d k) j"), writes=[gw.b])
        S.dma("sp", gbias.t[:], o_gb.rearrange("g d k j -> j (g d k)"), writes=[gbias.b], allow_slow_non_contiguous=True)
        S.dma("sp", lamc.t[:], o_lam.rearrange("d (k p) -> p (d k)", p=128), writes=[lamc.b], allow_slow_non_contiguous=True)
        S.dma("sp", cw.t[:], o_conv_w.rearrange("t (k p) -> p t k", p=128), writes=[cw.b], allow_slow_non_contiguous=True)
        S.dma("sp", cb.t[:], o_conv_b.rearrange("(k p) -> p k", p=128), writes=[cb.b], allow_slow_non_contiguous=True)
        S.op("act", "activation", [lamc.b], [lamc.b], out=lamc.t[:], in_=lamc.t[:], func=AF.Exp, scale=-1.0)
        S.op("act", "activation", [lamc.b, one1.b], [lamc.b], out=lamc.t[:], in_=lamc.t[:], func=AF.Ln, bias=one1.t[:, 0:1])
        h8, h16 = sc8, sc16
        S.op("dve", "tensor_scalar", [lamc.b], [h8.b], out=h8.t[:], in0=lamc.t[:], scalar1=-4.0, scalar2=None, op0=ALU.mult)
        S.op("dve", "tensor_scalar", [lamc.b], [h16.b], out=h16.t[:], in0=lamc.t[:], scalar1=-8.0, scalar2=None, op0=ALU.mult)
        hbias = P.sb(ph, "hbias", [128, 32], F32)
        S.op("dve", "tensor_scalar", [gbias.b], [hbias.b], out=hbias.t[:], in0=gbias.t[:], scalar1=0.5, scalar2=None, op0=ALU.mult)

        uh = P.sb(ph, "uh", [128, SEQ + 4], F32)
        uxc = P.sb(ph, "uxc", [128, CTX + 4], F32)
        ua = P.sb(ph, "ua", [128, T], F32)
        ubf = P.sb(ph, "ubf", [128, T], BF16)
        hc = [P.sb(ph, f"hc{d}", [128, CTX], F32) for d in range(2)]
        r_r = Ring([P.sb(ph, f"r{i}", [128, BLK], F32) for i in range(2)])
        i_r = Ring([P.sb(ph, f"i{i}", [128, BLK], F32) for i in range(4)])
        a_r = Ring([P.sb(ph, f"a{i}", [128, BLK], F32) for i in range(4)])
        s_r = Ring([P.sb(ph, f"s{i}", [128, BLK], F32) for i in range(4)])
        q25 = P.sb(ph, "q25", [128, 1], F32)
        S.op("dve", "memset", [], [q25.b], ap=q25.t[:], constant=0.25)
        g_r = Ring([P.sb(ph, f"g{i}", [128, BLK], F32) for i in range(2)])
        hb_r = Ring([P.sb(ph, f"hb{i}", [128, BLK], F32) for i in range(2)])
        sgl_r = Ring([P.sb(ph, f"sgl{i}", [128, BLK], BF16) for i in range(2)])
        hg_r = Ring([P.sb(ph, f"hg{i}", [128, BLK], BF16) for i in range(2)])
        gp_r = Ring(banks[0:8])
        S.op("pool", "memset", [], [uh.b], ap=uh.t[:], constant=0.0)
        S.op("pool", "memset", [], [uxc.b], ap=uxc.t[:], constant=0.0)

        for c in range(8):
            if c > 0:
                S.op("pool", "memset", [], [uh.b], ap=uh.t[:, 0:2], constant=0.0)
                S.op("pool", "memset", [], [uh.b], ap=uh.t[:, SEQ + 2:SEQ + 4], constant=0.0)
            S.dma("sp", uh.t[:, 2:2 + SEQ], UX[c, :, 0:SEQ], reads=[B_UX[c]], writes=[uh.b])
            S.dma("sp", uxc.t[:, 2:2 + CTX], UX[c, :, SEQ:T], reads=[B_UX[c]], writes=[uxc.b])
            for (src, n, o0) in ((uh, SEQ, 0), (uxc, CTX, SEQ)):
                S.op("dve", "tensor_scalar", [src.b, cw.b, cb.b], [ua.b], out=ua.t[:, o0:o0 + n], in0=src.t[:, 0:n], scalar1=cw.t[:, 0, c:c + 1],
                     scalar2=cb.t[:, c:c + 1], op0=ALU.mult, op1=ALU.add)
                for k in range(1, 4):
                    S.op("dve", "scalar_tensor_tensor", [src.b, cw.b, ua.b], [ua.b], out=ua.t[:, o0:o0 + n], in0=src.t[:, k:k + n],
                         scalar=cw.t[:, k, c:c + 1], in1=ua.t[:, o0:o0 + n], op0=ALU.mult, op1=ALU.add)
            S.op("act", "activation", [ua.b], [ubf.b], out=ubf.t[:], in_=ua.t[:], func=AF.Identity)
            for d in range(2):
                ia, ix = (0 * 2 + d) * 8 + c, (1 * 2 + d) * 8 + c
                dk = d * 8 + c
                lat = [(b * BLK, BLK) for b in range(SEQ // BLK)]
                blocks = [(SEQ, CTX, True)] + [(t0, n, False) for (t0, n) in (lat if d == 0 else lat[::-1])]
                st1 = {}
                chain = {"prev": None}

                def stage1(k):
                    t0, n, is_ctx = blocks[k]
                    r, it, a, s_ = r_r.next(), i_r.next(), a_r.next(), s_r.next()
                    for q0 in range(0, n, 512):
                        w = min(512, n - q0)
                        for (gi, dst) in ((ia, r), (ix, it)):
                            pb = gp_r.next()
                            S.op("pe", "matmul", [gw.b, ubf.b], [pb.b], out=pb.t[:, 0:w], lhsT=gw.t[:, gi, :], rhs=ubf.t[:, t0 + q0:t0 + q0 + w],
                                 start=True, stop=True)
                            S.op("act", "activation", [pb.b, hbias.b], [dst.b], out=dst.t[:, q0:q0 + w], in_=pb.t[:, 0:w], func=AF.Tanh,
                                 scale=0.5, bias=hbias.t[:, gi:gi + 1])
                    S.op("act", "activation", [r.b, h8.b], [a.b], out=a.t[:, 0:n], in_=r.t[:, 0:n], func=AF.Exp, scale=h8.t[:, dk:dk + 1], bias=h8.t[:, dk:dk + 1])
                    S.op("act", "activation", [r.b, h16.b], [s_.b], out=s_.t[:, 0:n], in_=r.t[:, 0:n], func=AF.Exp, scale=h16.t[:, dk:dk + 1], bias=h16.t[:, dk:dk + 1])
                    S.op("pool", "tensor_scalar", [s_.b], [s_.b], out=s_.t[:, 0:n], in0=s_.t[:, 0:n], scalar1=1.0, scalar2=0.0, op0=ALU.min, op1=ALU.max)
                    st1[k] = (it, a, s_)

                def stage2(k):
                    t0, n, is_ctx = blocks[k]
                    it, a, s_ = st1.pop(k)
                    g = g_r.next()
                    S.op("act", "activation", [s_.b, q25.b], [s_.b], out=s_.t[:, 0:n], in_=s_.t[:, 0:n], func=AF.Sqrt, scale=-0.25, bias=q25.t[:, 0:1])
                    S.op("dve", "scalar_tensor_tensor", [it.b, ua.b], [g.b], out=g.t[:, 0:n], in0=it.t[:, 0:n], scalar=1.0, in1=ua.t[:, t0:t0 + n],
                         op0=ALU.add, op1=ALU.mult)
                    S.op("dve", "tensor_tensor", [g.b, s_.b], [g.b], out=g.t[:, 0:n], in0=g.t[:, 0:n], in1=s_.t[:, 0:n], op=ALU.mult)
                    if is_ctx:
                        dst, dap, dbuf = hc[d], hc[d].t[:, 0:n], hc[d].b
                    elif d == 0:
                        dst, dap, dbuf = uh, uh.t[:, 2 + t0:2 + t0 + n], uh.b
                    else:
                        dst = hb_r.next()
                        dap, dbuf = dst.t[:, 0:n], dst.b
                    prev_init = chain["prev"]
                    init = 0.0 if prev_init is None else prev_init[0]
                    rd = [a.b, g.b] + ([] if prev_init is None else [prev_init[1]])
                    if d == 0:
                        S.op("dve", "tensor_tensor_scan", rd, [dbuf], out=dap, data0=a.t[:, 0:n], data1=g.t[:, 0:n], initial=init, op0=ALU.mult, op1=ALU.add)
                        chain["prev"] = (dap[:, n - 1:n], dbuf)
                    else:
                        S.op("dve", "tensor_tensor_scan", rd, [dbuf], out=dap[:, ::-1], data0=a.t[:, 0:n][:, ::-1], data1=g.t[:, 0:n][:, ::-1],
                             initial=init, op0=ALU.mult, op1=ALU.add)
                        chain["prev"] = (dap[:, 0:1], dbuf)
                    if d == 1 and not is_ctx:
                        sgl = sgl_r.next()
                        S.dma("sp", sgl.t[:, 0:n], SGL[c, :, t0:t0 + n], reads=[B_SGL[c]], writes=[sgl.b])
                        S.op("dve", "tensor_tensor", [dst.b, uh.b], [g.b], out=g.t[:, 0:n], in0=dst.t[:, 0:n], in1=uh.t[:, 2 + t0:2 + t0 + n], op=ALU.add)
                        hg = hg_r.next()
                        S.op("pool", "tensor_tensor", [g.b, sgl.b], [hg.b], out=hg.t[:, 0:n], in0=g.t[:, 0:n], in1=sgl.t[:, 0:n], op=ALU.mult)
                        S.dma("pool", HG[c, :, t0:t0 + n], hg.t[:, 0:n], reads=[hg.b], writes=[B_HG])

                for p0 in range(0, len(blocks), 2):
                    ks = [k for k in (p0, p0 + 1) if k < len(blocks)]
                    for k in ks:
                        stage1(k)
                    for k in ks:
                        stage2(k)
        S.barrier()
        S.emit()

    with ExitStack() as ph:
        wo = P.sb(ph, "wo1", [128, 8, D], BF16)
        S.dma("pool", wo.t[:], o_w_out.rearrange("(k p) n -> p k n", p=128), writes=[wo.b])
        qtiles = [(i * 512, 512, 0) for i in range(16)]
        out_proj_ln(P, G, L, qtiles, None, None, HG, wo, None, B_HG, X1, B_X1, G["out"], G["B_out"], ph)
        S.barrier()
        S.emit()


def _rope_tabs(pos, rot_dim):
    rows = (pos // 64).astype(np.float32)
    cols = (pos % 64).astype(np.float32)
    n_freq = rot_dim // 4
    freqs = (np.float32(10000.0) ** (-np.arange(n_freq, dtype=np.float32) / np.float32(n_freq))).astype(np.float32)
    ang = np.concatenate([rows[:, None] * freqs, cols[:, None] * freqs], -1).astype(np.float32)
    return np.cos(ang).T.astype(np.float32), np.sin(ang).T.astype(np.float32)


def _host_layout(inputs):
    f = lambda k: np.asarray(inputs[k], np.float32)
    x, ctx, c, c_ctx = f("x"), f("ctx"), f("c"), f("c_ctx")
    w_in, w_uq, w_ukv = f("e_w_in")[0], f("e_w_uq")[0], f("e_w_ukv")[0]
    kr = w_in[:, 640:672]
    junk = w_in[:, 384:448]
    blk_kr = np.concatenate([junk, kr], 1)
    blk_krr = np.concatenate([junk, kr[:, 16:32], kr[:, 0:16]], 1)

    def rot64(w):
        w4 = w.reshape(w.shape[0], -1, 2, 32)
        return np.concatenate([w4[:, :, 1, :], w4[:, :, 0, :]], -1).reshape(w.shape[0], -1)

    wA = np.ascontiguousarray(np.concatenate([w_in, blk_kr, blk_krr, rot64(w_in[:, 672:1184]), rot64(w_in[:, 1184:1696])], 1))
    uq3 = w_uq.reshape(384, 8, 96)
    uq_rot = np.concatenate([uq3[:, :, 0:64], uq3[:, :, 80:96], uq3[:, :, 64:80]], -1).reshape(384, 768)
    w_uq2 = np.ascontiguousarray(np.concatenate([w_uq, uq_rot], 1))
    kv3 = w_ukv.reshape(256, 8, 128)
    w_ukv2 = np.ascontiguousarray(np.concatenate([kv3[:, :, 0:64].reshape(256, 512), kv3[:, :, 64:128].reshape(256, 512)], 1))
    lamv = np.stack([f(k)[0] for k in ("e_lam_q1", "e_lam_k1", "e_lam_q2", "e_lam_k2")])
    pos = np.arange(SEQ)
    cA, sA = _rope_tabs(pos, 32)
    cB, sB = _rope_tabs(pos, 64)

    def padctx(cs, sn):
        return (np.concatenate([cs, np.ones((cs.shape[0], CTX), np.float32)], 1),
                np.concatenate([sn, np.zeros((sn.shape[0], CTX), np.float32)], 1))

    cA, sA = padctx(cA, sA)
    cB, sB = padctx(cB, sB)
    shared = dict(
        ada_w=f("ada_w"), ada_b=f("ada_b"), ln_g=f("post_ln_g"), ln_b=f("post_ln_b"), identd=np.eye(128, dtype=np.float32),
        wA=wA, w_uq=w_uq2, w_ukv=w_ukv2, qg=f("e_q_norm_g")[0], kvg=f("e_kv_norm_g")[0], lamv=np.ascontiguousarray(lamv),
        subg=f("e_subln_g")[0], e_w_out=f("e_w_out")[0],
        tabA=np.ascontiguousarray(np.stack([np.concatenate([cA, cA], 0), np.concatenate([-sA, sA], 0)])),
        tabB=np.ascontiguousarray(np.stack([np.concatenate([cB, cB], 0), np.concatenate([-sB, sB], 0)])),
        o_w_in=f("o_w_in")[0], o_conv_w=f("o_conv_w")[0], o_conv_b=f("o_conv_b")[0],
        o_gw=np.ascontiguousarray(np.stack([f("o_gate_a_w")[0], f("o_gate_x_w")[0]])),
        o_gb=np.ascontiguousarray(np.stack([f("o_gate_a_b")[0], f("o_gate_x_b")[0]])),
        o_lam=f("o_lru_lambda")[0], o_w_out=f("o_w_out")[0],
    )
    in_maps = []
    for b in range(4):
        m = dict(shared)
        m.update(xk=np.ascontiguousarray(np.concatenate([x[b], ctx[b]], 0)), cvec=np.ascontiguousarray(np.stack([c[b], c_ctx])))
        in_maps.append(m)
    return in_maps


_NC_CACHE = {}
CORE_IDS = [0, 2, 4, 6]


def kernel(**inputs):
    in_maps = _host_layout(inputs)
    if "nc" not in _NC_CACHE:
        _NC_CACHE["nc"] = build_program()
    nc = _NC_CACHE["nc"]
    res = run_bass_kernel_spmd(nc, in_maps, core_ids=CORE_IDS)
    return np.stack([res.results[b]["out"] for b in range(4)], 0).astype(np.float32)
```
